# Optimizing a Trainium2 kernel written in Bass

```python
import jax
import jax.numpy as jnp
from jax import lax
import numpy as np

D_MODEL = 1024
BATCH = 4
SEQ = 4096
DEPTH = 2

GRID_W = 64
CTX_LEN = 256
EPS = 1e-6
ROPE_THETA = 10000.0
N_MOD = 6
GROUP_W = D_MODEL // 4
CONV_W = GROUP_W
CONV_KSIZE = 31
GLA_HEADS = GROUP_W // 64
GLA_DV = 64
GLA_DK = 32
GLA_RANK = 16
GLA_TAU = 16.0
ATT_HEADS = GROUP_W // 64
ATT_KV_HEADS = ATT_HEADS // 2
ATT_DH = 64
RET_HEADS = GROUP_W // 64
RET_DV = 64
RET_DK = 32
CHUNK = 64
Q_BLOCK = 128
D_FF = 4 * D_MODEL
IN_SPLITS = (CONV_W, CONV_W,
             GLA_HEADS * GLA_DK, GLA_HEADS * GLA_DK, GLA_HEADS * GLA_DV, GLA_HEADS * GLA_DV, GLA_RANK,
             ATT_HEADS * ATT_DH, ATT_KV_HEADS * ATT_DH, ATT_KV_HEADS * ATT_DH,
             RET_HEADS * RET_DK, RET_HEADS * RET_DK, RET_HEADS * RET_DV, RET_HEADS * RET_DV)
P_IN = sum(IN_SPLITS)
MIX_W = CONV_W + GLA_HEADS * GLA_DV + ATT_HEADS * ATT_DH + RET_HEADS * RET_DV

kernel_name = 'hybrid_parallel_groups_prefix_ctx_dit'


def rms_norm(x, g):
    xf = x.astype(jnp.float32)
    y = xf * lax.rsqrt(jnp.mean(xf * xf, axis=-1, keepdims=True) + EPS)
    return (y * g.astype(jnp.float32)).astype(x.dtype)


def layer_norm(x, g, b):
    xf = x.astype(jnp.float32)
    mu = jnp.mean(xf, axis=-1, keepdims=True)
    xc = xf - mu
    y = xc * lax.rsqrt(jnp.mean(xc * xc, axis=-1, keepdims=True) + EPS)
    return (y * g.astype(jnp.float32) + b.astype(jnp.float32)).astype(x.dtype)


def split_cols(h):
    out, start = [], 0
    for size in IN_SPLITS:
        out.append(h[..., start:start + size])
        start += size
    return out


def split_heads(t, n_heads):
    B, T, W = t.shape
    return t.reshape(B, T, n_heads, W // n_heads).transpose(0, 2, 1, 3)


def merge_heads(t):
    B, H, T, d = t.shape
    return t.transpose(0, 2, 1, 3).reshape(B, T, H * d)


def axial_rope_tables(row_ids, col_ids, head_dim):
    n_ax = head_dim // 4
    inv = ROPE_THETA ** (-jnp.arange(n_ax, dtype=jnp.float32) / n_ax)
    ang = jnp.concatenate([row_ids.astype(jnp.float32)[:, None] * inv,
                           col_ids.astype(jnp.float32)[:, None] * inv], axis=-1)
    return jnp.cos(ang), jnp.sin(ang)


def apply_rope(x, cos, sin):
    d2 = x.shape[-1] // 2
    x1, x2 = x[..., :d2], x[..., d2:]
    return jnp.concatenate([x1 * cos - x2 * sin, x2 * cos + x1 * sin], axis=-1).astype(x.dtype)


def chunked_gated_scan(q, k, v, log_a, s0):
    out_dtype = v.dtype
    B, H, T, dk = q.shape
    dv = v.shape[-1]
    n = T // CHUNK
    qc = q.astype(jnp.float32).reshape(B, H, n, CHUNK, dk)
    kc = k.astype(jnp.float32).reshape(B, H, n, CHUNK, dk)
    vc = v.astype(jnp.float32).reshape(B, H, n, CHUNK, dv)
    b = jnp.cumsum(log_a.astype(jnp.float32).reshape(B, H, n, CHUNK, dk), axis=3)
    b_last = b[:, :, :, -1:, :]
    q_dec = qc * jnp.exp(b)
    k_inv = kc * jnp.exp(-b)
    k_end = kc * jnp.exp(b_last - b)
    lower_tri = jnp.tril(jnp.ones((CHUNK, CHUNK), dtype=bool))
    scores = jnp.where(lower_tri, jnp.einsum('bhncd,bhnsd->bhncs', q_dec, k_inv), 0.0)
    o_intra = jnp.einsum('bhncs,bhnse->bhnce', scores, vc)
    kv = jnp.einsum('bhncd,bhnce->bhnde', k_end, vc)
    decay = jnp.exp(b_last[:, :, :, 0, :])

    def step(s, inp):
        qd_i, kv_i, dec_i = inp
        o_i = jnp.einsum('bhcd,bhde->bhce', qd_i, s)
        return dec_i[..., None] * s + kv_i, o_i

    xs = (jnp.moveaxis(q_dec, 2, 0), jnp.moveaxis(kv, 2, 0), jnp.moveaxis(decay, 2, 0))
    s_final, o_inter = lax.scan(step, s0, xs)
    o = o_intra + jnp.moveaxis(o_inter, 0, 2)
    return o.reshape(B, H, T, dv).astype(out_dtype), s_final


def bidirectional_scan(lat, ctx, need_ctx):
    q_l, k_l, v_l, af_l, ab_l = lat
    q_c, k_c, v_c, af_c, ab_c = ctx
    B, H, _, dk = q_l.shape
    dv = v_l.shape[-1]
    s0 = jnp.zeros((B, H, dk, dv), jnp.float32)
    flip = lambda t: jnp.flip(t, axis=2)
    o_cf, s_cf = chunked_gated_scan(q_c, k_c, v_c, af_c, s0)
    o_lf, _ = chunked_gated_scan(q_l, k_l, v_l, af_l, s_cf)
    o_cb, s_cb = chunked_gated_scan(flip(q_c), flip(k_c), flip(v_c), flip(ab_c), s0)
    o_lb, _ = chunked_gated_scan(flip(q_l), flip(k_l), flip(v_l), flip(ab_l), s_cb)
    o_l = o_lf + flip(o_lb)
    o_c = (o_cf + flip(o_cb)) if need_ctx else None
    return o_l, o_c


def conformer_conv(a, g, w_dw, b_dw, ln_g, ln_b, w_pw, b_pw):
    u = a * jax.nn.sigmoid(g)
    pad = (CONV_KSIZE - 1) // 2
    y = lax.conv_general_dilated(u, w_dw[:, None, :].astype(u.dtype), (1,), [(pad, pad)],
                                 dimension_numbers=('NWC', 'WIO', 'NWC'),
                                 feature_group_count=u.shape[-1]) + b_dw
    y = layer_norm(y, ln_g, ln_b)
    return jax.nn.silu(y) @ w_pw + b_pw


def conv_mixer(a_l, g_l, a_c, g_c, w_dw, b_dw, ln_g, ln_b, w_pw, b_pw, need_ctx):
    y_l = conformer_conv(a_l, g_l, w_dw, b_dw, ln_g, ln_b, w_pw, b_pw)
    y_c = conformer_conv(a_c, g_c, w_dw, b_dw, ln_g, ln_b, w_pw, b_pw) if need_ctx else None
    return y_l, y_c


def gla_mixer(p_l, p_c, w_a_f, b_a_f, w_a_b, b_a_b, norm_g, need_ctx):
    def prep(q, k, v, z):
        q = split_heads(q, GLA_HEADS) * GLA_DK ** -0.5
        k = split_heads(k, GLA_HEADS)
        v = split_heads(v, GLA_HEADS)
        la_f = split_heads(jax.nn.log_sigmoid(z @ w_a_f + b_a_f) / GLA_TAU, GLA_HEADS)
        la_b = split_heads(jax.nn.log_sigmoid(z @ w_a_b + b_a_b) / GLA_TAU, GLA_HEADS)
        return q, k, v, la_f, la_b

    q_l, k_l, v_l, r_l, z_l = p_l
    q_c, k_c, v_c, r_c, z_c = p_c
    o_l, o_c = bidirectional_scan(prep(q_l, k_l, v_l, z_l), prep(q_c, k_c, v_c, z_c), need_ctx)
    g = norm_g[:, None, :]
    y_l = merge_heads(rms_norm(o_l, g)) * jax.nn.silu(r_l)
    y_c = merge_heads(rms_norm(o_c, g)) * jax.nn.silu(r_c) if need_ctx else None
    return y_l, y_c


def retention_mixer(p_l, p_c, log_gamma, norm_g, cos, sin, need_ctx):
    def prep(q, k, v, rope):
        q = split_heads(q, RET_HEADS)
        k = split_heads(k, RET_HEADS) * RET_DK ** -0.5
        v = split_heads(v, RET_HEADS)
        if rope:
            q = apply_rope(q, cos, sin)
            k = apply_rope(k, cos, sin)
        la = jnp.broadcast_to(log_gamma[None, :, None, None], q.shape)
        return q, k, v, la, la

    q_l, k_l, v_l, g_l = p_l
    q_c, k_c, v_c, g_c = p_c
    o_l, o_c = bidirectional_scan(prep(q_l, k_l, v_l, True), prep(q_c, k_c, v_c, False), need_ctx)
    g = norm_g[:, None, :]
    y_l = merge_heads(rms_norm(o_l, g)) * jax.nn.silu(g_l)
    y_c = merge_heads(rms_norm(o_c, g)) * jax.nn.silu(g_c) if need_ctx else None
    return y_l, y_c


def gqa_attend(q, k, v):
    s = jnp.einsum('bkgqd,bksd->bkgqs', q, k, preferred_element_type=jnp.float32) * ATT_DH ** -0.5
    p = jax.nn.softmax(s, axis=-1).astype(v.dtype)
    return jnp.einsum('bkgqs,bksd->bkgqd', p, v)


def attention_mixer(p_l, p_c, qn_g, kn_g, cos, sin, need_ctx):
    G = ATT_HEADS // ATT_KV_HEADS

    def prep(q, k, v):
        q = rms_norm(split_heads(q, ATT_HEADS), qn_g)
        k = rms_norm(split_heads(k, ATT_KV_HEADS), kn_g)
        return q, k, split_heads(v, ATT_KV_HEADS)

    q_l, k_l, v_l = prep(*p_l)
    q_l = apply_rope(q_l, cos, sin)
    k_l = apply_rope(k_l, cos, sin)
    q_c, k_c, v_c = prep(*p_c)
    keys = jnp.concatenate([k_c, k_l], axis=2)
    vals = jnp.concatenate([v_c, v_l], axis=2)
    B, _, T, _ = q_l.shape
    nblk = T // Q_BLOCK
    qb = q_l.reshape(B, ATT_KV_HEADS, G, nblk, Q_BLOCK, ATT_DH).transpose(3, 0, 1, 2, 4, 5)
    ob = lax.map(lambda qi: gqa_attend(qi, keys, vals), qb)
    o_l = ob.transpose(1, 2, 3, 0, 4, 5).reshape(B, ATT_HEADS, T, ATT_DH)
    y_l = merge_heads(o_l)
    y_c = None
    if need_ctx:
        Tc = q_c.shape[2]
        o_c = gqa_attend(q_c.reshape(B, ATT_KV_HEADS, G, Tc, ATT_DH), k_c, v_c)
        y_c = merge_heads(o_c.reshape(B, ATT_HEADS, Tc, ATT_DH))
    return y_l, y_c


def sq_relu_mlp(h, w_up, w_down):
    return jnp.square(jax.nn.relu(h @ w_up)) @ w_down


def setup_inputs(seed: int = 0) -> dict:
    key = jax.random.key(seed)
    ks = jax.random.split(key, 32)
    nrm = lambda i, shape, scale: jax.random.normal(ks[i], shape, jnp.float32) * scale
    L = DEPTH
    return {
        'x': nrm(0, (BATCH, SEQ, D_MODEL), 1.0),
        'c': nrm(1, (BATCH, D_MODEL), 1.0),
        'ctx': nrm(2, (BATCH, CTX_LEN, D_MODEL), 1.0),
        'c_ctx': nrm(3, (D_MODEL,), 1.0),
        'w_mod': nrm(4, (L, D_MODEL, N_MOD * D_MODEL), 0.5 * D_MODEL ** -0.5),
        'b_mod': nrm(5, (L, N_MOD * D_MODEL), 0.02),
        'norm1_g': 1.0 + nrm(6, (L, D_MODEL), 0.02),
        'norm2_g': 1.0 + nrm(7, (L, D_MODEL), 0.02),
        'w_in': nrm(8, (L, D_MODEL, P_IN), D_MODEL ** -0.5),
        'conv_w_dw': nrm(9, (L, CONV_KSIZE, CONV_W), CONV_KSIZE ** -0.5),
        'conv_b_dw': nrm(10, (L, CONV_W), 0.02),
        'conv_ln_g': 1.0 + nrm(11, (L, CONV_W), 0.02),
        'conv_ln_b': nrm(12, (L, CONV_W), 0.02),
        'conv_w_pw': nrm(13, (L, CONV_W, CONV_W), CONV_W ** -0.5),
        'conv_b_pw': nrm(14, (L, CONV_W), 0.02),
        'gla_w_a_f': nrm(15, (L, GLA_RANK, GLA_HEADS * GLA_DK), GLA_RANK ** -0.5),
        'gla_b_a_f': nrm(16, (L, GLA_HEADS * GLA_DK), 0.02),
        'gla_w_a_b': nrm(17, (L, GLA_RANK, GLA_HEADS * GLA_DK), GLA_RANK ** -0.5),
        'gla_b_a_b': nrm(18, (L, GLA_HEADS * GLA_DK), 0.02),
        'gla_norm_g': 1.0 + nrm(19, (L, GLA_HEADS, GLA_DV), 0.02),
        'att_q_norm_g': 1.0 + nrm(20, (L, ATT_DH), 0.02),
        'att_k_norm_g': 1.0 + nrm(21, (L, ATT_DH), 0.02),
        'ret_norm_g': 1.0 + nrm(22, (L, RET_HEADS, RET_DV), 0.02),
        'w_out': nrm(23, (L, MIX_W, D_MODEL), MIX_W ** -0.5),
        'w_up': nrm(24, (L, D_MODEL, D_FF), D_MODEL ** -0.5),
        'w_down': nrm(25, (L, D_FF, D_MODEL), D_FF ** -0.5),
        'final_norm_g': 1.0 + nrm(26, (D_MODEL,), 0.02),
    }


def reference(x, c, ctx, c_ctx, w_mod, b_mod, norm1_g, norm2_g, w_in,
              conv_w_dw, conv_b_dw, conv_ln_g, conv_ln_b, conv_w_pw, conv_b_pw,
              gla_w_a_f, gla_b_a_f, gla_w_a_b, gla_b_a_b, gla_norm_g,
              att_q_norm_g, att_k_norm_g, ret_norm_g,
              w_out, w_up, w_down, final_norm_g):
    B, T, _ = x.shape
    rows = T // GRID_W
    row_ids = jnp.repeat(jnp.arange(rows), GRID_W)
    col_ids = jnp.tile(jnp.arange(GRID_W), rows)
    cos_att, sin_att = axial_rope_tables(row_ids, col_ids, ATT_DH)
    cos_ret, sin_ret = axial_rope_tables(row_ids, col_ids, RET_DK)
    log_gamma = jnp.log1p(-jnp.exp2(-5.0 - jnp.arange(RET_HEADS, dtype=jnp.float32)))
    silu_c = jax.nn.silu(c)
    silu_cc = jax.nn.silu(c_ctx)
    xc = ctx
    for l in range(DEPTH):
        need_ctx = l < DEPTH - 1
        mod_l = (silu_c @ w_mod[l] + b_mod[l])[:, None, :]
        mod_c = (silu_cc @ w_mod[l] + b_mod[l])[None, None, :]
        sh1_l, sc1_l, g1_l, sh2_l, sc2_l, g2_l = jnp.split(mod_l, N_MOD, axis=-1)
        sh1_c, sc1_c, g1_c, sh2_c, sc2_c, g2_c = jnp.split(mod_c, N_MOD, axis=-1)

        h_l = rms_norm(x, norm1_g[l]) * (1.0 + sc1_l) + sh1_l
        h_c = rms_norm(xc, norm1_g[l]) * (1.0 + sc1_c) + sh1_c
        pl = split_cols(h_l @ w_in[l])
        pc = split_cols(h_c @ w_in[l])
        y_conv_l, y_conv_c = conv_mixer(pl[0], pl[1], pc[0], pc[1], conv_w_dw[l], conv_b_dw[l],
                                        conv_ln_g[l], conv_ln_b[l], conv_w_pw[l], conv_b_pw[l],
                                        need_ctx)
        y_gla_l, y_gla_c = gla_mixer(pl[2:7], pc[2:7], gla_w_a_f[l], gla_b_a_f[l],
                                     gla_w_a_b[l], gla_b_a_b[l], gla_norm_g[l], need_ctx)
        y_att_l, y_att_c = attention_mixer(pl[7:10], pc[7:10], att_q_norm_g[l], att_k_norm_g[l],
                                           cos_att, sin_att, need_ctx)
        y_ret_l, y_ret_c = retention_mixer(pl[10:14], pc[10:14], log_gamma, ret_norm_g[l],
                                           cos_ret, sin_ret, need_ctx)
        o_l = jnp.concatenate([y_conv_l, y_gla_l, y_att_l, y_ret_l], axis=-1) @ w_out[l]
        x = x + g1_l * o_l
        h2_l = rms_norm(x, norm2_g[l]) * (1.0 + sc2_l) + sh2_l
        x = x + g2_l * sq_relu_mlp(h2_l, w_up[l], w_down[l])

        if need_ctx:
            o_c = jnp.concatenate([y_conv_c, y_gla_c, y_att_c, y_ret_c], axis=-1) @ w_out[l]
            xc = xc + g1_c * o_c
            h2_c = rms_norm(xc, norm2_g[l]) * (1.0 + sc2_c) + sh2_c
            xc = xc + g2_c * sq_relu_mlp(h2_c, w_up[l], w_down[l])
    return rms_norm(x, final_norm_g)
```

```python
import math
from contextlib import ExitStack
import numpy as np
import concourse.bass as bass
import concourse.mybir as mybir
from concourse.bass_utils import run_bass_kernel_spmd

F32 = mybir.dt.float32
BF16 = mybir.dt.bfloat16
U8 = mybir.dt.uint8
AF = mybir.ActivationFunctionType
ALU = mybir.AluOpType
AX = mybir.AxisListType

D = 1024
TL = 4096
TC = 256
NT = TL + TC
NTILE = NT // 128
PIN = 2576
PTW = 2064
EPS = 1e-6
NPP = 142
NBC = 896
NCST = 1925
DEPTH = 2

ENGS = ['tensor', 'vector', 'scalar', 'gpsimd', 'sync']
DMA_POOL = 24
KEEPWARM = 0
DSIZE = {F32: 4, BF16: 2, U8: 1}


class Sched:
    def __init__(self, nc, stack):
        self.nc = nc
        self.items = {e: [] for e in ENGS}
        self.seq = {e: 0 for e in ENGS}
        self.esem = {e: stack.enter_context(nc.semaphore("sq_" + e)) for e in ENGS if e != 'sync'}
        self.dpool = {}
        self.dcount = {}
        for q in ['sync', 'gpsimd']:
            self.dpool[q] = [stack.enter_context(nc.semaphore("dq_%s_%d" % (q, i))) for i in range(DMA_POOL)]
            self.dcount[q] = 0
        self.waited = {}
        self.recs = {}
        self.dram_rowlen = {}
        self.arena_name = None
        self.arena_allocs = []
        self.n_ops = 0
        self.n_waits = 0

    def box(self, ap):
        t = ap.tensor
        name = t.name
        off = int(ap.offset)
        pairs = [(int(s), int(c)) for s, c in ap.ap]
        mx = off + sum(s * (c - 1) for s, c in pairs if c > 0)
        if name in self.dram_rowlen:
            rowlen = self.dram_rowlen[name]
            es = 1
        else:
            rowlen = pairs[0][0]
            es = DSIZE[ap.dtype]
        r0 = off // rowlen
        r1 = mx // rowlen
        c0 = off % rowlen
        c1 = c0 + sum(s * (c - 1) for s, c in pairs if s < rowlen and c > 0)
        if c1 >= rowlen:
            c0, c1 = 0, rowlen - 1
        c0 *= es
        c1 = c1 * es + es - 1
        if name == self.arena_name:
            for (a, b, i) in self.arena_allocs:
                if a <= c0 < b:
                    assert c1 < b, "arena view crosses allocation"
                    name = (name, i)
                    break
            else:
                raise AssertionError("arena box not found")
        return name, (r0, r1, c0, c1)

    def _need(self, eng, ev, waits):
        sem, val = ev
        k = (eng, id(sem))
        if self.waited.get(k, 0) >= val:
            return
        self.waited[k] = val
        waits[id(sem)] = (sem, val)

    def add(self, eng, fn, reads=(), writes=(), dma=False, acc=False):
        waits = {}
        rb = [self.box(a) for a in reads]
        wb = [self.box(a) for a in writes]
        for name, b in rb:
            ps = isinstance(name, str) and name.startswith("ps")
            for r in self.recs.get(name, ()):
                if ps and r[5] != eng:
                    self._need(eng, r[6], waits)
                elif r[4] and not (r[1] < b[0] or b[1] < r[0] or r[3] < b[2] or b[3] < r[2]):
                    self._need(eng, r[6], waits)
        for name, b in wb:
            ps = isinstance(name, str) and name.startswith("ps")
            for r in self.recs.get(name, ()):
                if ps and r[5] != eng:
                    self._need(eng, r[6], waits)
                elif ps and eng == 'tensor':
                    continue
                elif not (r[1] < b[0] or b[1] < r[0] or r[3] < b[2] or b[3] < r[2]):
                    if acc and r[4] and r[5] == 'tensor':
                        continue
                    self._need(eng, r[6], waits)
        if dma:
            q = eng
            i = self.dcount[q]
            self.dcount[q] += 1
            sem = self.dpool[q][i % DMA_POOL]
            val = 16 * (i // DMA_POOL + 1)
            if i >= DMA_POOL:
                self._need(eng, (sem, val - 16), waits)
            ev = (sem, val)
            inc = (sem, 16)
        else:
            self.seq[eng] += 1
            ev = (self.esem[eng], self.seq[eng])
            inc = (self.esem[eng], 1)
        for name, b in wb:
            lst = self.recs.setdefault(name, [])
            lst[:] = [r for r in lst if not (b[0] <= r[0] and r[1] <= b[1] and b[2] <= r[2] and r[3] <= b[3])]
            lst.append([b[0], b[1], b[2], b[3], True, eng, ev])
        for name, b in rb:
            lst = self.recs.setdefault(name, [])
            if not dma:
                lst[:] = [r for r in lst if not ((not r[4]) and r[5] == eng and r[6][0] is ev[0]
                                                 and b[0] <= r[0] and r[1] <= b[1] and b[2] <= r[2] and r[3] <= b[3])]
            lst.append([b[0], b[1], b[2], b[3], False, eng, ev])
        self.items[eng].append((list(waits.values()), fn, inc))
        self.n_ops += 1
        self.n_waits += len(waits)
        return ev

    def barrier(self):
        for eng in ENGS:
            waits = {}
            for q in self.dpool:
                n = self.dcount[q]
                for j in range(min(n, DMA_POOL)):
                    uses = (n - 1 - j) // DMA_POOL + 1
                    self._need(eng, (self.dpool[q][j], 16 * uses), waits)
            for e in self.esem:
                if self.seq[e] > 0:
                    self._need(eng, (self.esem[e], self.seq[e]), waits)
            if waits:
                self.items[eng].append((list(waits.values()), None, None))
        self.recs = {}

    def dma(self, out, in_, q='sync'):
        return self.add(q, lambda e: e.dma_start(out=out, in_=in_), reads=[in_], writes=[out], dma=True)

    def mm(self, out, pairs, start=True, stop=True, acc=False):
        n = len(pairs)

        def fn(e):
            ins = None
            for i, p in enumerate(pairs):
                ins = e.matmul(out, p[0], p[1], start=(start and i == 0), stop=(stop and i == n - 1))
            return ins
        rd = []
        for p in pairs:
            rd += [p[0], p[1]]
        return self.add('tensor', fn, reads=rd, writes=[out], acc=acc)

    def transpose(self, out, in_, ident):
        return self.add('tensor', lambda e: e.transpose(out, in_, ident), reads=[in_, ident], writes=[out])

    def act(self, out, in_, func, bias=None, scale=None, accum_out=None):
        kw = {}
        rd = [in_]
        wr = [out]
        if bias is not None:
            kw['bias'] = bias
            if not isinstance(bias, (int, float)):
                rd.append(bias)
        if scale is not None:
            kw['scale'] = scale
            if not isinstance(scale, (int, float)):
                rd.append(scale)
        if accum_out is not None:
            kw['accum_out'] = accum_out
            wr.append(accum_out)
        return self.add('scalar', lambda e: e.activation(out, in_, func, **kw), reads=rd, writes=wr)

    def tt(self, out, in0, in1, op, eng='vector'):
        return self.add(eng, lambda e: e.tensor_tensor(out, in0, in1, op), reads=[in0, in1], writes=[out])

    def ts(self, out, in0, s1, s2, op0, op1=None, eng='vector'):
        rd = [in0]
        if not isinstance(s1, (int, float)):
            rd.append(s1)
        if s2 is not None and not isinstance(s2, (int, float)):
            rd.append(s2)
        if op1 is None:
            return self.add(eng, lambda e: e.tensor_scalar(out, in0, s1, None, op0), reads=rd, writes=[out])
        return self.add(eng, lambda e: e.tensor_scalar(out, in0, s1, s2, op0, op1), reads=rd, writes=[out])

    def stt(self, out, in0, scalar, in1, op0, op1, eng='vector'):
        eng = 'vector'
        rd = [in0, in1]
        if not isinstance(scalar, (int, float)):
            rd.append(scalar)
        return self.add(eng, lambda e: e.scalar_tensor_tensor(out, in0, scalar, in1, op0, op1), reads=rd, writes=[out])

    def copy(self, out, in_, eng='vector'):
        if eng == 'scalar':
            return self.add('scalar', lambda e: e.copy(out, in_), reads=[in_], writes=[out])
        return self.add(eng, lambda e: e.tensor_copy(out, in_), reads=[in_], writes=[out])

    def memset(self, ap, val, eng='vector'):
        return self.add(eng, lambda e: e.memset(ap, val), reads=[], writes=[ap])

    def recip(self, out, in_):
        return self.add('vector', lambda e: e.reciprocal(out, in_), reads=[in_], writes=[out])

    def reduce(self, out, in_, op=ALU.add, eng='vector'):
        return self.add(eng, lambda e: e.tensor_reduce(out, in_, AX.X, op), reads=[in_], writes=[out])

    def finish(self):
        self.barrier()

    def emit(self):
        nc = self.nc
        items = self.items

        def run(e, lst):
            for waits, fn, inc in lst:
                for sem, v in waits:
                    e.wait_ge(sem, v)
                if fn is None:
                    continue
                ins = fn(e)
                ins.then_inc(inc[0], inc[1])

        with nc.Block() as block:
            @block.tensor
            def _(e):
                run(e, items['tensor'])

            @block.vector
            def _(e):
                run(e, items['vector'])

            @block.scalar
            def _(e):
                run(e, items['scalar'])

            @block.gpsimd
            def _(e):
                run(e, items['gpsimd'])

            @block.sync
            def _(e):
                run(e, items['sync'])


class Arena:
    def __init__(self, S, t, nbytes):
        self.S = S
        self.t = t
        self.n = nbytes
        self.off = 0
        self.uid = 0
        S.arena_name = t.name

    def alloc(self, shape, dtype):
        es = DSIZE[dtype]
        free = 1
        for s in shape[1:]:
            free *= s
        nb = (free * es + 63) // 64 * 64
        assert self.off + nb <= self.n, "arena overflow: need %d have %d" % (self.off + nb, self.n)
        a = self.off
        self.off += nb
        self.uid += 1
        self.S.arena_allocs.append((a, a + nb, self.uid))
        ap = self.t[0:shape[0], a:a + free * es].bitcast(dtype)
        if len(shape) > 2:
            names = ["d%d" % i for i in range(len(shape) - 1)]
            kw = {names[i]: shape[i + 1] for i in range(len(shape) - 2)}
            ap = ap.rearrange("p (%s) -> p %s" % (" ".join(names), " ".join(names)), **kw)
        return ap

    def mark(self):
        return (self.off, len(self.S.arena_allocs))

    def reset(self, mark):
        self.off = mark[0]
        del self.S.arena_allocs[mark[1]:]


def bcast_mid(ap, n):
    return ap.unsqueeze(1).to_broadcast([ap.shape[0], n, ap.shape[1]])


def build_program(dbg=False, nlayers=DEPTH, stop_after=None):
    nc = bass.Bass("TRN2", target_bir_lowering=False)
    dt_in = lambda name, shape, dt=F32: nc.dram_tensor(name, list(shape), dt, kind="ExternalInput").ap()
    xT_in = dt_in("xT", [D, NT])
    cvec_in = dt_in("cvec", [128, 16])
    cst_in = dt_in("cst", [128, NCST])
    ropeA_in = dt_in("ropeA", [128, 32 * 64])
    ropeR_in = dt_in("ropeR", [128, 32 * 32])
    W = []
    for l in range(DEPTH):
        W.append(dict(
            w_mod=dt_in("w_mod%d" % l, [D, 6 * D]), w_in=dt_in("w_in%d" % l, [D, PIN]),
            w_out=dt_in("w_out%d" % l, [D, D]), w_up=dt_in("w_up%d" % l, [D, 4 * D]),
            w_down=dt_in("w_down%d" % l, [4 * D, D]), pp=dt_in("pp%d" % l, [128, NPP]),
            bc=dt_in("bc%d" % l, [128, NBC]), waug=dt_in("waug%d" % l, [32, 256]),
            wpw=dt_in("wpw%d" % l, [256, 256])))
    outT = nc.dram_tensor("outT", [D, TL // 2], F32, kind="ExternalOutput").ap()
    XST = nc.dram_tensor("XST", [D, NT], F32).ap()
    PT = nc.dram_tensor("PTs", [NT, PTW], BF16).ap()
    YT = nc.dram_tensor("YTs", [D, NT], BF16).ap()
    WOb = nc.dram_tensor("WOb", [D, D], BF16).ap()
    WUb = nc.dram_tensor("WUb", [D, 4 * D], BF16).ap()
    WDb = nc.dram_tensor("WDb", [4 * D, D], BF16).ap()
    WIb = nc.dram_tensor("WIb", [D, PIN], BF16).ap()
    dbg_out = {}
    if dbg:
        dbg_out['PT'] = nc.dram_tensor("dbgPT", [NT, PTW], BF16, kind="ExternalOutput").ap()
        dbg_out['YT'] = nc.dram_tensor("dbgYT", [D, NT], BF16, kind="ExternalOutput").ap()
        dbg_out['X1'] = nc.dram_tensor("dbgX1", [D, NT], F32, kind="ExternalOutput").ap()
        dbg_out['MOD'] = nc.dram_tensor("dbgMOD", [128, 192], F32, kind="ExternalOutput").ap()

    with ExitStack() as st:
        S = Sched(nc, st)
        for nm, ap_ in [("WIb", WIb), ("WOb", WOb), ("WUb", WUb), ("WDb", WDb), ("xT", xT_in), ("XST", XST), ("PTs", PT), ("YTs", YT), ("outT", outT), ("cvec", cvec_in),
                        ("cst", cst_in), ("ropeA", ropeA_in), ("ropeR", ropeR_in)]:
            S.dram_rowlen[nm] = ap_.shape[1]
        for l in range(DEPTH):
            for k, v in W[l].items():
                S.dram_rowlen[v.tensor.name] = v.shape[1]
        for k, v in dbg_out.items():
            S.dram_rowlen[v.tensor.name] = v.shape[1]
        ARENA_BYTES = 206 * 1024
        at = st.enter_context(nc.sbuf_tensor("arena", [128, ARENA_BYTES], U8))
        A = Arena(S, at, ARENA_BYTES)
        PS = [st.enter_context(nc.psum_tensor("ps%d" % i, [128, 512], F32)) for i in range(8)]
        PSB = [p[:].bitcast(BF16) for p in PS]

        cst = A.alloc([128, NCST], F32)
        S.dma(cst, cst_in)
        identb = A.alloc([128, 128], BF16)
        onesb = A.alloc([128, 128], BF16)
        onesf = A.alloc([128, 128], F32)
        maskb16 = A.alloc([128, 2, 128], BF16)
        MOD = A.alloc([128, DEPTH, 6, 8, 2], F32)
        pp = A.alloc([128, NPP], F32)
        S.copy(identb, cst[:, 0:128])
        S.memset(onesb, 1.0 / 1024)
        S.memset(onesf, 1.0)
        S.copy(maskb16[:, 0, :], cst[:, 640:768])
        S.copy(maskb16[:, 1, :], cst[:, 768:896])
        TRI = [cst[:, 128:256], cst[:, 256:384]]
        TRIR = [cst[:, 384:512], cst[:, 512:640]]
        BDm = cst[:, 896:1152]
        HM = cst[:, 1152:1156]
        base_mark = A.mark()

        class _Stop(Exception):
            pass

        def load_cast(dst, src, stg, width, engs=('vector', 'gpsimd')):
            n = dst.shape[1]
            i = 0
            c0 = 0
            while c0 < n:
                w = min(width, n - c0)
                sg = stg[i % len(stg)]
                S.dma(sg[:, 0:w], src[:, c0:c0 + w], q=('sync' if i % 2 == 0 else 'gpsimd'))
                S.copy(dst[:, c0:c0 + w], sg[:, 0:w], eng=engs[i % len(engs)])
                c0 += w
                i += 1

        def phase_mod():
            mk = A.mark()
            cs = A.alloc([128, 8, 2], F32)
            sc = A.alloc([128, 8, 2], F32)
            S.dma(cs, cvec_in.rearrange("p (c m) -> p c m", m=2))
            S.act(sc, cs, AF.Silu)
            wst = [A.alloc([128, 6144], F32) for _ in range(2)]
            acc = A.alloc([128, 48, 2], F32)
            ppl = A.alloc([128, NPP], F32)
            for l in range(nlayers):
                S.dma(ppl, W[l]['pp'])
                for k in range(8):
                    wk = wst[k % 2]
                    S.dma(wk, W[l]['w_mod'][k * 128:(k + 1) * 128, :], q=('sync' if k % 2 == 0 else 'gpsimd'))
                    pm = PS[k % 2][:, 0:96].rearrange("p (j m) -> p j m", m=2)
                    for j in range(48):
                        S.mm(pm[:, j, :], [(wk[:, j * 128:(j + 1) * 128], sc[:, k, :])])
                    if k == 0:
                        S.copy(acc, pm)
                    else:
                        S.tt(acc, acc, pm, ALU.add)
                bm = ppl[:, 16:64].unsqueeze(2).to_broadcast([128, 48, 2])
                S.tt(acc, acc, bm, ALU.add)
                a4 = acc.rearrange("p (w c) m -> p w c m", w=6)
                g1n = ppl[:, 0:8].unsqueeze(2).to_broadcast([128, 8, 2])
                g2n = ppl[:, 8:16].unsqueeze(2).to_broadcast([128, 8, 2])
                S.stt(MOD[:, l, 0], a4[:, 1], 1.0, g1n, ALU.add, ALU.mult)
                S.copy(MOD[:, l, 1], a4[:, 0])
                S.copy(MOD[:, l, 2], a4[:, 2])
                S.stt(MOD[:, l, 3], a4[:, 4], 1.0, g2n, ALU.add, ALU.mult)
                S.copy(MOD[:, l, 4], a4[:, 3])
                S.copy(MOD[:, l, 5], a4[:, 5])
            if dbg:
                S.dma(dbg_out['MOD'], MOD.rearrange("p l w c m -> p (l w c m)"), q='gpsimd')
            S.barrier()
            A.reset(mk)

        def norm_mod(xg, n, sq, tmp, hT, Acol, Bcol, pbank):
            S.act(sq[:, :, 0:n], xg[:, :, 0:n], AF.Square)
            ps = PS[pbank][:, 0:n]
            S.mm(ps, [(onesb, sq[:, c, 0:n]) for c in range(8)])
            rstd = tmp[0][:, 0:n]
            S.act(rstd, ps, AF.Sqrt, bias=EPS)
            S.recip(rstd, rstd)
            for c in range(8):
                t = tmp[1 + c % 2][:, 0:n]
                S.tt(t, xg[:, c, 0:n], rstd, ALU.mult, eng=('vector' if c % 2 == 0 else 'gpsimd'))
                S.act(hT[:, c, 0:n], t, AF.Identity, bias=Bcol[:, c:c + 1], scale=Acol[:, c:c + 1])

        def phase_A(l, Xsrc, zT, uTl, uTc):
            mk = A.mark()
            WIN = A.alloc([128, 8, PIN], BF16)
            if l == 0:
                stg = [A.alloc([128, PIN], F32) for _ in range(2)]
                for k in range(8):
                    S.dma(stg[k % 2], W[l]['w_in'][k * 128:(k + 1) * 128, :], q=('sync' if k % 2 == 0 else 'gpsimd'))
                    S.copy(WIN[:, k, :], stg[k % 2], eng=('vector' if k % 2 == 0 else 'gpsimd'))
            else:
                for k2 in range(4):
                    S.dma(WIN[:, 2 * k2:2 * k2 + 2, :], WIb[k2 * 256:(k2 + 1) * 256, :].rearrange("(k p) n -> p k n", p=128),
                          q=('sync' if k2 % 2 == 0 else 'gpsimd'))
            xgs = [A.alloc([128, 8, 512], F32) for _ in range(2)]
            sq = A.alloc([128, 8, 512], BF16)
            tmp = [A.alloc([128, 512], F32) for _ in range(3)]
            hTs = [A.alloc([128, 8, 512], BF16) for _ in range(2)]
            OTs = [A.alloc([128, PTW], BF16) for _ in range(2)]
            sig = [A.alloc([128, 512], F32) for _ in range(2)]
            Xv = Xsrc.rearrange("(c p) t -> p c t", p=128)
            colblocks = [(512, 1024), (1024, 1536), (1536, 2048), (2048, 2560), (2560, 2576)]
            ti = 0

            def prep_group(g):
                n = 512 if g < 8 else 256
                m = 0 if g < 8 else 1
                S.dma(xgs[g % 2][:, :, 0:n], Xv[:, :, g * 512:g * 512 + n])
                norm_mod(xgs[g % 2], n, sq, tmp, hTs[g % 2], MOD[:, l, 0, :, m], MOD[:, l, 1, :, m], 0)
            prep_group(0)
            for g in range(9):
                n = 512 if g < 8 else 256
                t0 = g * 512
                m = 0 if g < 8 else 1
                hT = hTs[g % 2]
                if g + 1 < 9:
                    prep_group(g + 1)
                for tt_ in range(n // 128):
                    OT = OTs[ti % 2]
                    for bi, (c0, c1) in enumerate(colblocks):
                        w = c1 - c0
                        ps = PS[1 + (bi % 2)][:, 0:w]
                        S.mm(ps, [(hT[:, c, tt_ * 128:(tt_ + 1) * 128], WIN[:, c, c0:c1]) for c in range(8)])
                        if bi % 2 == 0:
                            S.copy(OT[:, c0 - 512:c1 - 512], ps, eng='scalar')
                        else:
                            S.copy(OT[:, c0 - 512:c1 - 512], ps, eng='vector')
                    tok = t0 + tt_ * 128
                    S.dma(PT[tok:tok + 128, :], OT, q='gpsimd')
                    ti += 1
                for cc in range(2):
                    pa = PS[3][:, 0:n]
                    pg = PS[4][:, 0:n]
                    S.mm(pa, [(WIN[:, c, cc * 128:(cc + 1) * 128], hT[:, c, 0:n]) for c in range(8)])
                    S.mm(pg, [(WIN[:, c, 256 + cc * 128:256 + (cc + 1) * 128], hT[:, c, 0:n]) for c in range(8)])
                    sg = sig[cc][:, 0:n]
                    S.act(sg, pg, AF.Sigmoid)
                    if g < 8:
                        dst = uTl[:, cc, 15 + t0:15 + t0 + n]
                    else:
                        dst = uTc[:, cc, 15:15 + n]
                    S.tt(dst, pa, sg, ALU.mult)
                pz = PS[5][0:16, 0:n]
                S.mm(pz, [(WIN[:, c, 1280:1296], hT[:, c, 0:n]) for c in range(8)])
                S.copy(zT[0:16, t0:t0 + n], pz)
            S.barrier()
            A.reset(mk)

        def phase_conv(l, uTl, uTc, need_ctx, nblk):
            mk = A.mark()
            DG = A.alloc([128, 2, 31, 128], BF16)
            WPW = A.alloc([128, 2, 256], BF16)
            wst = A.alloc([128, 2, 256], F32)
            S.dma(wst, W[l]['wpw'].rearrange("(c p) n -> p c n", p=128))
            S.copy(WPW, wst)
            for cc in range(2):
                for j in range(31):
                    S.ts(DG[:, cc, j, :], identb, pp[:, 72 + cc * 31 + j:73 + cc * 31 + j], None, ALU.mult,
                         eng=('vector' if j % 2 == 0 else 'gpsimd'))
            y = [A.alloc([128, 512], F32) for _ in range(2)]
            ysq = [A.alloc([128, 512], F32) for _ in range(2)]
            msb = A.alloc([128, 512], F32)
            m2 = A.alloc([128, 512], F32)
            rstd = A.alloc([128, 512], F32)
            t1 = [A.alloc([128, 512], F32) for _ in range(2)]
            sb = [A.alloc([128, 512], BF16) for _ in range(2)]
            yo = [A.alloc([128, 2, 512], BF16) for _ in range(2)]
            blocks = [(uTl, b * 512, 512, b * 512) for b in range(nblk)]
            if need_ctx:
                blocks.append((uTc, 0, 256, TL))
            for bi, (uT, t0, n, tok0) in enumerate(blocks):
                for cc in range(2):
                    pc = PS[cc][:, 0:n]
                    S.mm(pc, [(DG[:, cc, j, :], uT[:, cc, t0 + j:t0 + j + n]) for j in range(31)])
                    S.act(y[cc][:, 0:n], pc, AF.Identity, bias=pp[:, 64 + cc:65 + cc])
                    S.act(ysq[cc][:, 0:n], pc, AF.Square, bias=pp[:, 64 + cc:65 + cc])
                pm = PS[2][:, 0:n]
                pq = PS[3][:, 0:n]
                S.mm(pm, [(onesf, y[0][:, 0:n]), (onesf, y[1][:, 0:n])])
                S.mm(pq, [(onesf, ysq[0][:, 0:n]), (onesf, ysq[1][:, 0:n])])
                S.act(msb[:, 0:n], pm, AF.Identity, scale=1.0 / 256)
                S.act(m2[:, 0:n], pm, AF.Square, scale=1.0 / 256)
                S.stt(rstd[:, 0:n], pq, 1.0 / 256, m2[:, 0:n], ALU.mult, ALU.subtract)
                S.act(rstd[:, 0:n], rstd[:, 0:n], AF.Sqrt, bias=EPS)
                S.recip(rstd[:, 0:n], rstd[:, 0:n])
                for cc in range(2):
                    e_ = 'vector' if cc == 0 else 'gpsimd'
                    S.tt(t1[cc][:, 0:n], y[cc][:, 0:n], msb[:, 0:n], ALU.subtract, eng=e_)
                    S.tt(t1[cc][:, 0:n], t1[cc][:, 0:n], rstd[:, 0:n], ALU.mult, eng=e_)
                    S.act(sb[cc][:, 0:n], t1[cc][:, 0:n], AF.Silu, bias=pp[:, 68 + cc:69 + cc], scale=pp[:, 66 + cc:67 + cc])
                yob = yo[bi % 2]
                for co in range(2):
                    ppw = PS[4 + co][:, 0:n]
                    S.mm(ppw, [(WPW[:, ci, co * 128:(co + 1) * 128], sb[ci][:, 0:n]) for ci in range(2)])
                    S.act(yob[:, co, 0:n], ppw, AF.Identity, bias=pp[:, 70 + co:71 + co])
                S.dma(YT[0:256, tok0:tok0 + n].rearrange("(c p) t -> p c t", p=128), yob[:, :, 0:n], q='gpsimd')
            S.barrier()
            A.reset(mk)

        def phase_att(l, need_ctx, nqb):
            mk = A.mark()
            QK = A.alloc([128, 4, NT], BF16)
            VA = A.alloc([128, NTILE, 2, 128], BF16)
            QZ = A.alloc([128, 4, NT], BF16)
            VB = A.alloc([128, NTILE, 2, 128], BF16)
            rope = A.alloc([128, 32, 64], F32)
            bc = A.alloc([128, 384], F32)
            S.dma(rope, ropeA_in.rearrange("p (t k) -> p t k", k=64))
            S.dma(bc, W[l]['bc'][:, 512:896])
            S.memset(VA, 0.0)
            S.memset(VA[:, :, :, 64:65], 1.0)
            S.memset(QZ, 0.0, eng='gpsimd')
            S.memset(VB, 0.0, eng='gpsimd')
            S.memset(VB[:, :, :, 0:1], 1.0, eng='gpsimd')
            NB = 2
            raw = [A.alloc([128, NB, 512], BF16) for _ in range(2)]
            f1 = A.alloc([128, NB, 6, 64], F32)
            f2 = A.alloc([128, NB, 6, 64], F32)
            ssq = A.alloc([128, NB, 6], F32)
            qr = [A.alloc([128, NB, 8, 64], BF16) for _ in range(2)]
            tA = A.alloc([128, NB, 6, 32], F32)
            tB = A.alloc([128, NB, 6, 32], F32)
            for bi in range(NTILE // NB):
                tl0 = bi * NB
                rw = raw[bi % 2]
                qb_ = qr[bi % 2]
                S.dma(rw, PT[tl0 * 128:(tl0 + NB) * 128, 784:1296].rearrange("(t p) c -> p t c", p=128))
                qk = rw[:, :, 0:384].rearrange("p t (h e) -> p t h e", e=64)
                S.tt(f1, qk, qk, ALU.mult)
                S.reduce(ssq, f1)
                S.act(ssq, ssq, AF.Sqrt, bias=EPS, scale=1.0 / 64)
                S.recip(ssq, ssq)
                S.tt(f1, qk, ssq.unsqueeze(3).to_broadcast([128, NB, 6, 64]), ALU.mult)
                gq = bc.rearrange("p (h e) -> p h e", e=64).unsqueeze(1).to_broadcast([128, NB, 6, 64])
                S.tt(f2, f1, gq, ALU.mult, eng='gpsimd')
                is_lat = tl0 < 32
                if is_lat:
                    cosv = rope[:, tl0:tl0 + NB, 0:32].unsqueeze(2).to_broadcast([128, NB, 6, 32])
                    sinv = rope[:, tl0:tl0 + NB, 32:64].unsqueeze(2).to_broadcast([128, NB, 6, 32])
                    x1 = f2[:, :, :, 0:32]
                    x2 = f2[:, :, :, 32:64]
                    o1 = f1[:, :, :, 0:32]
                    o2 = f1[:, :, :, 32:64]
                    S.tt(tA, x1, cosv, ALU.mult)
                    S.tt(tB, x2, sinv, ALU.mult, eng='gpsimd')
                    S.tt(o1, tA, tB, ALU.subtract)
                    S.tt(tA, x2, cosv, ALU.mult)
                    S.tt(tB, x1, sinv, ALU.mult, eng='gpsimd')
                    S.tt(o2, tA, tB, ALU.add)
                    src = f1
                else:
                    src = f2
                S.copy(qb_[:, :, 0:4, :], src[:, :, 0:4, :])
                kdst = qb_[:, :, 4:8, :].rearrange("p t (k r) e -> p t k r e", r=2)
                for r_ in range(2):
                    S.copy(kdst[:, :, :, r_, :], src[:, :, 4:6, :], eng='gpsimd')
                for t_ in range(NB):
                    tl = tl0 + t_
                    pt = PSB[6][:, 0:512].rearrange("p (j k) -> p j k", k=128)
                    for j in range(4):
                        S.transpose(pt[:, j, :], qb_[:, t_, 2 * j:2 * j + 2, :].rearrange("p a e -> p (a e)"), identb)
                    S.copy(QK[:, :, tl * 128:(tl + 1) * 128], pt, eng=('vector' if t_ % 2 == 0 else 'scalar'))
                    for j in range(2):
                        S.copy(QZ[0:64, 2 * j, tl * 128:(tl + 1) * 128], pt[0:64, j, :], eng='vector')
                        S.copy(QZ[64:128, 2 * j + 1, tl * 128:(tl + 1) * 128], pt[64:128, j, :], eng='scalar')
                    vv = rw[:, t_, 384:512].rearrange("p (k e) -> p k e", e=64)
                    S.copy(VA[:, tl, :, 0:64], vv, eng='gpsimd')
                    S.copy(VB[:, tl, :, 64:128], vv, eng='gpsimd')
            NS, LA, NP = 4, 3, 5
            Pt = [A.alloc([128, 512], BF16) for _ in range(NP)]
            rsb = A.alloc([128, 512], F32)
            bcs = A.alloc([128, 512], F32)
            Yo = [A.alloc([128, 512], BF16) for _ in range(2)]
            qblocks = [(qb * 512, 512, list(range(NTILE))) for qb in range(nqb)]
            if need_ctx:
                qblocks.append((TL, 256, [32, 33]))
            it = 0
            gi = 0
            cvs = [A.alloc([128, 2048], F32) for _ in range(2)]
            cvb = [A.alloc([128, 2048], BF16) for _ in range(2)]
            pieces = []
            for k in range(8):
                pieces.append((W[l]['w_out'][k * 128:(k + 1) * 128, :], WOb[k * 128:(k + 1) * 128, :], 1024))
            for k in range(8):
                for c_ in range(2):
                    pieces.append((W[l]['w_up'][k * 128:(k + 1) * 128, c_ * 2048:(c_ + 1) * 2048],
                                   WUb[k * 128:(k + 1) * 128, c_ * 2048:(c_ + 1) * 2048], 2048))
            for k in range(32):
                pieces.append((W[l]['w_down'][k * 128:(k + 1) * 128, :], WDb[k * 128:(k + 1) * 128, :], 1024))
            if l + 1 < nlayers:
                for k in range(8):
                    pieces.append((W[l + 1]['w_in'][k * 128:(k + 1) * 128, 0:2048], WIb[k * 128:(k + 1) * 128, 0:2048], 2048))
                    pieces.append((W[l + 1]['w_in'][k * 128:(k + 1) * 128, 2048:PIN], WIb[k * 128:(k + 1) * 128, 2048:PIN], PIN - 2048))
            n_iters = 4 * len(qblocks)
            ppi = (len(pieces) + n_iters - 1) // n_iters
            pci = [0]

            def convert_some():
                for _ in range(ppi):
                    if pci[0] >= len(pieces):
                        return
                    src, dst, w = pieces[pci[0]]
                    sg = cvs[pci[0] % 2]
                    cb = cvb[pci[0] % 2]
                    S.dma(sg[:, 0:w], src, q='sync')
                    S.copy(cb[:, 0:w], sg[:, 0:w], eng='gpsimd')
                    S.dma(dst, cb[:, 0:w], q='gpsimd')
                    pci[0] += 1
            for h in range(4):
                pair, half = h // 2, h % 2
                kvh = pair
                p0 = 64 * half
                for (q0, nq, kts) in qblocks:
                    convert_some()
                    O = PS[4 + it % 2]
                    rhsq = QZ[:, h, q0:q0 + nq]
                    nk = len(kts)
                    sps = {}

                    def smm(i):
                        kt = kts[i]
                        sp = PS[(gi + i) % NS][:, 0:nq]
                        S.mm(sp, [(QK[:, 2 + kvh, kt * 128:(kt + 1) * 128], rhsq)])
                        sps[i] = sp
                    for i in range(min(LA, nk)):
                        smm(i)
                    for i in range(nk):
                        if i + LA < nk:
                            smm(i + LA)
                        sp = sps.pop(i)
                        kt = kts[i]
                        pt_ = Pt[(gi + i) % NP][:, 0:nq]
                        S.act(pt_, sp, AF.Exp, scale=0.125)
                        if KEEPWARM:
                            S.add('tensor', lambda e: e.matmul(PS[7][:, 0:KEEPWARM], identb, QK[:, 0, 0:KEEPWARM],
                                                               start=True, stop=True), reads=[], writes=[])
                        if half == 0:
                            S.mm(O[:, 0:nq], [(VA[:, kt, kvh, :], pt_)], start=(i == 0), stop=(i == nk - 1), acc=(i > 0))
                        else:
                            S.mm(O[:, 0:nq], [(VB[:, kt, kvh, :], pt_)], start=(i == 0), stop=(i == nk - 1), acc=(i > 0))
                    gi += nk
                    yo_ = Yo[it % 2]
                    pb = PS[6]
                    if half == 0:
                        S.recip(rsb[64:65, 0:nq], O[64:65, 0:nq])
                        S.mm(pb[0:64, 0:nq], [(onesf[64:65, 0:64], rsb[64:65, 0:nq])])
                        S.copy(bcs[0:64, 0:nq], pb[0:64, 0:nq], eng='vector')
                        S.tt(yo_[0:64, 0:nq], O[0:64, 0:nq], bcs[0:64, 0:nq], ALU.mult)
                        S.dma(YT[512 + h * 64:512 + (h + 1) * 64, q0:q0 + nq], yo_[0:64, 0:nq], q='gpsimd')
                    else:
                        S.recip(rsb[0:1, 0:nq], O[0:1, 0:nq])
                        S.mm(pb[:, 0:nq], [(onesf[0:1, :], rsb[0:1, 0:nq])])
                        S.copy(bcs[64:128, 0:nq], pb[64:128, 0:nq], eng='vector')
                        S.tt(yo_[64:128, 0:nq], O[64:128, 0:nq], bcs[64:128, 0:nq], ALU.mult)
                        S.dma(YT[512 + h * 64:512 + (h + 1) * 64, q0:q0 + nq], yo_[64:128, 0:nq], q='gpsimd')
                    it += 1
            S.barrier()
            A.reset(mk)

        def phase_scan(l, kind, zT, need_ctx, nlt):
            mk = A.mark()
            is_gla = (kind == 'gla')
            pc0 = 0 if is_gla else 1296
            yrow = 256 if is_gla else 768
            qk_all = A.alloc([128, NTILE, 256], BF16)
            vg_all = A.alloc([128, NTILE, 512], BF16)
            QKT = A.alloc([128, 2, NT], BF16)
            oacc = A.alloc([128, NTILE, 256], F32)
            PTv = PT.rearrange("(t p) c -> p t c", p=128)
            for i_ in range(17):
                sl = slice(2 * i_, 2 * i_ + 2)
                S.dma(qk_all[:, sl, :], PTv[:, sl, pc0:pc0 + 256], q=('sync' if i_ % 2 == 0 else 'gpsimd'))
                S.dma(vg_all[:, sl, :], PTv[:, sl, pc0 + 256:pc0 + 768], q=('gpsimd' if i_ % 2 == 0 else 'sync'))
            gn = A.alloc([128, 256], F32)
            S.dma(gn, W[l]['bc'][:, (0 if is_gla else 256):(256 if is_gla else 512)])
            mk2 = A.mark()
            if not is_gla:
                rope = A.alloc([128, 32, 32], F32)
                S.dma(rope, ropeR_in.rearrange("p (t k) -> p t k", k=32))
                NB = 4
                tA = A.alloc([128, NB, 8, 16], F32)
                tB = A.alloc([128, NB, 8, 16], F32)
                tC = A.alloc([128, NB, 8, 16], F32)
                tD = A.alloc([128, NB, 8, 16], F32)
                for bi in range(32 // NB):
                    tl0 = bi * NB
                    v4 = qk_all[:, tl0:tl0 + NB, :].rearrange("p t (h e) -> p t h e", e=32)
                    x1 = v4[:, :, :, 0:16]
                    x2 = v4[:, :, :, 16:32]
                    cosv = rope[:, tl0:tl0 + NB, 0:16].unsqueeze(2).to_broadcast([128, NB, 8, 16])
                    sinv = rope[:, tl0:tl0 + NB, 16:32].unsqueeze(2).to_broadcast([128, NB, 8, 16])
                    S.tt(tA, x1, cosv, ALU.mult)
                    S.tt(tB, x2, sinv, ALU.mult, eng='gpsimd')
                    S.tt(tC, x2, cosv, ALU.mult)
                    S.tt(tD, x1, sinv, ALU.mult, eng='gpsimd')
                    S.tt(x1, tA, tB, ALU.subtract)
                    S.tt(x2, tC, tD, ALU.add, eng='gpsimd')
            for tl in range(NTILE):
                pt = PSB[6][:, 0:256].rearrange("p (j k) -> p j k", k=128)
                for j in range(2):
                    S.transpose(pt[:, j, :], qk_all[:, tl, j * 128:(j + 1) * 128], identb)
                S.copy(QKT[:, :, tl * 128:(tl + 1) * 128], pt, eng=('vector' if tl % 2 == 0 else 'scalar'))
            S.barrier()
            A.reset(mk2)
            if stop_after == kind + '_prep':
                raise _Stop()
            Sf = A.alloc([128, 256], F32)
            Sb = A.alloc([128, 256], BF16)
            kiT = [A.alloc([128, 4, 128], BF16) for _ in range(2)]
            kvm = A.alloc([128, 256], F32)
            if is_gla:
                WA = A.alloc([32, 256], BF16)
                was = A.alloc([32, 256], F32)
                S.dma(was, W[l]['waug'])
                S.copy(WA, was)
                ee = [A.alloc([128, 128], F32) for _ in range(2)]
                ll = [A.alloc([128, 128], F32) for _ in range(2)]
                E1s = [A.alloc([128, 128], F32) for _ in range(2)]
                E2s = [A.alloc([128, 128], F32) for _ in range(2)]
                E3s = [A.alloc([128, 128], F32) for _ in range(2)]
            LNS = math.log(32 ** -0.5)
            NQ = 6
            qdT = [A.alloc([128, 128], BF16) for _ in range(NQ)]
            kend = [A.alloc([128, 128], BF16) for _ in range(NQ)]
            scm = [A.alloc([128, 4, 128], BF16) for _ in range(3)]
            if is_gla:
                dcs = [A.alloc([128, 1], F32) for _ in range(NQ)]
            for d_ in range(2):
                S.memset(Sf, 0.0)
                S.memset(Sb, 0.0)
                order = [32, 33] + list(range(32)) if d_ == 0 else [33, 32] + list(range(31, -1, -1))
                if stop_after and stop_after.startswith(kind + '_n'):
                    order = order[:int(stop_after[len(kind) + 2:])]
                n_st = len(order)
                want = [need_ctx or ch < nlt for ch in order]
                tks = [slice(ch * 128, (ch + 1) * 128) for ch in order]
                if not is_gla:
                    o_ = 1156 + d_ * 384
                    cE1 = cst[:, o_:o_ + 128]
                    cE2 = cst[:, o_ + 128:o_ + 256]
                    cE3 = cst[:, o_ + 256:o_ + 384]
                    cdc = cst[:, 1924:1925]

                def stA(j):
                    if is_gla:
                        S.mm(PS[0][:, 0:128], [(zT[0:32, tks[j]], WA[0:32, d_ * 128:(d_ + 1) * 128])])

                def stB(j):
                    if is_gla:
                        S.act(ee[j % 2], PS[0][:, 0:128], AF.Exp, scale=-1.0)
                        S.act(ll[j % 2], ee[j % 2], AF.Ln, bias=1.0)

                def stC(j):
                    if is_gla:
                        S.mm(PS[1][:, 0:128], [(ll[j % 2], TRI[d_])])
                        S.mm(PS[1][:, 128:256], [(TRIR[d_], ll[j % 2])])

                def stD(j):
                    if is_gla:
                        pbT = PS[1][:, 0:128]
                        S.act(E1s[j % 2], pbT, AF.Exp, bias=LNS)
                        S.act(E2s[j % 2], pbT, AF.Exp, scale=-1.0)
                        S.act(E3s[j % 2], PS[1][:, 128:256], AF.Exp)
                        col = 127 if d_ == 0 else 0
                        S.act(dcs[j % NQ], pbT[:, col:col + 1], AF.Exp)

                def stE(j):
                    E1, E2, E3 = (E1s[j % 2], E2s[j % 2], E3s[j % 2]) if is_gla else (cE1, cE2, cE3)
                    S.tt(qdT[j % NQ], QKT[:, 0, tks[j]], E1, ALU.mult, eng='gpsimd')
                    if want[j]:
                        for hh in range(4):
                            S.stt(kiT[j % 2][:, hh, :], QKT[:, 1, tks[j]], HM[:, hh:hh + 1], E2, ALU.mult, ALU.mult)
                    S.tt(kend[j % NQ], qk_all[:, order[j], 128:256], E3, ALU.mult, eng='gpsimd')

                def stF(j):
                    if want[j]:
                        psc = PS[2 + j % 2][:, :].rearrange("p (h c) -> p h c", c=128)
                        for hh in range(4):
                            S.mm(psc[:, hh, :], [(kiT[j % 2][:, hh, :], qdT[j % NQ])])

                def stG(j):
                    if want[j]:
                        psc = PS[2 + j % 2][:, :].rearrange("p (h c) -> p h c", c=128)
                        S.tt(scm[j % 3], psc, bcast_mid(maskb16[:, d_, :], 4), ALU.mult)

                def stH(j):
                    ch = order[j]
                    qd = qdT[j % NQ]
                    ke = kend[j % NQ]
                    sm = scm[j % 3]
                    dc = dcs[j % NQ] if is_gla else cdc
                    vch = vg_all[:, ch, 0:256]
                    if want[j]:
                        po = PS[4 + j % 2][:, 0:256]
                        S.mm(po, [(qd, Sb)], start=True, stop=False)
                        for hh in range(4):
                            S.mm(po[:, hh * 64:(hh + 1) * 64], [(sm[:, hh, :], vch[:, hh * 64:(hh + 1) * 64])],
                                 start=False, stop=True, acc=True)
                        if d_ == 0:
                            S.copy(oacc[:, ch, :], po, eng='scalar')
                        else:
                            S.tt(oacc[:, ch, :], oacc[:, ch, :], po, ALU.add)
                    pkv = PS[6 + j % 2][:, 0:256]
                    S.mm(pkv, [(ke, vch)])
                    S.tt(kvm, pkv, BDm, ALU.mult)
                    S.stt(Sb, Sf, dc, kvm, ALU.mult, ALU.add)
                    S.stt(Sf, Sf, dc, kvm, ALU.mult, ALU.add)

                stages = [(stH, 0), (stG, 1), (stF, 2), (stE, 3), (stD, 4), (stC, 5), (stB, 6), (stA, 7)]
                for t in range(-7, n_st):
                    for fn_, lead in stages:
                        j = t + lead
                        if 0 <= j < n_st:
                            fn_(j)
            if stop_after and stop_after.startswith(kind + '_n'):
                raise _Stop()
            NB = 2
            f1 = A.alloc([128, NB, 4, 64], F32)
            ssq = A.alloc([128, NB, 4], F32)
            sg = A.alloc([128, NB, 256], F32)
            yb = [A.alloc([128, NB, 256], BF16) for _ in range(2)]
            yts = [A.alloc([128, 2, NB * 128], BF16) for _ in range(2)]
            ntl = NTILE if need_ctx else nlt
            for bi in range(ntl // NB):
                tl0 = bi * NB
                o4 = oacc[:, tl0:tl0 + NB, :].rearrange("p t (h e) -> p t h e", e=64)
                S.tt(f1, o4, o4, ALU.mult)
                S.reduce(ssq, f1)
                S.act(ssq, ssq, AF.Sqrt, bias=EPS, scale=1.0 / 64)
                S.recip(ssq, ssq)
                S.tt(f1, o4, ssq.unsqueeze(3).to_broadcast([128, NB, 4, 64]), ALU.mult)
                f1f = f1.rearrange("p t h e -> p t (h e)")
                S.tt(f1f, f1f, bcast_mid(gn, NB), ALU.mult, eng='gpsimd')
                S.act(sg, vg_all[:, tl0:tl0 + NB, 256:512], AF.Silu)
                y_ = yb[bi % 2]
                S.tt(y_, f1f, sg, ALU.mult)
                yt_ = yts[bi % 2]
                pt = PSB[6][:, 0:2 * NB * 128].rearrange("p (j k) -> p j k", j=2)
                for t_ in range(NB):
                    for j in range(2):
                        S.transpose(pt[:, j, t_ * 128:(t_ + 1) * 128], y_[:, t_, j * 128:(j + 1) * 128], identb)
                S.copy(yt_, pt, eng='scalar')
                S.dma(YT[yrow:yrow + 256, tl0 * 128:(tl0 + NB) * 128].rearrange("(c p) t -> p c t", p=128), yt_, q='gpsimd')
            S.barrier()
            A.reset(mk)

        def phase_C(l, Xsrc, last, need_ctx, nlg):
            mk = A.mark()
            WOUT = A.alloc([128, 8, D], BF16)
            WUP = A.alloc([128, 8, 4 * D], BF16)
            WDN = A.alloc([128, 32, D], BF16)
            hid = A.alloc([128, 32, 256], BF16)
            S.dma(WOUT, WOb.rearrange("(k p) n -> p k n", p=128))
            for k2 in range(4):
                S.dma(WUP[:, 2 * k2:2 * k2 + 2, :], WUb[k2 * 256:(k2 + 1) * 256, :].rearrange("(k p) n -> p k n", p=128),
                      q=('gpsimd' if k2 % 2 == 0 else 'sync'))
            for k8 in range(4):
                S.dma(WDN[:, 8 * k8:8 * k8 + 8, :], WDb[k8 * 1024:(k8 + 1) * 1024, :].rearrange("(k p) n -> p k n", p=128),
                      q=('sync' if k8 % 2 == 0 else 'gpsimd'))
            xgs = [A.alloc([128, 8, 256], F32) for _ in range(2)]
            Yg = A.alloc([128, 8, 256], BF16)
            sq = A.alloc([128, 8, 256], BF16)
            tmp = [A.alloc([128, 256], F32) for _ in range(3)]
            h2T = A.alloc([128, 8, 256], BF16)
            rr = [tmp[1], tmp[2]] + [A.alloc([128, 256], F32) for _ in range(2)]
            Xv = Xsrc.rearrange("(c p) t -> p c t", p=128)
            XSv = XST.rearrange("(c p) t -> p c t", p=128)
            OUv = outT.rearrange("(c p) t -> p c t", p=128)
            YTv = YT.rearrange("(c p) t -> p c t", p=128)
            ngroups = 17 if need_ctx else nlg
            n = 256
            S.dma(xgs[0], Xv[:, :, 0:n])
            for g in range(ngroups):
                t0 = g * 256
                m = 0 if g < 16 else 1
                xg = xgs[g % 2]
                S.dma(Yg, YTv[:, :, t0:t0 + n], q='sync')
                if g + 1 < ngroups:
                    S.dma(xgs[(g + 1) % 2], Xv[:, :, t0 + n:t0 + 2 * n])
                G1 = MOD[:, l, 2, :, m]
                G2 = MOD[:, l, 5, :, m]
                for dc in range(8):
                    po = PS[1 + dc % 2][:, 0:n]
                    S.mm(po, [(WOUT[:, kc, dc * 128:(dc + 1) * 128], Yg[:, kc, :]) for kc in range(8)])
                    S.stt(xg[:, dc, :], po, G1[:, dc:dc + 1], xg[:, dc, :], ALU.mult, ALU.add)
                norm_mod(xg, n, sq, tmp, h2T, MOD[:, l, 3, :, m], MOD[:, l, 4, :, m], 0)
                for fc in range(32):
                    pu = PS[3 + fc % 4][:, 0:n]
                    S.mm(pu, [(WUP[:, kc, fc * 128:(fc + 1) * 128], h2T[:, kc, :]) for kc in range(8)])
                    r_ = rr[fc % 4]
                    S.act(r_, pu, AF.Relu)
                    S.tt(hid[:, fc, :], r_, r_, ALU.mult, eng=('vector' if fc % 2 == 0 else 'gpsimd'))
                for dc in range(8):
                    pd = PS[1 + dc % 2][:, 0:n]
                    S.mm(pd, [(WDN[:, fc, dc * 128:(dc + 1) * 128], hid[:, fc, :]) for fc in range(32)])
                    S.stt(xg[:, dc, :], pd, G2[:, dc:dc + 1], xg[:, dc, :], ALU.mult, ALU.add)
                if not last:
                    S.dma(XSv[:, :, t0:t0 + n], xg, q='gpsimd')
                    if dbg:
                        S.dma(dbg_out['X1'].rearrange("(c p) t -> p c t", p=128)[:, :, t0:t0 + n], xg, q='gpsimd')
                else:
                    S.act(sq, xg, AF.Square)
                    ps = PS[0][:, 0:n]
                    S.mm(ps, [(onesb, sq[:, c, :]) for c in range(8)])
                    rstd = tmp[0]
                    S.act(rstd, ps, AF.Sqrt, bias=EPS)
                    S.recip(rstd, rstd)
                    for c in range(8):
                        S.stt(xg[:, c, :], xg[:, c, :], pp[:, 134 + c:135 + c], rstd, ALU.mult, ALU.mult,
                              eng=('vector' if c % 2 == 0 else 'gpsimd'))
                    S.dma(OUv[:, :, t0:t0 + n], xg, q='gpsimd')
            S.barrier()
            A.reset(mk)

        def chk(name):
            if stop_after == name:
                raise _Stop()
        try:
          phase_mod()
          chk('mod')
          for l in range(nlayers):
              need_ctx = l < DEPTH - 1
              last = (l == DEPTH - 1)
              Xsrc = xT_in if l == 0 else XST
              S.dma(pp, W[l]['pp'])
              mk = A.mark()
              zT = A.alloc([32, NT], BF16)
              uTl = A.alloc([128, 2, TL + 30], BF16)
              uTc = A.alloc([128, 2, TC + 30], BF16)
              S.memset(zT, 1.0)
              S.memset(uTl, 0.0, eng='gpsimd')
              S.memset(uTc, 0.0, eng='gpsimd')
              phase_A(l, Xsrc, zT, uTl, uTc)
              if dbg and l == 0:
                  for t_ in range(NTILE):
                      S.dma(dbg_out['PT'][t_ * 128:(t_ + 1) * 128, :], PT[t_ * 128:(t_ + 1) * 128, :], q='gpsimd')
              chk('A')
              hf = 2 if last else 1
              phase_conv(l, uTl, uTc, need_ctx, 8 // hf)
              chk('conv')
              phase_att(l, need_ctx, 8 // hf)
              chk('att')
              phase_scan(l, 'gla', zT, need_ctx, 32 // hf)
              chk('gla')
              phase_scan(l, 'ret', zT, need_ctx, 32 // hf)
              chk('ret')
              if dbg and l == 0:
                  for c_ in range(8):
                      S.dma(dbg_out['YT'][c_ * 128:(c_ + 1) * 128, :], YT[c_ * 128:(c_ + 1) * 128, :], q='gpsimd')
              S.barrier()
              A.reset(mk)
              phase_C(l, Xsrc, last, need_ctx, 16 // hf)
        except _Stop:
            if dbg:
                for c_ in range(8):
                    S.dma(dbg_out['YT'][c_ * 128:(c_ + 1) * 128, :], YT[c_ * 128:(c_ + 1) * 128, :], q='gpsimd')
        S.finish()
        S.emit()
        print("ops", S.n_ops, "waits", S.n_waits, {e: len(S.items[e]) for e in ENGS})
    return nc


def make_consts(rev=False):
    cst = np.zeros((128, NCST), np.float64)
    cst[:, 0:128] = np.eye(128)
    s = np.arange(128)[:, None]
    c = np.arange(128)[None, :]
    cst[:, 128:256] = np.where(s <= c, -1.0 / 16, 0.0)
    cst[:, 256:384] = np.where(s >= c, -1.0 / 16, 0.0)
    cst[:, 384:512] = np.where(s > c, -1.0 / 16, 0.0)
    cst[:, 512:640] = np.where(s < c, -1.0 / 16, 0.0)
    cst[:, 640:768] = np.where(s <= c, 1.0, 0.0)
    cst[:, 768:896] = np.where(s >= c, 1.0, 0.0)
    f = np.arange(128)[:, None] // 32
    he = np.arange(256)[None, :] // 64
    cst[:, 896:1152] = (f == he).astype(np.float64)
    for h in range(4):
        cst[:, 1152 + h] = (np.arange(128) // 32 == h)
    lg = np.log1p(-np.exp2(-5.0 - np.arange(4)))
    lgf = lg[np.arange(128) // 32]
    sc = 32 ** -0.5
    cc = np.arange(128)
    cst[:, 1156:1284] = np.exp(lgf[:, None] * (cc[None, :] + 1))
    cst[:, 1284:1412] = np.exp(-lgf[:, None] * (cc[None, :] + 1)) * sc
    cst[:, 1412:1540] = np.exp(lgf[None, :] * (127 - cc[:, None])) * sc
    cst[:, 1540:1668] = np.exp(lgf[:, None] * (128 - cc[None, :]))
    cst[:, 1668:1796] = np.exp(-lgf[:, None] * (128 - cc[None, :])) * sc
    cst[:, 1796:1924] = np.exp(lgf[None, :] * cc[:, None]) * sc
    cst[:, 1924] = np.exp(lgf * 128)
    t = np.arange(TL)
    if rev:
        t = t[::-1]
    row = (t // 64).astype(np.float64)
    col = (t % 64).astype(np.float64)

    def tables(hd):
        n_ax = hd // 4
        inv = 10000.0 ** (-np.arange(n_ax, dtype=np.float64) / n_ax)
        inv = inv.astype(np.float32).astype(np.float64)
        ang = np.concatenate([row[:, None] * inv, col[:, None] * inv], -1).astype(np.float32)
        return np.cos(ang), np.sin(ang)
    ca, sa = tables(64)
    cr, sr = tables(32)
    ropeA = np.concatenate([ca, sa], -1).reshape(32, 128, 64).transpose(1, 0, 2).reshape(128, 32 * 64)
    ropeR = np.concatenate([cr, sr], -1).reshape(32, 128, 32).transpose(1, 0, 2).reshape(128, 32 * 32)
    return (cst.astype(np.float32), np.ascontiguousarray(ropeA, np.float32), np.ascontiguousarray(ropeR, np.float32))


def fm(v):
    v = np.asarray(v, np.float32)
    return v.reshape(-1, 128).T


def make_in_maps(inp):
    f32 = lambda a: np.ascontiguousarray(np.asarray(a, np.float32))
    consts = [make_consts(False), make_consts(True)]
    per_layer = [[], []]
    for rev in (0, 1):
        for l in range(DEPTH):
            pp = np.zeros((128, NPP), np.float32)
            pp[:, 0:8] = fm(inp['norm1_g'][l])
            pp[:, 8:16] = fm(inp['norm2_g'][l])
            pp[:, 16:64] = fm(inp['b_mod'][l])
            pp[:, 64:66] = fm(inp['conv_b_dw'][l])
            pp[:, 66:68] = fm(inp['conv_ln_g'][l])
            pp[:, 68:70] = fm(inp['conv_ln_b'][l])
            pp[:, 70:72] = fm(inp['conv_b_pw'][l])
            wdw = np.asarray(inp['conv_w_dw'][l], np.float32)
            if rev:
                wdw = wdw[::-1]
            for cc in range(2):
                pp[:, 72 + cc * 31:72 + (cc + 1) * 31] = wdw[:, cc * 128:(cc + 1) * 128].T
            pp[:, 134:142] = fm(inp['final_norm_g'])
            bc = np.zeros((128, NBC), np.float32)
            bc[:, 0:256] = np.asarray(inp['gla_norm_g'][l], np.float32).reshape(1, 256)
            bc[:, 256:512] = np.asarray(inp['ret_norm_g'][l], np.float32).reshape(1, 256)
            qg = np.asarray(inp['att_q_norm_g'][l], np.float32)
            kg = np.asarray(inp['att_k_norm_g'][l], np.float32)
            bc[:, 512:896] = np.concatenate([qg, qg, qg, qg, kg, kg])[None, :]
            waug = np.zeros((32, 256), np.float32)
            fo, bo = (128, 0) if rev else (0, 128)
            waug[0:16, fo:fo + 128] = inp['gla_w_a_f'][l]
            waug[0:16, bo:bo + 128] = inp['gla_w_a_b'][l]
            waug[16, fo:fo + 128] = inp['gla_b_a_f'][l]
            waug[16, bo:bo + 128] = inp['gla_b_a_b'][l]
            per_layer[rev].append({
                "w_mod%d" % l: f32(inp['w_mod'][l]), "w_in%d" % l: f32(inp['w_in'][l]),
                "w_out%d" % l: f32(inp['w_out'][l]), "w_up%d" % l: f32(inp['w_up'][l]),
                "w_down%d" % l: f32(inp['w_down'][l]), "pp%d" % l: pp, "bc%d" % l: bc,
                "waug%d" % l: waug, "wpw%d" % l: f32(inp['conv_w_pw'][l])})
    x = np.asarray(inp['x'], np.float32)
    ctx = np.asarray(inp['ctx'], np.float32)
    c = np.asarray(inp['c'], np.float32)
    cctx = np.asarray(inp['c_ctx'], np.float32)
    maps = []
    for core in range(8):
        b = core % 4
        rev = core // 4
        if rev:
            xT = np.ascontiguousarray(np.concatenate([x[b][::-1], ctx[b][::-1]], 0).T)
        else:
            xT = np.ascontiguousarray(np.concatenate([x[b], ctx[b]], 0).T)
        cvec = np.ascontiguousarray(np.stack([fm(c[b]), fm(cctx)], -1).reshape(128, 16))
        cst, ropeA, ropeR = consts[rev]
        m = {"xT": xT, "cvec": cvec, "cst": cst, "ropeA": ropeA, "ropeR": ropeR}
        for l in range(DEPTH):
            m.update(per_layer[rev][l])
        maps.append(m)
    return maps


_NC_CACHE = {}


def kernel(**inputs):
    maps = make_in_maps(inputs)
    if 'nc' not in _NC_CACHE:
        _NC_CACHE['nc'] = build_program()
    nc = _NC_CACHE['nc']
    res = run_bass_kernel_spmd(nc, maps, core_ids=list(range(8)))
    out = np.empty((4, TL, D), np.float32)
    h = TL // 2
    for b in range(4):
        out[b, 0:h] = res.results[b]["outT"].T
        out[b, h:TL] = res.results[b + 4]["outT"].T[::-1]
    return out
```

```python
import math
from contextlib import ExitStack
import numpy as np
import concourse.bass as bass
import concourse.mybir as mybir
from concourse.bass_utils import run_bass_kernel_spmd

F32 = mybir.dt.float32
BF16 = mybir.dt.bfloat16
U8 = mybir.dt.uint8
AF = mybir.ActivationFunctionType
ALU = mybir.AluOpType
AX = mybir.AxisListType

D = 1024
TL = 4096
TC = 256
NT = TL + TC
NTILE = NT // 128
PIN = 2576
PTW = 2064
EPS = 1e-6
NPP = 142
NBC = 896
NCST = 1925
DEPTH = 2

ENGS = ['tensor', 'vector', 'scalar', 'gpsimd', 'sync']
DMA_POOL = 24
KEEPWARM = 0
DSIZE = {F32: 4, BF16: 2, U8: 1}


class Sched:
    def __init__(self, nc, stack):
        self.nc = nc
        self.items = {e: [] for e in ENGS}
        self.seq = {e: 0 for e in ENGS}
        self.esem = {e: stack.enter_context(nc.semaphore("sq_" + e)) for e in ENGS if e != 'sync'}
        self.dpool = {}
        self.dcount = {}
        for q in ['sync', 'gpsimd']:
            self.dpool[q] = [stack.enter_context(nc.semaphore("dq_%s_%d" % (q, i))) for i in range(DMA_POOL)]
            self.dcount[q] = 0
        self.waited = {}
        self.recs = {}
        self.dram_rowlen = {}
        self.arena_name = None
        self.arena_allocs = []
        self.n_ops = 0
        self.n_waits = 0

    def box(self, ap):
        t = ap.tensor
        name = t.name
        off = int(ap.offset)
        pairs = [(int(s), int(c)) for s, c in ap.ap]
        mx = off + sum(s * (c - 1) for s, c in pairs if c > 0)
        if name in self.dram_rowlen:
            rowlen = self.dram_rowlen[name]
            es = 1
        else:
            rowlen = pairs[0][0]
            es = DSIZE[ap.dtype]
        r0 = off // rowlen
        r1 = mx // rowlen
        c0 = off % rowlen
        c1 = c0 + sum(s * (c - 1) for s, c in pairs if s < rowlen and c > 0)
        if c1 >= rowlen:
            c0, c1 = 0, rowlen - 1
        c0 *= es
        c1 = c1 * es + es - 1
        if name == self.arena_name:
            for (a, b, i) in self.arena_allocs:
                if a <= c0 < b:
                    assert c1 < b, "arena view crosses allocation"
                    name = (name, i)
                    break
            else:
                raise AssertionError("arena box not found")
        return name, (r0, r1, c0, c1)

    def _need(self, eng, ev, waits):
        sem, val = ev
        k = (eng, id(sem))
        if self.waited.get(k, 0) >= val:
            return
        self.waited[k] = val
        waits[id(sem)] = (sem, val)

    def add(self, eng, fn, reads=(), writes=(), dma=False, acc=False):
        waits = {}
        rb = [self.box(a) for a in reads]
        wb = [self.box(a) for a in writes]
        for name, b in rb:
            ps = isinstance(name, str) and name.startswith("ps")
            for r in self.recs.get(name, ()):
                if ps and r[5] != eng:
                    self._need(eng, r[6], waits)
                elif r[4] and not (r[1] < b[0] or b[1] < r[0] or r[3] < b[2] or b[3] < r[2]):
                    self._need(eng, r[6], waits)
        for name, b in wb:
            ps = isinstance(name, str) and name.startswith("ps")
            for r in self.recs.get(name, ()):
                if ps and r[5] != eng:
                    self._need(eng, r[6], waits)
                elif ps and eng == 'tensor':
                    continue
                elif not (r[1] < b[0] or b[1] < r[0] or r[3] < b[2] or b[3] < r[2]):
                    if acc and r[4] and r[5] == 'tensor':
                        continue
                    self._need(eng, r[6], waits)
        if dma:
            q = eng
            i = self.dcount[q]
            self.dcount[q] += 1
            sem = self.dpool[q][i % DMA_POOL]
            val = 16 * (i // DMA_POOL + 1)
            if i >= DMA_POOL:
                self._need(eng, (sem, val - 16), waits)
            ev = (sem, val)
            inc = (sem, 16)
        else:
            self.seq[eng] += 1
            ev = (self.esem[eng], self.seq[eng])
            inc = (self.esem[eng], 1)
        for name, b in wb:
            lst = self.recs.setdefault(name, [])
            lst[:] = [r for r in lst if not (b[0] <= r[0] and r[1] <= b[1] and b[2] <= r[2] and r[3] <= b[3])]
            lst.append([b[0], b[1], b[2], b[3], True, eng, ev])
        for name, b in rb:
            lst = self.recs.setdefault(name, [])
            if not dma:
                lst[:] = [r for r in lst if not ((not r[4]) and r[5] == eng and r[6][0] is ev[0]
                                                 and b[0] <= r[0] and r[1] <= b[1] and b[2] <= r[2] and r[3] <= b[3])]
            lst.append([b[0], b[1], b[2], b[3], False, eng, ev])
        self.items[eng].append((list(waits.values()), fn, inc))
        self.n_ops += 1
        self.n_waits += len(waits)
        return ev

    def barrier(self):
        for eng in ENGS:
            waits = {}
            for q in self.dpool:
                n = self.dcount[q]
                for j in range(min(n, DMA_POOL)):
                    uses = (n - 1 - j) // DMA_POOL + 1
                    self._need(eng, (self.dpool[q][j], 16 * uses), waits)
            for e in self.esem:
                if self.seq[e] > 0:
                    self._need(eng, (self.esem[e], self.seq[e]), waits)
            if waits:
                self.items[eng].append((list(waits.values()), None, None))
        self.recs = {}

    def dma(self, out, in_, q='sync'):
        return self.add(q, lambda e: e.dma_start(out=out, in_=in_), reads=[in_], writes=[out], dma=True)

    def mm(self, out, pairs, start=True, stop=True, acc=False):
        n = len(pairs)

        def fn(e):
            ins = None
            for i, p in enumerate(pairs):
                ins = e.matmul(out, p[0], p[1], start=(start and i == 0), stop=(stop and i == n - 1))
            return ins
        rd = []
        for p in pairs:
            rd += [p[0], p[1]]
        return self.add('tensor', fn, reads=rd, writes=[out], acc=acc)

    def transpose(self, out, in_, ident):
        return self.add('tensor', lambda e: e.transpose(out, in_, ident), reads=[in_, ident], writes=[out])

    def act(self, out, in_, func, bias=None, scale=None, accum_out=None):
        kw = {}
        rd = [in_]
        wr = [out]
        if bias is not None:
            kw['bias'] = bias
            if not isinstance(bias, (int, float)):
                rd.append(bias)
        if scale is not None:
            kw['scale'] = scale
            if not isinstance(scale, (int, float)):
                rd.append(scale)
        if accum_out is not None:
            kw['accum_out'] = accum_out
            wr.append(accum_out)
        return self.add('scalar', lambda e: e.activation(out, in_, func, **kw), reads=rd, writes=wr)

    def tt(self, out, in0, in1, op, eng='vector'):
        return self.add(eng, lambda e: e.tensor_tensor(out, in0, in1, op), reads=[in0, in1], writes=[out])

    def ts(self, out, in0, s1, s2, op0, op1=None, eng='vector'):
        rd = [in0]
        if not isinstance(s1, (int, float)):
            rd.append(s1)
        if s2 is not None and not isinstance(s2, (int, float)):
            rd.append(s2)
        if op1 is None:
            return self.add(eng, lambda e: e.tensor_scalar(out, in0, s1, None, op0), reads=rd, writes=[out])
        return self.add(eng, lambda e: e.tensor_scalar(out, in0, s1, s2, op0, op1), reads=rd, writes=[out])

    def stt(self, out, in0, scalar, in1, op0, op1, eng='vector'):
        eng = 'vector'
        rd = [in0, in1]
        if not isinstance(scalar, (int, float)):
            rd.append(scalar)
        return self.add(eng, lambda e: e.scalar_tensor_tensor(out, in0, scalar, in1, op0, op1), reads=rd, writes=[out])

    def copy(self, out, in_, eng='vector'):
        if eng == 'scalar':
            return self.add('scalar', lambda e: e.copy(out, in_), reads=[in_], writes=[out])
        return self.add(eng, lambda e: e.tensor_copy(out, in_), reads=[in_], writes=[out])

    def memset(self, ap, val, eng='vector'):
        return self.add(eng, lambda e: e.memset(ap, val), reads=[], writes=[ap])

    def recip(self, out, in_):
        return self.add('vector', lambda e: e.reciprocal(out, in_), reads=[in_], writes=[out])

    def reduce(self, out, in_, op=ALU.add, eng='vector'):
        return self.add(eng, lambda e: e.tensor_reduce(out, in_, AX.X, op), reads=[in_], writes=[out])

    def finish(self):
        self.barrier()

    def emit(self):
        nc = self.nc
        items = self.items

        def run(e, lst):
            for waits, fn, inc in lst:
                for sem, v in waits:
                    e.wait_ge(sem, v)
                if fn is None:
                    continue
                ins = fn(e)
                ins.then_inc(inc[0], inc[1])

        with nc.Block() as block:
            @block.tensor
            def _(e):
                run(e, items['tensor'])

            @block.vector
            def _(e):
                run(e, items['vector'])

            @block.scalar
            def _(e):
                run(e, items['scalar'])

            @block.gpsimd
            def _(e):
                run(e, items['gpsimd'])

            @block.sync
            def _(e):
                run(e, items['sync'])


class Arena:
    def __init__(self, S, t, nbytes):
        self.S = S
        self.t = t
        self.n = nbytes
        self.off = 0
        self.uid = 0
        S.arena_name = t.name

    def alloc(self, shape, dtype):
        es = DSIZE[dtype]
        free = 1
        for s in shape[1:]:
            free *= s
        nb = (free * es + 63) // 64 * 64
        assert self.off + nb <= self.n, "arena overflow: need %d have %d" % (self.off + nb, self.n)
        a = self.off
        self.off += nb
        self.uid += 1
        self.S.arena_allocs.append((a, a + nb, self.uid))
        ap = self.t[0:shape[0], a:a + free * es].bitcast(dtype)
        if len(shape) > 2:
            names = ["d%d" % i for i in range(len(shape) - 1)]
            kw = {names[i]: shape[i + 1] for i in range(len(shape) - 2)}
            ap = ap.rearrange("p (%s) -> p %s" % (" ".join(names), " ".join(names)), **kw)
        return ap

    def mark(self):
        return (self.off, len(self.S.arena_allocs))

    def reset(self, mark):
        self.off = mark[0]
        del self.S.arena_allocs[mark[1]:]


def bcast_mid(ap, n):
    return ap.unsqueeze(1).to_broadcast([ap.shape[0], n, ap.shape[1]])


def build_program(dbg=False, nlayers=DEPTH, stop_after=None):
    nc = bass.Bass("TRN2", target_bir_lowering=False)
    dt_in = lambda name, shape, dt=F32: nc.dram_tensor(name, list(shape), dt, kind="ExternalInput").ap()
    xT_in = dt_in("xT", [D, NT])
    cvec_in = dt_in("cvec", [128, 16])
    cst_in = dt_in("cst", [128, NCST])
    ropeA_in = dt_in("ropeA", [128, 32 * 64])
    ropeR_in = dt_in("ropeR", [128, 32 * 32])
    W = []
    for l in range(DEPTH):
        W.append(dict(
            w_mod=dt_in("w_mod%d" % l, [D, 6 * D]), w_in=dt_in("w_in%d" % l, [D, PIN]),
            w_out=dt_in("w_out%d" % l, [D, D]), w_up=dt_in("w_up%d" % l, [D, 4 * D]),
            w_down=dt_in("w_down%d" % l, [4 * D, D]), pp=dt_in("pp%d" % l, [128, NPP]),
            bc=dt_in("bc%d" % l, [128, NBC]), waug=dt_in("waug%d" % l, [32, 256]),
            wpw=dt_in("wpw%d" % l, [256, 256])))
    outT = nc.dram_tensor("outT", [D, TL // 2], F32, kind="ExternalOutput").ap()
    XST = nc.dram_tensor("XST", [D, NT], F32).ap()
    PT = nc.dram_tensor("PTs", [NT, PTW], BF16).ap()
    YT = nc.dram_tensor("YTs", [D, NT], BF16).ap()
    WOb = nc.dram_tensor("WOb", [D, D], BF16).ap()
    WUb = nc.dram_tensor("WUb", [D, 4 * D], BF16).ap()
    WDb = nc.dram_tensor("WDb", [4 * D, D], BF16).ap()
    WIb = nc.dram_tensor("WIb", [D, PIN], BF16).ap()
    dbg_out = {}
    if dbg:
        dbg_out['PT'] = nc.dram_tensor("dbgPT", [NT, PTW], BF16, kind="ExternalOutput").ap()
        dbg_out['YT'] = nc.dram_tensor("dbgYT", [D, NT], BF16, kind="ExternalOutput").ap()
        dbg_out['X1'] = nc.dram_tensor("dbgX1", [D, NT], F32, kind="ExternalOutput").ap()
        dbg_out['MOD'] = nc.dram_tensor("dbgMOD", [128, 192], F32, kind="ExternalOutput").ap()

    with ExitStack() as st:
        S = Sched(nc, st)
        for nm, ap_ in [("WIb", WIb), ("WOb", WOb), ("WUb", WUb), ("WDb", WDb), ("xT", xT_in), ("XST", XST), ("PTs", PT), ("YTs", YT), ("outT", outT), ("cvec", cvec_in),
                        ("cst", cst_in), ("ropeA", ropeA_in), ("ropeR", ropeR_in)]:
            S.dram_rowlen[nm] = ap_.shape[1]
        for l in range(DEPTH):
            for k, v in W[l].items():
                S.dram_rowlen[v.tensor.name] = v.shape[1]
        for k, v in dbg_out.items():
            S.dram_rowlen[v.tensor.name] = v.shape[1]
        ARENA_BYTES = 206 * 1024
        at = st.enter_context(nc.sbuf_tensor("arena", [128, ARENA_BYTES], U8))
        A = Arena(S, at, ARENA_BYTES)
        PS = [st.enter_context(nc.psum_tensor("ps%d" % i, [128, 512], F32)) for i in range(8)]
        PSB = [p[:].bitcast(BF16) for p in PS]

        cst = A.alloc([128, NCST], F32)
        S.dma(cst, cst_in)
        identb = A.alloc([128, 128], BF16)
        onesb = A.alloc([128, 128], BF16)
        onesf = A.alloc([128, 128], F32)
        maskb16 = A.alloc([128, 2, 128], BF16)
        MOD = A.alloc([128, DEPTH, 6, 8, 2], F32)
        pp = A.alloc([128, NPP], F32)
        S.copy(identb, cst[:, 0:128])
        S.memset(onesb, 1.0 / 1024)
        S.memset(onesf, 1.0)
        S.copy(maskb16[:, 0, :], cst[:, 640:768])
        S.copy(maskb16[:, 1, :], cst[:, 768:896])
        TRI = [cst[:, 128:256], cst[:, 256:384]]
        TRIR = [cst[:, 384:512], cst[:, 512:640]]
        BDm = cst[:, 896:1152]
        HM = cst[:, 1152:1156]
        base_mark = A.mark()

        class _Stop(Exception):
            pass

        def load_cast(dst, src, stg, width, engs=('vector', 'gpsimd')):
            n = dst.shape[1]
            i = 0
            c0 = 0
            while c0 < n:
                w = min(width, n - c0)
                sg = stg[i % len(stg)]
                S.dma(sg[:, 0:w], src[:, c0:c0 + w], q=('sync' if i % 2 == 0 else 'gpsimd'))
                S.copy(dst[:, c0:c0 + w], sg[:, 0:w], eng=engs[i % len(engs)])
                c0 += w
                i += 1

        def phase_mod():
            mk = A.mark()
            cs = A.alloc([128, 8, 2], F32)
            sc = A.alloc([128, 8, 2], F32)
            S.dma(cs, cvec_in.rearrange("p (c m) -> p c m", m=2))
            S.act(sc, cs, AF.Silu)
            wst = [A.alloc([128, 6144], F32) for _ in range(2)]
            acc = A.alloc([128, 48, 2], F32)
            ppl = A.alloc([128, NPP], F32)
            for l in range(nlayers):
                S.dma(ppl, W[l]['pp'])
                for k in range(8):
                    wk = wst[k % 2]
                    S.dma(wk, W[l]['w_mod'][k * 128:(k + 1) * 128, :], q=('sync' if k % 2 == 0 else 'gpsimd'))
                    pm = PS[k % 2][:, 0:96].rearrange("p (j m) -> p j m", m=2)
                    for j in range(48):
                        S.mm(pm[:, j, :], [(wk[:, j * 128:(j + 1) * 128], sc[:, k, :])])
                    if k == 0:
                        S.copy(acc, pm)
                    else:
                        S.tt(acc, acc, pm, ALU.add)
                bm = ppl[:, 16:64].unsqueeze(2).to_broadcast([128, 48, 2])
                S.tt(acc, acc, bm, ALU.add)
                a4 = acc.rearrange("p (w c) m -> p w c m", w=6)
                g1n = ppl[:, 0:8].unsqueeze(2).to_broadcast([128, 8, 2])
                g2n = ppl[:, 8:16].unsqueeze(2).to_broadcast([128, 8, 2])
                S.stt(MOD[:, l, 0], a4[:, 1], 1.0, g1n, ALU.add, ALU.mult)
                S.copy(MOD[:, l, 1], a4[:, 0])
                S.copy(MOD[:, l, 2], a4[:, 2])
                S.stt(MOD[:, l, 3], a4[:, 4], 1.0, g2n, ALU.add, ALU.mult)
                S.copy(MOD[:, l, 4], a4[:, 3])
                S.copy(MOD[:, l, 5], a4[:, 5])
            if dbg:
                S.dma(dbg_out['MOD'], MOD.rearrange("p l w c m -> p (l w c m)"), q='gpsimd')
            S.barrier()
            A.reset(mk)

        def norm_mod(xg, n, sq, tmp, hT, Acol, Bcol, pbank):
            S.act(sq[:, :, 0:n], xg[:, :, 0:n], AF.Square)
            ps = PS[pbank][:, 0:n]
            S.mm(ps, [(onesb, sq[:, c, 0:n]) for c in range(8)])
            rstd = tmp[0][:, 0:n]
            S.act(rstd, ps, AF.Sqrt, bias=EPS)
            S.recip(rstd, rstd)
            for c in range(8):
                t = tmp[1 + c % 2][:, 0:n]
                S.tt(t, xg[:, c, 0:n], rstd, ALU.mult, eng=('vector' if c % 2 == 0 else 'gpsimd'))
                S.act(hT[:, c, 0:n], t, AF.Identity, bias=Bcol[:, c:c + 1], scale=Acol[:, c:c + 1])

        def phase_A(l, Xsrc, zT, uTl, uTc):
            mk = A.mark()
            WIN = A.alloc([128, 8, PIN], BF16)
            if l == 0:
                stg = [A.alloc([128, PIN], F32) for _ in range(2)]
                for k in range(8):
                    S.dma(stg[k % 2], W[l]['w_in'][k * 128:(k + 1) * 128, :], q=('sync' if k % 2 == 0 else 'gpsimd'))
                    S.copy(WIN[:, k, :], stg[k % 2], eng=('vector' if k % 2 == 0 else 'gpsimd'))
            else:
                for k2 in range(4):
                    S.dma(WIN[:, 2 * k2:2 * k2 + 2, :], WIb[k2 * 256:(k2 + 1) * 256, :].rearrange("(k p) n -> p k n", p=128),
                          q=('sync' if k2 % 2 == 0 else 'gpsimd'))
            xgs = [A.alloc([128, 8, 512], F32) for _ in range(2)]
            sq = A.alloc([128, 8, 512], BF16)
            tmp = [A.alloc([128, 512], F32) for _ in range(3)]
            hTs = [A.alloc([128, 8, 512], BF16) for _ in range(2)]
            OTs = [A.alloc([128, PTW], BF16) for _ in range(2)]
            sig = [A.alloc([128, 512], F32) for _ in range(2)]
            Xv = Xsrc.rearrange("(c p) t -> p c t", p=128)
            colblocks = [(512, 1024), (1024, 1536), (1536, 2048), (2048, 2560), (2560, 2576)]
            ti = 0

            def prep_group(g):
                n = 512 if g < 8 else 256
                m = 0 if g < 8 else 1
                S.dma(xgs[g % 2][:, :, 0:n], Xv[:, :, g * 512:g * 512 + n])
                norm_mod(xgs[g % 2], n, sq, tmp, hTs[g % 2], MOD[:, l, 0, :, m], MOD[:, l, 1, :, m], 0)
            prep_group(0)
            for g in range(9):
                n = 512 if g < 8 else 256
                t0 = g * 512
                m = 0 if g < 8 else 1
                hT = hTs[g % 2]
                if g + 1 < 9:
                    prep_group(g + 1)
                for tt_ in range(n // 128):
                    OT = OTs[ti % 2]
                    for bi, (c0, c1) in enumerate(colblocks):
                        w = c1 - c0
                        ps = PS[1 + (bi % 2)][:, 0:w]
                        S.mm(ps, [(hT[:, c, tt_ * 128:(tt_ + 1) * 128], WIN[:, c, c0:c1]) for c in range(8)])
                        if bi % 2 == 0:
                            S.copy(OT[:, c0 - 512:c1 - 512], ps, eng='scalar')
                        else:
                            S.copy(OT[:, c0 - 512:c1 - 512], ps, eng='vector')
                    tok = t0 + tt_ * 128
                    S.dma(PT[tok:tok + 128, :], OT, q='gpsimd')
                    ti += 1
                for cc in range(2):
                    pa = PS[3][:, 0:n]
                    pg = PS[4][:, 0:n]
                    S.mm(pa, [(WIN[:, c, cc * 128:(cc + 1) * 128], hT[:, c, 0:n]) for c in range(8)])
                    S.mm(pg, [(WIN[:, c, 256 + cc * 128:256 + (cc + 1) * 128], hT[:, c, 0:n]) for c in range(8)])
                    sg = sig[cc][:, 0:n]
                    S.act(sg, pg, AF.Sigmoid)
                    if g < 8:
                        dst = uTl[:, cc, 15 + t0:15 + t0 + n]
                    else:
                        dst = uTc[:, cc, 15:15 + n]
                    S.tt(dst, pa, sg, ALU.mult)
                pz = PS[5][0:16, 0:n]
                S.mm(pz, [(WIN[:, c, 1280:1296], hT[:, c, 0:n]) for c in range(8)])
                S.copy(zT[0:16, t0:t0 + n], pz)
            S.barrier()
            A.reset(mk)

        def phase_conv(l, uTl, uTc, need_ctx, nblk):
            mk = A.mark()
            DG = A.alloc([128, 2, 31, 128], BF16)
            WPW = A.alloc([128, 2, 256], BF16)
            wst = A.alloc([128, 2, 256], F32)
            S.dma(wst, W[l]['wpw'].rearrange("(c p) n -> p c n", p=128))
            S.copy(WPW, wst)
            for cc in range(2):
                for j in range(31):
                    S.ts(DG[:, cc, j, :], identb, pp[:, 72 + cc * 31 + j:73 + cc * 31 + j], None, ALU.mult,
                         eng=('vector' if j % 2 == 0 else 'gpsimd'))
            y = [A.alloc([128, 512], F32) for _ in range(2)]
            ysq = [A.alloc([128, 512], F32) for _ in range(2)]
            msb = A.alloc([128, 512], F32)
            m2 = A.alloc([128, 512], F32)
            rstd = A.alloc([128, 512], F32)
            t1 = [A.alloc([128, 512], F32) for _ in range(2)]
            sb = [A.alloc([128, 512], BF16) for _ in range(2)]
            yo = [A.alloc([128, 2, 512], BF16) for _ in range(2)]
            blocks = [(uTl, b * 512, 512, b * 512) for b in range(nblk)]
            if need_ctx:
                blocks.append((uTc, 0, 256, TL))
            for bi, (uT, t0, n, tok0) in enumerate(blocks):
                for cc in range(2):
                    pc = PS[cc][:, 0:n]
                    S.mm(pc, [(DG[:, cc, j, :], uT[:, cc, t0 + j:t0 + j + n]) for j in range(31)])
                    S.act(y[cc][:, 0:n], pc, AF.Identity, bias=pp[:, 64 + cc:65 + cc])
                    S.act(ysq[cc][:, 0:n], pc, AF.Square, bias=pp[:, 64 + cc:65 + cc])
                pm = PS[2][:, 0:n]
                pq = PS[3][:, 0:n]
                S.mm(pm, [(onesf, y[0][:, 0:n]), (onesf, y[1][:, 0:n])])
                S.mm(pq, [(onesf, ysq[0][:, 0:n]), (onesf, ysq[1][:, 0:n])])
                S.act(msb[:, 0:n], pm, AF.Identity, scale=1.0 / 256)
                S.act(m2[:, 0:n], pm, AF.Square, scale=1.0 / 256)
                S.stt(rstd[:, 0:n], pq, 1.0 / 256, m2[:, 0:n], ALU.mult, ALU.subtract)
                S.act(rstd[:, 0:n], rstd[:, 0:n], AF.Sqrt, bias=EPS)
                S.recip(rstd[:, 0:n], rstd[:, 0:n])
                for cc in range(2):
                    e_ = 'vector' if cc == 0 else 'gpsimd'
                    S.tt(t1[cc][:, 0:n], y[cc][:, 0:n], msb[:, 0:n], ALU.subtract, eng=e_)
                    S.tt(t1[cc][:, 0:n], t1[cc][:, 0:n], rstd[:, 0:n], ALU.mult, eng=e_)
                    S.act(sb[cc][:, 0:n], t1[cc][:, 0:n], AF.Silu, bias=pp[:, 68 + cc:69 + cc], scale=pp[:, 66 + cc:67 + cc])
                yob = yo[bi % 2]
                for co in range(2):
                    ppw = PS[4 + co][:, 0:n]
                    S.mm(ppw, [(WPW[:, ci, co * 128:(co + 1) * 128], sb[ci][:, 0:n]) for ci in range(2)])
                    S.act(yob[:, co, 0:n], ppw, AF.Identity, bias=pp[:, 70 + co:71 + co])
                S.dma(YT[0:256, tok0:tok0 + n].rearrange("(c p) t -> p c t", p=128), yob[:, :, 0:n], q='gpsimd')
            S.barrier()
            A.reset(mk)

        def phase_att(l, need_ctx, nqb):
            mk = A.mark()
            QK = A.alloc([128, 4, NT], BF16)
            VA = A.alloc([128, NTILE, 2, 128], BF16)
            QZ = A.alloc([128, 4, NT], BF16)
            VB = A.alloc([128, NTILE, 2, 128], BF16)
            rope = A.alloc([128, 32, 64], F32)
            bc = A.alloc([128, 384], F32)
            S.dma(rope, ropeA_in.rearrange("p (t k) -> p t k", k=64))
            S.dma(bc, W[l]['bc'][:, 512:896])
            S.memset(VA, 0.0)
            S.memset(VA[:, :, :, 64:65], 1.0)
            S.memset(QZ, 0.0, eng='gpsimd')
            S.memset(VB, 0.0, eng='gpsimd')
            S.memset(VB[:, :, :, 0:1], 1.0, eng='gpsimd')
            NB = 2
            raw = [A.alloc([128, NB, 512], BF16) for _ in range(2)]
            f1 = A.alloc([128, NB, 6, 64], F32)
            f2 = A.alloc([128, NB, 6, 64], F32)
            ssq = A.alloc([128, NB, 6], F32)
            qr = [A.alloc([128, NB, 8, 64], BF16) for _ in range(2)]
            tA = A.alloc([128, NB, 6, 32], F32)
            tB = A.alloc([128, NB, 6, 32], F32)
            for bi in range(NTILE // NB):
                tl0 = bi * NB
                rw = raw[bi % 2]
                qb_ = qr[bi % 2]
                S.dma(rw, PT[tl0 * 128:(tl0 + NB) * 128, 784:1296].rearrange("(t p) c -> p t c", p=128))
                qk = rw[:, :, 0:384].rearrange("p t (h e) -> p t h e", e=64)
                S.tt(f1, qk, qk, ALU.mult)
                S.reduce(ssq, f1)
                S.act(ssq, ssq, AF.Sqrt, bias=EPS, scale=1.0 / 64)
                S.recip(ssq, ssq)
                S.tt(f1, qk, ssq.unsqueeze(3).to_broadcast([128, NB, 6, 64]), ALU.mult)
                gq = bc.rearrange("p (h e) -> p h e", e=64).unsqueeze(1).to_broadcast([128, NB, 6, 64])
                S.tt(f2, f1, gq, ALU.mult, eng='gpsimd')
                is_lat = tl0 < 32
                if is_lat:
                    cosv = rope[:, tl0:tl0 + NB, 0:32].unsqueeze(2).to_broadcast([128, NB, 6, 32])
                    sinv = rope[:, tl0:tl0 + NB, 32:64].unsqueeze(2).to_broadcast([128, NB, 6, 32])
                    x1 = f2[:, :, :, 0:32]
                    x2 = f2[:, :, :, 32:64]
                    o1 = f1[:, :, :, 0:32]
                    o2 = f1[:, :, :, 32:64]
                    S.tt(tA, x1, cosv, ALU.mult)
                    S.tt(tB, x2, sinv, ALU.mult, eng='gpsimd')
                    S.tt(o1, tA, tB, ALU.subtract)
                    S.tt(tA, x2, cosv, ALU.mult)
                    S.tt(tB, x1, sinv, ALU.mult, eng='gpsimd')
                    S.tt(o2, tA, tB, ALU.add)
                    src = f1
                else:
                    src = f2
                S.copy(qb_[:, :, 0:4, :], src[:, :, 0:4, :])
                kdst = qb_[:, :, 4:8, :].rearrange("p t (k r) e -> p t k r e", r=2)
                for r_ in range(2):
                    S.copy(kdst[:, :, :, r_, :], src[:, :, 4:6, :], eng='gpsimd')
                for t_ in range(NB):
                    tl = tl0 + t_
                    pt = PSB[6][:, 0:512].rearrange("p (j k) -> p j k", k=128)
                    for j in range(4):
                        S.transpose(pt[:, j, :], qb_[:, t_, 2 * j:2 * j + 2, :].rearrange("p a e -> p (a e)"), identb)
                    S.copy(QK[:, :, tl * 128:(tl + 1) * 128], pt, eng=('vector' if t_ % 2 == 0 else 'scalar'))
                    for j in range(2):
                        S.copy(QZ[0:64, 2 * j, tl * 128:(tl + 1) * 128], pt[0:64, j, :], eng='vector')
                        S.copy(QZ[64:128, 2 * j + 1, tl * 128:(tl + 1) * 128], pt[64:128, j, :], eng='scalar')
                    vv = rw[:, t_, 384:512].rearrange("p (k e) -> p k e", e=64)
                    S.copy(VA[:, tl, :, 0:64], vv, eng='gpsimd')
                    S.copy(VB[:, tl, :, 64:128], vv, eng='gpsimd')
            NS, LA, NP = 4, 3, 5
            Pt = [A.alloc([128, 512], BF16) for _ in range(NP)]
            rsb = A.alloc([128, 512], F32)
            bcs = A.alloc([128, 512], F32)
            Yo = [A.alloc([128, 512], BF16) for _ in range(2)]
            qblocks = [(qb * 512, 512, list(range(NTILE))) for qb in range(nqb)]
            if need_ctx:
                qblocks.append((TL, 256, [32, 33]))
            it = 0
            gi = 0
            cvs = [A.alloc([128, 2048], F32) for _ in range(2)]
            cvb = [A.alloc([128, 2048], BF16) for _ in range(2)]
            pieces = []
            for k in range(8):
                pieces.append((W[l]['w_out'][k * 128:(k + 1) * 128, :], WOb[k * 128:(k + 1) * 128, :], 1024))
            for k in range(8):
                for c_ in range(2):
                    pieces.append((W[l]['w_up'][k * 128:(k + 1) * 128, c_ * 2048:(c_ + 1) * 2048],
                                   WUb[k * 128:(k + 1) * 128, c_ * 2048:(c_ + 1) * 2048], 2048))
            for k in range(32):
                pieces.append((W[l]['w_down'][k * 128:(k + 1) * 128, :], WDb[k * 128:(k + 1) * 128, :], 1024))
            if l + 1 < nlayers:
                for k in range(8):
                    pieces.append((W[l + 1]['w_in'][k * 128:(k + 1) * 128, 0:2048], WIb[k * 128:(k + 1) * 128, 0:2048], 2048))
                    pieces.append((W[l + 1]['w_in'][k * 128:(k + 1) * 128, 2048:PIN], WIb[k * 128:(k + 1) * 128, 2048:PIN], PIN - 2048))
            n_iters = 4 * len(qblocks)
            ppi = (len(pieces) + n_iters - 1) // n_iters
            pci = [0]

            def convert_some():
                for _ in range(ppi):
                    if pci[0] >= len(pieces):
                        return
                    src, dst, w = pieces[pci[0]]
                    sg = cvs[pci[0] % 2]
                    cb = cvb[pci[0] % 2]
                    S.dma(sg[:, 0:w], src, q='sync')
                    S.copy(cb[:, 0:w], sg[:, 0:w], eng='gpsimd')
                    S.dma(dst, cb[:, 0:w], q='gpsimd')
                    pci[0] += 1
            for h in range(4):
                pair, half = h // 2, h % 2
                kvh = pair
                p0 = 64 * half
                for (q0, nq, kts) in qblocks:
                    convert_some()
                    O = PS[4 + it % 2]
                    rhsq = QZ[:, h, q0:q0 + nq]
                    nk = len(kts)
                    sps = {}

                    def smm(i):
                        kt = kts[i]
                        sp = PS[(gi + i) % NS][:, 0:nq]
                        S.mm(sp, [(QK[:, 2 + kvh, kt * 128:(kt + 1) * 128], rhsq)])
                        sps[i] = sp
                    for i in range(min(LA, nk)):
                        smm(i)
                    for i in range(nk):
                        if i + LA < nk:
                            smm(i + LA)
                        sp = sps.pop(i)
                        kt = kts[i]
                        pt_ = Pt[(gi + i) % NP][:, 0:nq]
                        S.act(pt_, sp, AF.Exp, scale=0.125)
                        if KEEPWARM:
                            S.add('tensor', lambda e: e.matmul(PS[7][:, 0:KEEPWARM], identb, QK[:, 0, 0:KEEPWARM],
                                                               start=True, stop=True), reads=[], writes=[])
                        if half == 0:
                            S.mm(O[:, 0:nq], [(VA[:, kt, kvh, :], pt_)], start=(i == 0), stop=(i == nk - 1), acc=(i > 0))
                        else:
                            S.mm(O[:, 0:nq], [(VB[:, kt, kvh, :], pt_)], start=(i == 0), stop=(i == nk - 1), acc=(i > 0))
                    gi += nk
                    yo_ = Yo[it % 2]
                    pb = PS[6]
                    if half == 0:
                        S.recip(rsb[64:65, 0:nq], O[64:65, 0:nq])
                        S.mm(pb[0:64, 0:nq], [(onesf[64:65, 0:64], rsb[64:65, 0:nq])])
                        S.copy(bcs[0:64, 0:nq], pb[0:64, 0:nq], eng='vector')
                        S.tt(yo_[0:64, 0:nq], O[0:64, 0:nq], bcs[0:64, 0:nq], ALU.mult)
                        S.dma(YT[512 + h * 64:512 + (h + 1) * 64, q0:q0 + nq], yo_[0:64, 0:nq], q='gpsimd')
                    else:
                        S.recip(rsb[0:1, 0:nq], O[0:1, 0:nq])
                        S.mm(pb[:, 0:nq], [(onesf[0:1, :], rsb[0:1, 0:nq])])
                        S.copy(bcs[64:128, 0:nq], pb[64:128, 0:nq], eng='vector')
                        S.tt(yo_[64:128, 0:nq], O[64:128, 0:nq], bcs[64:128, 0:nq], ALU.mult)
                        S.dma(YT[512 + h * 64:512 + (h + 1) * 64, q0:q0 + nq], yo_[64:128, 0:nq], q='gpsimd')
                    it += 1
            S.barrier()
            A.reset(mk)

        def phase_scan(l, kind, zT, need_ctx, nlt):
            mk = A.mark()
            is_gla = (kind == 'gla')
            pc0 = 0 if is_gla else 1296
            yrow = 256 if is_gla else 768
            qk_all = A.alloc([128, NTILE, 256], BF16)
            vg_all = A.alloc([128, NTILE, 512], BF16)
            QKT = A.alloc([128, 2, NT], BF16)
            oacc = A.alloc([128, NTILE, 256], F32)
            PTv = PT.rearrange("(t p) c -> p t c", p=128)
            for i_ in range(17):
                sl = slice(2 * i_, 2 * i_ + 2)
                S.dma(qk_all[:, sl, :], PTv[:, sl, pc0:pc0 + 256], q=('sync' if i_ % 2 == 0 else 'gpsimd'))
                S.dma(vg_all[:, sl, :], PTv[:, sl, pc0 + 256:pc0 + 768], q=('gpsimd' if i_ % 2 == 0 else 'sync'))
            gn = A.alloc([128, 256], F32)
            S.dma(gn, W[l]['bc'][:, (0 if is_gla else 256):(256 if is_gla else 512)])
            mk2 = A.mark()
            if not is_gla:
                rope = A.alloc([128, 32, 32], F32)
                S.dma(rope, ropeR_in.rearrange("p (t k) -> p t k", k=32))
                NB = 4
                tA = A.alloc([128, NB, 8, 16], F32)
                tB = A.alloc([128, NB, 8, 16], F32)
                tC = A.alloc([128, NB, 8, 16], F32)
                tD = A.alloc([128, NB, 8, 16], F32)
                for bi in range(32 // NB):
                    tl0 = bi * NB
                    v4 = qk_all[:, tl0:tl0 + NB, :].rearrange("p t (h e) -> p t h e", e=32)
                    x1 = v4[:, :, :, 0:16]
                    x2 = v4[:, :, :, 16:32]
                    cosv = rope[:, tl0:tl0 + NB, 0:16].unsqueeze(2).to_broadcast([128, NB, 8, 16])
                    sinv = rope[:, tl0:tl0 + NB, 16:32].unsqueeze(2).to_broadcast([128, NB, 8, 16])
                    S.tt(tA, x1, cosv, ALU.mult)
                    S.tt(tB, x2, sinv, ALU.mult, eng='gpsimd')
                    S.tt(tC, x2, cosv, ALU.mult)
                    S.tt(tD, x1, sinv, ALU.mult, eng='gpsimd')
                    S.tt(x1, tA, tB, ALU.subtract)
                    S.tt(x2, tC, tD, ALU.add, eng='gpsimd')
            for tl in range(NTILE):
                pt = PSB[6][:, 0:256].rearrange("p (j k) -> p j k", k=128)
                for j in range(2):
                    S.transpose(pt[:, j, :], qk_all[:, tl, j * 128:(j + 1) * 128], identb)
                S.copy(QKT[:, :, tl * 128:(tl + 1) * 128], pt, eng=('vector' if tl % 2 == 0 else 'scalar'))
            S.barrier()
            A.reset(mk2)
            if stop_after == kind + '_prep':
                raise _Stop()
            Sf = A.alloc([128, 256], F32)
            Sb = A.alloc([128, 256], BF16)
            kiT = [A.alloc([128, 4, 128], BF16) for _ in range(2)]
            kvm = A.alloc([128, 256], F32)
            if is_gla:
                WA = A.alloc([32, 256], BF16)
                was = A.alloc([32, 256], F32)
                S.dma(was, W[l]['waug'])
                S.copy(WA, was)
                ee = [A.alloc([128, 128], F32) for _ in range(2)]
                ll = [A.alloc([128, 128], F32) for _ in range(2)]
                E1s = [A.alloc([128, 128], F32) for _ in range(2)]
                E2s = [A.alloc([128, 128], F32) for _ in range(2)]
                E3s = [A.alloc([128, 128], F32) for _ in range(2)]
            LNS = math.log(32 ** -0.5)
            NQ = 6
            qdT = [A.alloc([128, 128], BF16) for _ in range(NQ)]
            kend = [A.alloc([128, 128], BF16) for _ in range(NQ)]
            scm = [A.alloc([128, 4, 128], BF16) for _ in range(3)]
            if is_gla:
                dcs = [A.alloc([128, 1], F32) for _ in range(NQ)]
            for d_ in range(2):
                S.memset(Sf, 0.0)
                S.memset(Sb, 0.0)
                order = [32, 33] + list(range(32)) if d_ == 0 else [33, 32] + list(range(31, -1, -1))
                if stop_after and stop_after.startswith(kind + '_n'):
                    order = order[:int(stop_after[len(kind) + 2:])]
                n_st = len(order)
                want = [need_ctx or ch < nlt for ch in order]
                tks = [slice(ch * 128, (ch + 1) * 128) for ch in order]
                if not is_gla:
                    o_ = 1156 + d_ * 384
                    cE1 = cst[:, o_:o_ + 128]
                    cE2 = cst[:, o_ + 128:o_ + 256]
                    cE3 = cst[:, o_ + 256:o_ + 384]
                    cdc = cst[:, 1924:1925]

                def stA(j):
                    if is_gla:
                        S.mm(PS[0][:, 0:128], [(zT[0:32, tks[j]], WA[0:32, d_ * 128:(d_ + 1) * 128])])

                def stB(j):
                    if is_gla:
                        S.act(ee[j % 2], PS[0][:, 0:128], AF.Exp, scale=-1.0)
                        S.act(ll[j % 2], ee[j % 2], AF.Ln, bias=1.0)

                def stC(j):
                    if is_gla:
                        S.mm(PS[1][:, 0:128], [(ll[j % 2], TRI[d_])])
                        S.mm(PS[1][:, 128:256], [(TRIR[d_], ll[j % 2])])

                def stD(j):
                    if is_gla:
                        pbT = PS[1][:, 0:128]
                        S.act(E1s[j % 2], pbT, AF.Exp, bias=LNS)
                        S.act(E2s[j % 2], pbT, AF.Exp, scale=-1.0)
                        S.act(E3s[j % 2], PS[1][:, 128:256], AF.Exp)
                        col = 127 if d_ == 0 else 0
                        S.act(dcs[j % NQ], pbT[:, col:col + 1], AF.Exp)

                def stE(j):
                    E1, E2, E3 = (E1s[j % 2], E2s[j % 2], E3s[j % 2]) if is_gla else (cE1, cE2, cE3)
                    S.tt(qdT[j % NQ], QKT[:, 0, tks[j]], E1, ALU.mult, eng='gpsimd')
                    if want[j]:
                        for hh in range(4):
                            S.stt(kiT[j % 2][:, hh, :], QKT[:, 1, tks[j]], HM[:, hh:hh + 1], E2, ALU.mult, ALU.mult)
                    S.tt(kend[j % NQ], qk_all[:, order[j], 128:256], E3, ALU.mult, eng='gpsimd')

                def stF(j):
                    if want[j]:
                        psc = PS[2 + j % 2][:, :].rearrange("p (h c) -> p h c", c=128)
                        for hh in range(4):
                            S.mm(psc[:, hh, :], [(kiT[j % 2][:, hh, :], qdT[j % NQ])])

                def stG(j):
                    if want[j]:
                        psc = PS[2 + j % 2][:, :].rearrange("p (h c) -> p h c", c=128)
                        S.tt(scm[j % 3], psc, bcast_mid(maskb16[:, d_, :], 4), ALU.mult)

                def stH(j):
                    ch = order[j]
                    qd = qdT[j % NQ]
                    ke = kend[j % NQ]
                    sm = scm[j % 3]
                    dc = dcs[j % NQ] if is_gla else cdc
                    vch = vg_all[:, ch, 0:256]
                    if want[j]:
                        po = PS[4 + j % 2][:, 0:256]
                        S.mm(po, [(qd, Sb)], start=True, stop=False)
                        for hh in range(4):
                            S.mm(po[:, hh * 64:(hh + 1) * 64], [(sm[:, hh, :], vch[:, hh * 64:(hh + 1) * 64])],
                                 start=False, stop=True, acc=True)
                        if d_ == 0:
                            S.copy(oacc[:, ch, :], po, eng='scalar')
                        else:
                            S.tt(oacc[:, ch, :], oacc[:, ch, :], po, ALU.add)
                    pkv = PS[6 + j % 2][:, 0:256]
                    S.mm(pkv, [(ke, vch)])
                    S.tt(kvm, pkv, BDm, ALU.mult)
                    S.stt(Sb, Sf, dc, kvm, ALU.mult, ALU.add)
                    S.stt(Sf, Sf, dc, kvm, ALU.mult, ALU.add)

                stages = [(stH, 0), (stG, 1), (stF, 2), (stE, 3), (stD, 4), (stC, 5), (stB, 6), (stA, 7)]
                for t in range(-7, n_st):
                    for fn_, lead in stages:
                        j = t + lead
                        if 0 <= j < n_st:
                            fn_(j)
            if stop_after and stop_after.startswith(kind + '_n'):
                raise _Stop()
            NB = 2
            f1 = A.alloc([128, NB, 4, 64], F32)
            ssq = A.alloc([128, NB, 4], F32)
            sg = A.alloc([128, NB, 256], F32)
            yb = [A.alloc([128, NB, 256], BF16) for _ in range(2)]
            yts = [A.alloc([128, 2, NB * 128], BF16) for _ in range(2)]
            ntl = NTILE if need_ctx else nlt
            for bi in range(ntl // NB):
                tl0 = bi * NB
                o4 = oacc[:, tl0:tl0 + NB, :].rearrange("p t (h e) -> p t h e", e=64)
                S.tt(f1, o4, o4, ALU.mult)
                S.reduce(ssq, f1)
                S.act(ssq, ssq, AF.Sqrt, bias=EPS, scale=1.0 / 64)
                S.recip(ssq, ssq)
                S.tt(f1, o4, ssq.unsqueeze(3).to_broadcast([128, NB, 4, 64]), ALU.mult)
                f1f = f1.rearrange("p t h e -> p t (h e)")
                S.tt(f1f, f1f, bcast_mid(gn, NB), ALU.mult, eng='gpsimd')
                S.act(sg, vg_all[:, tl0:tl0 + NB, 256:512], AF.Silu)
                y_ = yb[bi % 2]
                S.tt(y_, f1f, sg, ALU.mult)
                yt_ = yts[bi % 2]
                pt = PSB[6][:, 0:2 * NB * 128].rearrange("p (j k) -> p j k", j=2)
                for t_ in range(NB):
                    for j in range(2):
                        S.transpose(pt[:, j, t_ * 128:(t_ + 1) * 128], y_[:, t_, j * 128:(j + 1) * 128], identb)
                S.copy(yt_, pt, eng='scalar')
                S.dma(YT[yrow:yrow + 256, tl0 * 128:(tl0 + NB) * 128].rearrange("(c p) t -> p c t", p=128), yt_, q='gpsimd')
            S.barrier()
            A.reset(mk)

        def phase_C(l, Xsrc, last, need_ctx, nlg):
            mk = A.mark()
            WOUT = A.alloc([128, 8, D], BF16)
            WUP = A.alloc([128, 8, 4 * D], BF16)
            WDN = A.alloc([128, 32, D], BF16)
            hid = A.alloc([128, 32, 256], BF16)
            S.dma(WOUT, WOb.rearrange("(k p) n -> p k n", p=128))
            for k2 in range(4):
                S.dma(WUP[:, 2 * k2:2 * k2 + 2, :], WUb[k2 * 256:(k2 + 1) * 256, :].rearrange("(k p) n -> p k n", p=128),
                      q=('gpsimd' if k2 % 2 == 0 else 'sync'))
            for k8 in range(4):
                S.dma(WDN[:, 8 * k8:8 * k8 + 8, :], WDb[k8 * 1024:(k8 + 1) * 1024, :].rearrange("(k p) n -> p k n", p=128),
                      q=('sync' if k8 % 2 == 0 else 'gpsimd'))
            xgs = [A.alloc([128, 8, 256], F32) for _ in range(2)]
            Yg = A.alloc([128, 8, 256], BF16)
            sq = A.alloc([128, 8, 256], BF16)
            tmp = [A.alloc([128, 256], F32) for _ in range(3)]
            h2T = A.alloc([128, 8, 256], BF16)
            rr = [tmp[1], tmp[2]] + [A.alloc([128, 256], F32) for _ in range(2)]
            Xv = Xsrc.rearrange("(c p) t -> p c t", p=128)
            XSv = XST.rearrange("(c p) t -> p c t", p=128)
            OUv = outT.rearrange("(c p) t -> p c t", p=128)
            YTv = YT.rearrange("(c p) t -> p c t", p=128)
            ngroups = 17 if need_ctx else nlg
            n = 256
            S.dma(xgs[0], Xv[:, :, 0:n])

            def down_part(gp, dc):
                xp = xgs[gp % 2]
                mp = 0 if gp < 16 else 1
                pd = PS[1 + dc % 2][:, 0:n]
                S.mm(pd, [(WDN[:, fc, dc * 128:(dc + 1) * 128], hid[:, fc, :]) for fc in range(32)])
                S.stt(xp[:, dc, :], pd, MOD[:, l, 5, :, mp][:, dc:dc + 1], xp[:, dc, :], ALU.mult, ALU.add)

            def finish_group(gp):
                xp = xgs[gp % 2]
                tp = gp * 256
                if not last:
                    S.dma(XSv[:, :, tp:tp + n], xp, q='gpsimd')
                    if dbg:
                        S.dma(dbg_out['X1'].rearrange("(c p) t -> p c t", p=128)[:, :, tp:tp + n], xp, q='gpsimd')
                else:
                    S.act(sq, xp, AF.Square)
                    ps = PS[0][:, 0:n]
                    S.mm(ps, [(onesb, sq[:, c, :]) for c in range(8)])
                    rstd = tmp[0]
                    S.act(rstd, ps, AF.Sqrt, bias=EPS)
                    S.recip(rstd, rstd)
                    for c in range(8):
                        S.stt(xp[:, c, :], xp[:, c, :], pp[:, 134 + c:135 + c], rstd, ALU.mult, ALU.mult)
                    S.dma(OUv[:, :, tp:tp + n], xp, q='gpsimd')

            for g in range(ngroups + 1):
                cur = g if g < ngroups else None
                prev = g - 1 if g > 0 else None
                if cur is not None:
                    t0 = g * 256
                    m = 0 if g < 16 else 1
                    xg = xgs[g % 2]
                    S.dma(Yg, YTv[:, :, t0:t0 + n], q='sync')
                    G1 = MOD[:, l, 2, :, m]
                    for dc in range(8):
                        po = PS[1 + dc % 2][:, 0:n]
                        S.mm(po, [(WOUT[:, kc, dc * 128:(dc + 1) * 128], Yg[:, kc, :]) for kc in range(8)])
                        S.stt(xg[:, dc, :], po, G1[:, dc:dc + 1], xg[:, dc, :], ALU.mult, ALU.add)
                    S.act(sq, xg, AF.Square)
                if prev is not None:
                    down_part(prev, 0)
                if cur is not None:
                    pss = PS[0][:, 0:n]
                    S.mm(pss, [(onesb, sq[:, c, :]) for c in range(8)])
                if prev is not None:
                    down_part(prev, 1)
                if cur is not None:
                    rstd = tmp[0]
                    S.act(rstd, pss, AF.Sqrt, bias=EPS)
                    S.recip(rstd, rstd)
                    Acol = MOD[:, l, 3, :, m]
                    Bcol = MOD[:, l, 4, :, m]
                for c in range(8):
                    if cur is not None:
                        t_ = tmp[1 + c % 2]
                        S.tt(t_, xg[:, c, :], rstd, ALU.mult, eng=('vector' if c % 2 == 0 else 'gpsimd'))
                        S.act(h2T[:, c, :], t_, AF.Identity, bias=Bcol[:, c:c + 1], scale=Acol[:, c:c + 1])
                    if prev is not None and c < 6:
                        down_part(prev, 2 + c)
                if prev is not None:
                    finish_group(prev)
                if cur is not None:
                    if g + 1 < ngroups:
                        S.dma(xgs[(g + 1) % 2], Xv[:, :, t0 + n:t0 + 2 * n])
                    for fc in range(32):
                        pu = PS[3 + fc % 4][:, 0:n]
                        S.mm(pu, [(WUP[:, kc, fc * 128:(fc + 1) * 128], h2T[:, kc, :]) for kc in range(8)])
                        r_ = rr[fc % 4]
                        S.act(r_, pu, AF.Relu)
                        S.tt(hid[:, fc, :], r_, r_, ALU.mult, eng=('vector' if fc % 2 == 0 else 'gpsimd'))
            S.barrier()
            A.reset(mk)

        def chk(name):
            if stop_after == name:
                raise _Stop()
        try:
          phase_mod()
          chk('mod')
          for l in range(nlayers):
              need_ctx = l < DEPTH - 1
              last = (l == DEPTH - 1)
              Xsrc = xT_in if l == 0 else XST
              S.dma(pp, W[l]['pp'])
              mk = A.mark()
              zT = A.alloc([32, NT], BF16)
              uTl = A.alloc([128, 2, TL + 30], BF16)
              uTc = A.alloc([128, 2, TC + 30], BF16)
              S.memset(zT, 1.0)
              S.memset(uTl, 0.0, eng='gpsimd')
              S.memset(uTc, 0.0, eng='gpsimd')
              phase_A(l, Xsrc, zT, uTl, uTc)
              if dbg and l == 0:
                  for t_ in range(NTILE):
                      S.dma(dbg_out['PT'][t_ * 128:(t_ + 1) * 128, :], PT[t_ * 128:(t_ + 1) * 128, :], q='gpsimd')
              chk('A')
              hf = 2 if last else 1
              phase_conv(l, uTl, uTc, need_ctx, 8 // hf)
              chk('conv')
              phase_att(l, need_ctx, 8 // hf)
              chk('att')
              phase_scan(l, 'gla', zT, need_ctx, 32 // hf)
              chk('gla')
              phase_scan(l, 'ret', zT, need_ctx, 32 // hf)
              chk('ret')
              if dbg and l == 0:
                  for c_ in range(8):
                      S.dma(dbg_out['YT'][c_ * 128:(c_ + 1) * 128, :], YT[c_ * 128:(c_ + 1) * 128, :], q='gpsimd')
              S.barrier()
              A.reset(mk)
              phase_C(l, Xsrc, last, need_ctx, 16 // hf)
        except _Stop:
            if dbg:
                for c_ in range(8):
                    S.dma(dbg_out['YT'][c_ * 128:(c_ + 1) * 128, :], YT[c_ * 128:(c_ + 1) * 128, :], q='gpsimd')
        S.finish()
        S.emit()
        print("ops", S.n_ops, "waits", S.n_waits, {e: len(S.items[e]) for e in ENGS})
    return nc


def make_consts(rev=False):
    cst = np.zeros((128, NCST), np.float64)
    cst[:, 0:128] = np.eye(128)
    s = np.arange(128)[:, None]
    c = np.arange(128)[None, :]
    cst[:, 128:256] = np.where(s <= c, -1.0 / 16, 0.0)
    cst[:, 256:384] = np.where(s >= c, -1.0 / 16, 0.0)
    cst[:, 384:512] = np.where(s > c, -1.0 / 16, 0.0)
    cst[:, 512:640] = np.where(s < c, -1.0 / 16, 0.0)
    cst[:, 640:768] = np.where(s <= c, 1.0, 0.0)
    cst[:, 768:896] = np.where(s >= c, 1.0, 0.0)
    f = np.arange(128)[:, None] // 32
    he = np.arange(256)[None, :] // 64
    cst[:, 896:1152] = (f == he).astype(np.float64)
    for h in range(4):
        cst[:, 1152 + h] = (np.arange(128) // 32 == h)
    lg = np.log1p(-np.exp2(-5.0 - np.arange(4)))
    lgf = lg[np.arange(128) // 32]
    sc = 32 ** -0.5
    cc = np.arange(128)
    cst[:, 1156:1284] = np.exp(lgf[:, None] * (cc[None, :] + 1))
    cst[:, 1284:1412] = np.exp(-lgf[:, None] * (cc[None, :] + 1)) * sc
    cst[:, 1412:1540] = np.exp(lgf[None, :] * (127 - cc[:, None])) * sc
    cst[:, 1540:1668] = np.exp(lgf[:, None] * (128 - cc[None, :]))
    cst[:, 1668:1796] = np.exp(-lgf[:, None] * (128 - cc[None, :])) * sc
    cst[:, 1796:1924] = np.exp(lgf[None, :] * cc[:, None]) * sc
    cst[:, 1924] = np.exp(lgf * 128)
    t = np.arange(TL)
    if rev:
        t = t[::-1]
    row = (t // 64).astype(np.float64)
    col = (t % 64).astype(np.float64)

    def tables(hd):
        n_ax = hd // 4
        inv = 10000.0 ** (-np.arange(n_ax, dtype=np.float64) / n_ax)
        inv = inv.astype(np.float32).astype(np.float64)
        ang = np.concatenate([row[:, None] * inv, col[:, None] * inv], -1).astype(np.float32)
        return np.cos(ang), np.sin(ang)
    ca, sa = tables(64)
    cr, sr = tables(32)
    ropeA = np.concatenate([ca, sa], -1).reshape(32, 128, 64).transpose(1, 0, 2).reshape(128, 32 * 64)
    ropeR = np.concatenate([cr, sr], -1).reshape(32, 128, 32).transpose(1, 0, 2).reshape(128, 32 * 32)
    return (cst.astype(np.float32), np.ascontiguousarray(ropeA, np.float32), np.ascontiguousarray(ropeR, np.float32))


def fm(v):
    v = np.asarray(v, np.float32)
    return v.reshape(-1, 128).T


def make_in_maps(inp):
    f32 = lambda a: np.ascontiguousarray(np.asarray(a, np.float32))
    consts = [make_consts(False), make_consts(True)]
    per_layer = [[], []]
    for rev in (0, 1):
        for l in range(DEPTH):
            pp = np.zeros((128, NPP), np.float32)
            pp[:, 0:8] = fm(inp['norm1_g'][l])
            pp[:, 8:16] = fm(inp['norm2_g'][l])
            pp[:, 16:64] = fm(inp['b_mod'][l])
            pp[:, 64:66] = fm(inp['conv_b_dw'][l])
            pp[:, 66:68] = fm(inp['conv_ln_g'][l])
            pp[:, 68:70] = fm(inp['conv_ln_b'][l])
            pp[:, 70:72] = fm(inp['conv_b_pw'][l])
            wdw = np.asarray(inp['conv_w_dw'][l], np.float32)
            if rev:
                wdw = wdw[::-1]
            for cc in range(2):
                pp[:, 72 + cc * 31:72 + (cc + 1) * 31] = wdw[:, cc * 128:(cc + 1) * 128].T
            pp[:, 134:142] = fm(inp['final_norm_g'])
            bc = np.zeros((128, NBC), np.float32)
            bc[:, 0:256] = np.asarray(inp['gla_norm_g'][l], np.float32).reshape(1, 256)
            bc[:, 256:512] = np.asarray(inp['ret_norm_g'][l], np.float32).reshape(1, 256)
            qg = np.asarray(inp['att_q_norm_g'][l], np.float32)
            kg = np.asarray(inp['att_k_norm_g'][l], np.float32)
            bc[:, 512:896] = np.concatenate([qg, qg, qg, qg, kg, kg])[None, :]
            waug = np.zeros((32, 256), np.float32)
            fo, bo = (128, 0) if rev else (0, 128)
            waug[0:16, fo:fo + 128] = inp['gla_w_a_f'][l]
            waug[0:16, bo:bo + 128] = inp['gla_w_a_b'][l]
            waug[16, fo:fo + 128] = inp['gla_b_a_f'][l]
            waug[16, bo:bo + 128] = inp['gla_b_a_b'][l]
            per_layer[rev].append({
                "w_mod%d" % l: f32(inp['w_mod'][l]), "w_in%d" % l: f32(inp['w_in'][l]),
                "w_out%d" % l: f32(inp['w_out'][l]), "w_up%d" % l: f32(inp['w_up'][l]),
                "w_down%d" % l: f32(inp['w_down'][l]), "pp%d" % l: pp, "bc%d" % l: bc,
                "waug%d" % l: waug, "wpw%d" % l: f32(inp['conv_w_pw'][l])})
    x = np.asarray(inp['x'], np.float32)
    ctx = np.asarray(inp['ctx'], np.float32)
    c = np.asarray(inp['c'], np.float32)
    cctx = np.asarray(inp['c_ctx'], np.float32)
    maps = []
    for core in range(8):
        b = core % 4
        rev = core // 4
        if rev:
            xT = np.ascontiguousarray(np.concatenate([x[b][::-1], ctx[b][::-1]], 0).T)
        else:
            xT = np.ascontiguousarray(np.concatenate([x[b], ctx[b]], 0).T)
        cvec = np.ascontiguousarray(np.stack([fm(c[b]), fm(cctx)], -1).reshape(128, 16))
        cst, ropeA, ropeR = consts[rev]
        m = {"xT": xT, "cvec": cvec, "cst": cst, "ropeA": ropeA, "ropeR": ropeR}
        for l in range(DEPTH):
            m.update(per_layer[rev][l])
        maps.append(m)
    return maps


_NC_CACHE = {}


def kernel(**inputs):
    maps = make_in_maps(inputs)
    if 'nc' not in _NC_CACHE:
        _NC_CACHE['nc'] = build_program()
    nc = _NC_CACHE['nc']
    res = run_bass_kernel_spmd(nc, maps, core_ids=list(range(8)))
    out = np.empty((4, TL, D), np.float32)
    h = TL // 2
    for b in range(4):
        out[b, 0:h] = res.results[b]["outT"].T
        out[b, h:TL] = res.results[b + 4]["outT"].T[::-1]
    return out
```

```python
import math
from contextlib import ExitStack
import numpy as np
import concourse.bass as bass
import concourse.mybir as mybir
from concourse.bass_utils import run_bass_kernel_spmd

F32 = mybir.dt.float32
BF16 = mybir.dt.bfloat16
U8 = mybir.dt.uint8
AF = mybir.ActivationFunctionType
ALU = mybir.AluOpType
AX = mybir.AxisListType

D = 1024
TL = 4096
TC = 256
NT = TL + TC
NTILE = NT // 128
PIN = 2576
PTW = 2064
EPS = 1e-6
NPP = 142
NBC = 896
NCST = 1925
DEPTH = 2

ENGS = ['tensor', 'vector', 'scalar', 'gpsimd', 'sync']
DMA_POOL = 24
KEEPWARM = 0
DSIZE = {F32: 4, BF16: 2, U8: 1}


class Sched:
    def __init__(self, nc, stack):
        self.nc = nc
        self.items = {e: [] for e in ENGS}
        self.seq = {e: 0 for e in ENGS}
        self.esem = {e: stack.enter_context(nc.semaphore("sq_" + e)) for e in ENGS if e != 'sync'}
        self.dpool = {}
        self.dcount = {}
        for q in ['sync', 'gpsimd']:
            self.dpool[q] = [stack.enter_context(nc.semaphore("dq_%s_%d" % (q, i))) for i in range(DMA_POOL)]
            self.dcount[q] = 0
        self.waited = {}
        self.recs = {}
        self.dram_rowlen = {}
        self.arena_name = None
        self.arena_allocs = []
        self.n_ops = 0
        self.n_waits = 0

    def box(self, ap):
        t = ap.tensor
        name = t.name
        off = int(ap.offset)
        pairs = [(int(s), int(c)) for s, c in ap.ap]
        mx = off + sum(s * (c - 1) for s, c in pairs if c > 0)
        if name in self.dram_rowlen:
            rowlen = self.dram_rowlen[name]
            es = 1
        else:
            rowlen = pairs[0][0]
            es = DSIZE[ap.dtype]
        r0 = off // rowlen
        r1 = mx // rowlen
        c0 = off % rowlen
        c1 = c0 + sum(s * (c - 1) for s, c in pairs if s < rowlen and c > 0)
        if c1 >= rowlen:
            c0, c1 = 0, rowlen - 1
        c0 *= es
        c1 = c1 * es + es - 1
        if name == self.arena_name:
            for (a, b, i) in self.arena_allocs:
                if a <= c0 < b:
                    assert c1 < b, "arena view crosses allocation"
                    name = (name, i)
                    break
            else:
                raise AssertionError("arena box not found")
        return name, (r0, r1, c0, c1)

    def _need(self, eng, ev, waits):
        sem, val = ev
        k = (eng, id(sem))
        if self.waited.get(k, 0) >= val:
            return
        self.waited[k] = val
        waits[id(sem)] = (sem, val)

    def add(self, eng, fn, reads=(), writes=(), dma=False, acc=False):
        waits = {}
        rb = [self.box(a) for a in reads]
        wb = [self.box(a) for a in writes]
        for name, b in rb:
            ps = isinstance(name, str) and name.startswith("ps")
            for r in self.recs.get(name, ()):
                if ps and r[5] != eng:
                    self._need(eng, r[6], waits)
                elif r[4] and not (r[1] < b[0] or b[1] < r[0] or r[3] < b[2] or b[3] < r[2]):
                    self._need(eng, r[6], waits)
        for name, b in wb:
            ps = isinstance(name, str) and name.startswith("ps")
            for r in self.recs.get(name, ()):
                if ps and r[5] != eng:
                    self._need(eng, r[6], waits)
                elif ps and eng == 'tensor':
                    continue
                elif not (r[1] < b[0] or b[1] < r[0] or r[3] < b[2] or b[3] < r[2]):
                    if acc and r[4] and r[5] == 'tensor':
                        continue
                    self._need(eng, r[6], waits)
        if dma:
            q = eng
            i = self.dcount[q]
            self.dcount[q] += 1
            sem = self.dpool[q][i % DMA_POOL]
            val = 16 * (i // DMA_POOL + 1)
            if i >= DMA_POOL:
                self._need(eng, (sem, val - 16), waits)
            ev = (sem, val)
            inc = (sem, 16)
        else:
            self.seq[eng] += 1
            ev = (self.esem[eng], self.seq[eng])
            inc = (self.esem[eng], 1)
        for name, b in wb:
            lst = self.recs.setdefault(name, [])
            lst[:] = [r for r in lst if not (b[0] <= r[0] and r[1] <= b[1] and b[2] <= r[2] and r[3] <= b[3])]
            lst.append([b[0], b[1], b[2], b[3], True, eng, ev])
        for name, b in rb:
            lst = self.recs.setdefault(name, [])
            if not dma:
                lst[:] = [r for r in lst if not ((not r[4]) and r[5] == eng and r[6][0] is ev[0]
                                                 and b[0] <= r[0] and r[1] <= b[1] and b[2] <= r[2] and r[3] <= b[3])]
            lst.append([b[0], b[1], b[2], b[3], False, eng, ev])
        self.items[eng].append((list(waits.values()), fn, inc))
        self.n_ops += 1
        self.n_waits += len(waits)
        return ev

    def barrier(self):
        for eng in ENGS:
            waits = {}
            for q in self.dpool:
                n = self.dcount[q]
                for j in range(min(n, DMA_POOL)):
                    uses = (n - 1 - j) // DMA_POOL + 1
                    self._need(eng, (self.dpool[q][j], 16 * uses), waits)
            for e in self.esem:
                if self.seq[e] > 0:
                    self._need(eng, (self.esem[e], self.seq[e]), waits)
            if waits:
                self.items[eng].append((list(waits.values()), None, None))
        self.recs = {}

    def dma(self, out, in_, q='sync'):
        return self.add(q, lambda e: e.dma_start(out=out, in_=in_), reads=[in_], writes=[out], dma=True)

    def mm(self, out, pairs, start=True, stop=True, acc=False):
        n = len(pairs)

        def fn(e):
            ins = None
            for i, p in enumerate(pairs):
                ins = e.matmul(out, p[0], p[1], start=(start and i == 0), stop=(stop and i == n - 1))
            return ins
        rd = []
        for p in pairs:
            rd += [p[0], p[1]]
        return self.add('tensor', fn, reads=rd, writes=[out], acc=acc)

    def transpose(self, out, in_, ident):
        return self.add('tensor', lambda e: e.transpose(out, in_, ident), reads=[in_, ident], writes=[out])

    def act(self, out, in_, func, bias=None, scale=None, accum_out=None):
        kw = {}
        rd = [in_]
        wr = [out]
        if bias is not None:
            kw['bias'] = bias
            if not isinstance(bias, (int, float)):
                rd.append(bias)
        if scale is not None:
            kw['scale'] = scale
            if not isinstance(scale, (int, float)):
                rd.append(scale)
        if accum_out is not None:
            kw['accum_out'] = accum_out
            wr.append(accum_out)
        return self.add('scalar', lambda e: e.activation(out, in_, func, **kw), reads=rd, writes=wr)

    def tt(self, out, in0, in1, op, eng='vector'):
        return self.add(eng, lambda e: e.tensor_tensor(out, in0, in1, op), reads=[in0, in1], writes=[out])

    def ts(self, out, in0, s1, s2, op0, op1=None, eng='vector'):
        rd = [in0]
        if not isinstance(s1, (int, float)):
            rd.append(s1)
        if s2 is not None and not isinstance(s2, (int, float)):
            rd.append(s2)
        if op1 is None:
            return self.add(eng, lambda e: e.tensor_scalar(out, in0, s1, None, op0), reads=rd, writes=[out])
        return self.add(eng, lambda e: e.tensor_scalar(out, in0, s1, s2, op0, op1), reads=rd, writes=[out])

    def stt(self, out, in0, scalar, in1, op0, op1, eng='vector'):
        eng = 'vector'
        rd = [in0, in1]
        if not isinstance(scalar, (int, float)):
            rd.append(scalar)
        return self.add(eng, lambda e: e.scalar_tensor_tensor(out, in0, scalar, in1, op0, op1), reads=rd, writes=[out])

    def copy(self, out, in_, eng='vector'):
        if eng == 'scalar':
            return self.add('scalar', lambda e: e.copy(out, in_), reads=[in_], writes=[out])
        return self.add(eng, lambda e: e.tensor_copy(out, in_), reads=[in_], writes=[out])

    def memset(self, ap, val, eng='vector'):
        return self.add(eng, lambda e: e.memset(ap, val), reads=[], writes=[ap])

    def recip(self, out, in_):
        return self.add('vector', lambda e: e.reciprocal(out, in_), reads=[in_], writes=[out])

    def reduce(self, out, in_, op=ALU.add, eng='vector'):
        return self.add(eng, lambda e: e.tensor_reduce(out, in_, AX.X, op), reads=[in_], writes=[out])

    def finish(self):
        self.barrier()

    def emit(self):
        nc = self.nc
        items = self.items

        def run(e, lst):
            for waits, fn, inc in lst:
                for sem, v in waits:
                    e.wait_ge(sem, v)
                if fn is None:
                    continue
                ins = fn(e)
                ins.then_inc(inc[0], inc[1])

        with nc.Block() as block:
            @block.tensor
            def _(e):
                run(e, items['tensor'])

            @block.vector
            def _(e):
                run(e, items['vector'])

            @block.scalar
            def _(e):
                run(e, items['scalar'])

            @block.gpsimd
            def _(e):
                run(e, items['gpsimd'])

            @block.sync
            def _(e):
                run(e, items['sync'])


class Arena:
    def __init__(self, S, t, nbytes):
        self.S = S
        self.t = t
        self.n = nbytes
        self.off = 0
        self.uid = 0
        S.arena_name = t.name

    def alloc(self, shape, dtype):
        es = DSIZE[dtype]
        free = 1
        for s in shape[1:]:
            free *= s
        nb = (free * es + 63) // 64 * 64
        assert self.off + nb <= self.n, "arena overflow: need %d have %d" % (self.off + nb, self.n)
        a = self.off
        self.off += nb
        self.uid += 1
        self.S.arena_allocs.append((a, a + nb, self.uid))
        ap = self.t[0:shape[0], a:a + free * es].bitcast(dtype)
        if len(shape) > 2:
            names = ["d%d" % i for i in range(len(shape) - 1)]
            kw = {names[i]: shape[i + 1] for i in range(len(shape) - 2)}
            ap = ap.rearrange("p (%s) -> p %s" % (" ".join(names), " ".join(names)), **kw)
        return ap

    def mark(self):
        return (self.off, len(self.S.arena_allocs))

    def reset(self, mark):
        self.off = mark[0]
        del self.S.arena_allocs[mark[1]:]


def bcast_mid(ap, n):
    return ap.unsqueeze(1).to_broadcast([ap.shape[0], n, ap.shape[1]])


def build_program(dbg=False, nlayers=DEPTH, stop_after=None):
    nc = bass.Bass("TRN2", target_bir_lowering=False)
    dt_in = lambda name, shape, dt=F32: nc.dram_tensor(name, list(shape), dt, kind="ExternalInput").ap()
    xT_in = dt_in("xT", [D, NT])
    cvec_in = dt_in("cvec", [128, 16])
    cst_in = dt_in("cst", [128, NCST])
    ropeA_in = dt_in("ropeA", [128, 32 * 64])
    ropeR_in = dt_in("ropeR", [128, 32 * 32])
    W = []
    for l in range(DEPTH):
        W.append(dict(
            w_mod=dt_in("w_mod%d" % l, [D, 6 * D]), w_in=dt_in("w_in%d" % l, [D, PIN]),
            w_out=dt_in("w_out%d" % l, [D, D]), w_up=dt_in("w_up%d" % l, [D, 4 * D]),
            w_down=dt_in("w_down%d" % l, [4 * D, D]), pp=dt_in("pp%d" % l, [128, NPP]),
            bc=dt_in("bc%d" % l, [128, NBC]), waug=dt_in("waug%d" % l, [32, 256]),
            wpw=dt_in("wpw%d" % l, [256, 256])))
    outT = nc.dram_tensor("outT", [D, TL // 2], F32, kind="ExternalOutput").ap()
    XST = nc.dram_tensor("XST", [D, NT], F32).ap()
    PT = nc.dram_tensor("PTs", [NT, PTW], BF16).ap()
    YT = nc.dram_tensor("YTs", [D, NT], BF16).ap()
    WOb = nc.dram_tensor("WOb", [D, D], BF16).ap()
    WUb = nc.dram_tensor("WUb", [D, 4 * D], BF16).ap()
    WDb = nc.dram_tensor("WDb", [4 * D, D], BF16).ap()
    WIb = nc.dram_tensor("WIb", [D, PIN], BF16).ap()
    dbg_out = {}
    if dbg:
        dbg_out['PT'] = nc.dram_tensor("dbgPT", [NT, PTW], BF16, kind="ExternalOutput").ap()
        dbg_out['YT'] = nc.dram_tensor("dbgYT", [D, NT], BF16, kind="ExternalOutput").ap()
        dbg_out['X1'] = nc.dram_tensor("dbgX1", [D, NT], F32, kind="ExternalOutput").ap()
        dbg_out['MOD'] = nc.dram_tensor("dbgMOD", [128, 192], F32, kind="ExternalOutput").ap()

    with ExitStack() as st:
        S = Sched(nc, st)
        for nm, ap_ in [("WIb", WIb), ("WOb", WOb), ("WUb", WUb), ("WDb", WDb), ("xT", xT_in), ("XST", XST), ("PTs", PT), ("YTs", YT), ("outT", outT), ("cvec", cvec_in),
                        ("cst", cst_in), ("ropeA", ropeA_in), ("ropeR", ropeR_in)]:
            S.dram_rowlen[nm] = ap_.shape[1]
        for l in range(DEPTH):
            for k, v in W[l].items():
                S.dram_rowlen[v.tensor.name] = v.shape[1]
        for k, v in dbg_out.items():
            S.dram_rowlen[v.tensor.name] = v.shape[1]
        ARENA_BYTES = 206 * 1024
        at = st.enter_context(nc.sbuf_tensor("arena", [128, ARENA_BYTES], U8))
        A = Arena(S, at, ARENA_BYTES)
        PS = [st.enter_context(nc.psum_tensor("ps%d" % i, [128, 512], F32)) for i in range(8)]
        PSB = [p[:].bitcast(BF16) for p in PS]

        cst = A.alloc([128, NCST], F32)
        S.dma(cst, cst_in)
        identb = A.alloc([128, 128], BF16)
        onesb = A.alloc([128, 128], BF16)
        onesf = A.alloc([128, 128], F32)
        maskb16 = A.alloc([128, 2, 128], BF16)
        MOD = A.alloc([128, DEPTH, 6, 8, 2], F32)
        pp = A.alloc([128, NPP], F32)
        S.copy(identb, cst[:, 0:128])
        S.memset(onesb, 1.0 / 1024)
        S.memset(onesf, 1.0)
        S.copy(maskb16[:, 0, :], cst[:, 640:768])
        S.copy(maskb16[:, 1, :], cst[:, 768:896])
        TRI = [cst[:, 128:256], cst[:, 256:384]]
        TRIR = [cst[:, 384:512], cst[:, 512:640]]
        BDm = cst[:, 896:1152]
        HM = cst[:, 1152:1156]
        base_mark = A.mark()

        class _Stop(Exception):
            pass

        def load_cast(dst, src, stg, width, engs=('vector', 'gpsimd')):
            n = dst.shape[1]
            i = 0
            c0 = 0
            while c0 < n:
                w = min(width, n - c0)
                sg = stg[i % len(stg)]
                S.dma(sg[:, 0:w], src[:, c0:c0 + w], q=('sync' if i % 2 == 0 else 'gpsimd'))
                S.copy(dst[:, c0:c0 + w], sg[:, 0:w], eng=engs[i % len(engs)])
                c0 += w
                i += 1

        sc_p = A.alloc([128, 8, 2], F32)
        base_mark = A.mark()

        def mod_finalize(l, acc, ppl):
            bm = ppl[:, 16:64].unsqueeze(2).to_broadcast([128, 48, 2])
            S.tt(acc, acc, bm, ALU.add)
            a4 = acc.rearrange("p (w c) m -> p w c m", w=6)
            g1n = ppl[:, 0:8].unsqueeze(2).to_broadcast([128, 8, 2])
            g2n = ppl[:, 8:16].unsqueeze(2).to_broadcast([128, 8, 2])
            S.stt(MOD[:, l, 0], a4[:, 1], 1.0, g1n, ALU.add, ALU.mult)
            S.copy(MOD[:, l, 1], a4[:, 0])
            S.copy(MOD[:, l, 2], a4[:, 2])
            S.stt(MOD[:, l, 3], a4[:, 4], 1.0, g2n, ALU.add, ALU.mult)
            S.copy(MOD[:, l, 4], a4[:, 3])
            S.copy(MOD[:, l, 5], a4[:, 5])

        def mod_chunk(l, k, wk, acc, pbank):
            pm = PS[pbank][:, 0:96].rearrange("p (j m) -> p j m", m=2)
            for j in range(48):
                S.mm(pm[:, j, :], [(wk[:, j * 128:(j + 1) * 128], sc_p[:, k, :])])
            if k == 0:
                S.copy(acc, pm)
            else:
                S.tt(acc, acc, pm, ALU.add)

        def phase_mod():
            mk = A.mark()
            cs = A.alloc([128, 8, 2], F32)
            S.dma(cs, cvec_in.rearrange("p (c m) -> p c m", m=2))
            S.act(sc_p, cs, AF.Silu)
            wst = [A.alloc([128, 6144], F32) for _ in range(2)]
            acc = A.alloc([128, 48, 2], F32)
            ppl = A.alloc([128, NPP], F32)
            S.dma(ppl, W[0]['pp'])
            for k in range(8):
                wk = wst[k % 2]
                S.dma(wk, W[0]['w_mod'][k * 128:(k + 1) * 128, :], q=('sync' if k % 2 == 0 else 'gpsimd'))
                mod_chunk(0, k, wk, acc, k % 2)
            mod_finalize(0, acc, ppl)
            if dbg:
                S.dma(dbg_out['MOD'], MOD.rearrange("p l w c m -> p (l w c m)"), q='gpsimd')
            S.barrier()
            A.reset(mk)

        def norm_mod(xg, n, sq, tmp, hT, Acol, Bcol, pbank):
            S.act(sq[:, :, 0:n], xg[:, :, 0:n], AF.Square)
            ps = PS[pbank][:, 0:n]
            S.mm(ps, [(onesb, sq[:, c, 0:n]) for c in range(8)])
            rstd = tmp[0][:, 0:n]
            S.act(rstd, ps, AF.Sqrt, bias=EPS)
            S.recip(rstd, rstd)
            for c in range(8):
                t = tmp[1 + c % 2][:, 0:n]
                S.tt(t, xg[:, c, 0:n], rstd, ALU.mult, eng=('vector' if c % 2 == 0 else 'gpsimd'))
                S.act(hT[:, c, 0:n], t, AF.Identity, bias=Bcol[:, c:c + 1], scale=Acol[:, c:c + 1])

        def phase_A(l, Xsrc, zT, uTl, uTc):
            mk = A.mark()
            WIN = A.alloc([128, 8, PIN], BF16)
            if l == 0:
                stg = [A.alloc([128, PIN], F32) for _ in range(2)]
                for k in range(8):
                    S.dma(stg[k % 2], W[l]['w_in'][k * 128:(k + 1) * 128, :], q=('sync' if k % 2 == 0 else 'gpsimd'))
                    S.copy(WIN[:, k, :], stg[k % 2], eng=('vector' if k % 2 == 0 else 'gpsimd'))
            else:
                for k2 in range(4):
                    S.dma(WIN[:, 2 * k2:2 * k2 + 2, :], WIb[k2 * 256:(k2 + 1) * 256, :].rearrange("(k p) n -> p k n", p=128),
                          q=('sync' if k2 % 2 == 0 else 'gpsimd'))
            xgs = [A.alloc([128, 8, 512], F32) for _ in range(2)]
            sq = A.alloc([128, 8, 512], BF16)
            tmp = [A.alloc([128, 512], F32) for _ in range(3)]
            hTs = [A.alloc([128, 8, 512], BF16) for _ in range(2)]
            OTs = [A.alloc([128, PTW], BF16) for _ in range(2)]
            sig = [A.alloc([128, 512], F32) for _ in range(2)]
            Xv = Xsrc.rearrange("(c p) t -> p c t", p=128)
            colblocks = [(512, 1024), (1024, 1536), (1536, 2048), (2048, 2560), (2560, 2576)]
            ti = 0

            def prep_group(g):
                n = 512 if g < 8 else 256
                m = 0 if g < 8 else 1
                S.dma(xgs[g % 2][:, :, 0:n], Xv[:, :, g * 512:g * 512 + n])
                norm_mod(xgs[g % 2], n, sq, tmp, hTs[g % 2], MOD[:, l, 0, :, m], MOD[:, l, 1, :, m], 0)
            prep_group(0)
            for g in range(9):
                n = 512 if g < 8 else 256
                t0 = g * 512
                m = 0 if g < 8 else 1
                hT = hTs[g % 2]
                if g + 1 < 9:
                    prep_group(g + 1)
                for tt_ in range(n // 128):
                    OT = OTs[ti % 2]
                    for bi, (c0, c1) in enumerate(colblocks):
                        w = c1 - c0
                        ps = PS[1 + (bi % 2)][:, 0:w]
                        S.mm(ps, [(hT[:, c, tt_ * 128:(tt_ + 1) * 128], WIN[:, c, c0:c1]) for c in range(8)])
                        if bi % 2 == 0:
                            S.copy(OT[:, c0 - 512:c1 - 512], ps, eng='scalar')
                        else:
                            S.copy(OT[:, c0 - 512:c1 - 512], ps, eng='vector')
                    tok = t0 + tt_ * 128
                    S.dma(PT[tok:tok + 128, :], OT, q='gpsimd')
                    ti += 1
                for cc in range(2):
                    pa = PS[3][:, 0:n]
                    pg = PS[4][:, 0:n]
                    S.mm(pa, [(WIN[:, c, cc * 128:(cc + 1) * 128], hT[:, c, 0:n]) for c in range(8)])
                    S.mm(pg, [(WIN[:, c, 256 + cc * 128:256 + (cc + 1) * 128], hT[:, c, 0:n]) for c in range(8)])
                    sg = sig[cc][:, 0:n]
                    S.act(sg, pg, AF.Sigmoid)
                    if g < 8:
                        dst = uTl[:, cc, 15 + t0:15 + t0 + n]
                    else:
                        dst = uTc[:, cc, 15:15 + n]
                    S.tt(dst, pa, sg, ALU.mult)
                pz = PS[5][0:16, 0:n]
                S.mm(pz, [(WIN[:, c, 1280:1296], hT[:, c, 0:n]) for c in range(8)])
                S.copy(zT[0:16, t0:t0 + n], pz)
            S.barrier()
            A.reset(mk)

        def phase_conv(l, uTl, uTc, need_ctx, nblk):
            mk = A.mark()
            DG = A.alloc([128, 2, 31, 128], BF16)
            WPW = A.alloc([128, 2, 256], BF16)
            wst = A.alloc([128, 2, 256], F32)
            S.dma(wst, W[l]['wpw'].rearrange("(c p) n -> p c n", p=128))
            S.copy(WPW, wst)
            for cc in range(2):
                for j in range(31):
                    S.ts(DG[:, cc, j, :], identb, pp[:, 72 + cc * 31 + j:73 + cc * 31 + j], None, ALU.mult,
                         eng=('vector' if j % 2 == 0 else 'gpsimd'))
            y = [A.alloc([128, 512], F32) for _ in range(2)]
            ysq = [A.alloc([128, 512], F32) for _ in range(2)]
            msb = A.alloc([128, 512], F32)
            m2 = A.alloc([128, 512], F32)
            rstd = A.alloc([128, 512], F32)
            t1 = [A.alloc([128, 512], F32) for _ in range(2)]
            sb = [A.alloc([128, 512], BF16) for _ in range(2)]
            yo = [A.alloc([128, 2, 512], BF16) for _ in range(2)]
            blocks = [(uTl, b * 512, 512, b * 512) for b in range(nblk)]
            if need_ctx:
                blocks.append((uTc, 0, 256, TL))
            bg_mod = (l + 1 < nlayers) and len(blocks) >= 9
            if bg_mod:
                wst = [A.alloc([128, 6144], F32) for _ in range(2)]
                macc = A.alloc([128, 48, 2], F32)
                mppl = A.alloc([128, NPP], F32)
                S.dma(mppl, W[l + 1]['pp'])
                S.dma(wst[0], W[l + 1]['w_mod'][0:128, :], q='sync')
            for bi, (uT, t0, n, tok0) in enumerate(blocks):
                if bg_mod and bi + 1 < 8:
                    S.dma(wst[(bi + 1) % 2], W[l + 1]['w_mod'][(bi + 1) * 128:(bi + 2) * 128, :], q='sync')
                for cc in range(2):
                    pc = PS[cc][:, 0:n]
                    S.mm(pc, [(DG[:, cc, j, :], uT[:, cc, t0 + j:t0 + j + n]) for j in range(31)])
                    S.act(y[cc][:, 0:n], pc, AF.Identity, bias=pp[:, 64 + cc:65 + cc])
                    S.act(ysq[cc][:, 0:n], pc, AF.Square, bias=pp[:, 64 + cc:65 + cc])
                pm = PS[2][:, 0:n]
                pq = PS[3][:, 0:n]
                S.mm(pm, [(onesf, y[0][:, 0:n]), (onesf, y[1][:, 0:n])])
                S.mm(pq, [(onesf, ysq[0][:, 0:n]), (onesf, ysq[1][:, 0:n])])
                S.act(msb[:, 0:n], pm, AF.Identity, scale=1.0 / 256)
                S.act(m2[:, 0:n], pm, AF.Square, scale=1.0 / 256)
                S.stt(rstd[:, 0:n], pq, 1.0 / 256, m2[:, 0:n], ALU.mult, ALU.subtract)
                S.act(rstd[:, 0:n], rstd[:, 0:n], AF.Sqrt, bias=EPS)
                S.recip(rstd[:, 0:n], rstd[:, 0:n])
                for cc in range(2):
                    e_ = 'vector' if cc == 0 else 'gpsimd'
                    S.tt(t1[cc][:, 0:n], y[cc][:, 0:n], msb[:, 0:n], ALU.subtract, eng=e_)
                    S.tt(t1[cc][:, 0:n], t1[cc][:, 0:n], rstd[:, 0:n], ALU.mult, eng=e_)
                    S.act(sb[cc][:, 0:n], t1[cc][:, 0:n], AF.Silu, bias=pp[:, 68 + cc:69 + cc], scale=pp[:, 66 + cc:67 + cc])
                yob = yo[bi % 2]
                for co in range(2):
                    ppw = PS[4 + co][:, 0:n]
                    S.mm(ppw, [(WPW[:, ci, co * 128:(co + 1) * 128], sb[ci][:, 0:n]) for ci in range(2)])
                    S.act(yob[:, co, 0:n], ppw, AF.Identity, bias=pp[:, 70 + co:71 + co])
                S.dma(YT[0:256, tok0:tok0 + n].rearrange("(c p) t -> p c t", p=128), yob[:, :, 0:n], q='gpsimd')
                if bg_mod and bi < 8:
                    mod_chunk(l + 1, bi, wst[bi % 2], macc, 6 + bi % 2)
            if bg_mod:
                mod_finalize(l + 1, macc, mppl)
            S.barrier()
            A.reset(mk)

        def phase_att(l, need_ctx, nqb):
            mk = A.mark()
            QK = A.alloc([128, 4, NT], BF16)
            VA = A.alloc([128, NTILE, 2, 128], BF16)
            QZ = A.alloc([128, 4, NT], BF16)
            VB = A.alloc([128, NTILE, 2, 128], BF16)
            rope = A.alloc([128, 32, 64], F32)
            bc = A.alloc([128, 384], F32)
            S.dma(rope, ropeA_in.rearrange("p (t k) -> p t k", k=64))
            S.dma(bc, W[l]['bc'][:, 512:896])
            S.memset(VA, 0.0)
            S.memset(VA[:, :, :, 64:65], 1.0)
            S.memset(QZ, 0.0, eng='gpsimd')
            S.memset(VB, 0.0, eng='gpsimd')
            S.memset(VB[:, :, :, 0:1], 1.0, eng='gpsimd')
            NB = 2
            raw = [A.alloc([128, NB, 512], BF16) for _ in range(2)]
            f1 = A.alloc([128, NB, 6, 64], F32)
            f2 = A.alloc([128, NB, 6, 64], F32)
            ssq = A.alloc([128, NB, 6], F32)
            qr = [A.alloc([128, NB, 8, 64], BF16) for _ in range(2)]
            tA = A.alloc([128, NB, 6, 32], F32)
            tB = A.alloc([128, NB, 6, 32], F32)
            for bi in range(NTILE // NB):
                tl0 = bi * NB
                rw = raw[bi % 2]
                qb_ = qr[bi % 2]
                S.dma(rw, PT[tl0 * 128:(tl0 + NB) * 128, 784:1296].rearrange("(t p) c -> p t c", p=128))
                qk = rw[:, :, 0:384].rearrange("p t (h e) -> p t h e", e=64)
                S.tt(f1, qk, qk, ALU.mult)
                S.reduce(ssq, f1)
                S.act(ssq, ssq, AF.Sqrt, bias=EPS, scale=1.0 / 64)
                S.recip(ssq, ssq)
                S.tt(f1, qk, ssq.unsqueeze(3).to_broadcast([128, NB, 6, 64]), ALU.mult)
                gq = bc.rearrange("p (h e) -> p h e", e=64).unsqueeze(1).to_broadcast([128, NB, 6, 64])
                S.tt(f2, f1, gq, ALU.mult, eng='gpsimd')
                is_lat = tl0 < 32
                if is_lat:
                    cosv = rope[:, tl0:tl0 + NB, 0:32].unsqueeze(2).to_broadcast([128, NB, 6, 32])
                    sinv = rope[:, tl0:tl0 + NB, 32:64].unsqueeze(2).to_broadcast([128, NB, 6, 32])
                    x1 = f2[:, :, :, 0:32]
                    x2 = f2[:, :, :, 32:64]
                    o1 = f1[:, :, :, 0:32]
                    o2 = f1[:, :, :, 32:64]
                    S.tt(tA, x1, cosv, ALU.mult)
                    S.tt(tB, x2, sinv, ALU.mult, eng='gpsimd')
                    S.tt(o1, tA, tB, ALU.subtract)
                    S.tt(tA, x2, cosv, ALU.mult)
                    S.tt(tB, x1, sinv, ALU.mult, eng='gpsimd')
                    S.tt(o2, tA, tB, ALU.add)
                    src = f1
                else:
                    src = f2
                S.copy(qb_[:, :, 0:4, :], src[:, :, 0:4, :])
                kdst = qb_[:, :, 4:8, :].rearrange("p t (k r) e -> p t k r e", r=2)
                for r_ in range(2):
                    S.copy(kdst[:, :, :, r_, :], src[:, :, 4:6, :], eng='gpsimd')
                for t_ in range(NB):
                    tl = tl0 + t_
                    pt = PSB[6][:, 0:512].rearrange("p (j k) -> p j k", k=128)
                    for j in range(4):
                        S.transpose(pt[:, j, :], qb_[:, t_, 2 * j:2 * j + 2, :].rearrange("p a e -> p (a e)"), identb)
                    S.copy(QK[:, :, tl * 128:(tl + 1) * 128], pt, eng=('vector' if t_ % 2 == 0 else 'scalar'))
                    for j in range(2):
                        S.copy(QZ[0:64, 2 * j, tl * 128:(tl + 1) * 128], pt[0:64, j, :], eng='vector')
                        S.copy(QZ[64:128, 2 * j + 1, tl * 128:(tl + 1) * 128], pt[64:128, j, :], eng='scalar')
                    vv = rw[:, t_, 384:512].rearrange("p (k e) -> p k e", e=64)
                    S.copy(VA[:, tl, :, 0:64], vv, eng='gpsimd')
                    S.copy(VB[:, tl, :, 64:128], vv, eng='gpsimd')
            NS, LA, NP = 4, 3, 5
            Pt = [A.alloc([128, 512], BF16) for _ in range(NP)]
            rsb = A.alloc([128, 512], F32)
            bcs = A.alloc([128, 512], F32)
            Yo = [A.alloc([128, 512], BF16) for _ in range(2)]
            qblocks = [(qb * 512, 512, list(range(NTILE))) for qb in range(nqb)]
            if need_ctx:
                qblocks.append((TL, 256, [32, 33]))
            it = 0
            gi = 0
            cvs = [A.alloc([128, 2048], F32) for _ in range(2)]
            cvb = [A.alloc([128, 2048], BF16) for _ in range(2)]
            pieces = []
            for k in range(8):
                pieces.append((W[l]['w_out'][k * 128:(k + 1) * 128, :], WOb[k * 128:(k + 1) * 128, :], 1024))
            for k in range(8):
                for c_ in range(2):
                    pieces.append((W[l]['w_up'][k * 128:(k + 1) * 128, c_ * 2048:(c_ + 1) * 2048],
                                   WUb[k * 128:(k + 1) * 128, c_ * 2048:(c_ + 1) * 2048], 2048))
            for k in range(32):
                pieces.append((W[l]['w_down'][k * 128:(k + 1) * 128, :], WDb[k * 128:(k + 1) * 128, :], 1024))
            if l + 1 < nlayers:
                for k in range(8):
                    pieces.append((W[l + 1]['w_in'][k * 128:(k + 1) * 128, 0:2048], WIb[k * 128:(k + 1) * 128, 0:2048], 2048))
                    pieces.append((W[l + 1]['w_in'][k * 128:(k + 1) * 128, 2048:PIN], WIb[k * 128:(k + 1) * 128, 2048:PIN], PIN - 2048))
            n_iters = 4 * len(qblocks)
            ppi = (len(pieces) + n_iters - 1) // n_iters
            pci = [0]

            def convert_some():
                for _ in range(ppi):
                    if pci[0] >= len(pieces):
                        return
                    src, dst, w = pieces[pci[0]]
                    sg = cvs[pci[0] % 2]
                    cb = cvb[pci[0] % 2]
                    S.dma(sg[:, 0:w], src, q='sync')
                    S.copy(cb[:, 0:w], sg[:, 0:w], eng='gpsimd')
                    S.dma(dst, cb[:, 0:w], q='gpsimd')
                    pci[0] += 1
            for h in range(4):
                pair, half = h // 2, h % 2
                kvh = pair
                p0 = 64 * half
                for (q0, nq, kts) in qblocks:
                    convert_some()
                    O = PS[4 + it % 2]
                    rhsq = QZ[:, h, q0:q0 + nq]
                    nk = len(kts)
                    sps = {}

                    def smm(i):
                        kt = kts[i]
                        sp = PS[(gi + i) % NS][:, 0:nq]
                        S.mm(sp, [(QK[:, 2 + kvh, kt * 128:(kt + 1) * 128], rhsq)])
                        sps[i] = sp
                    for i in range(min(LA, nk)):
                        smm(i)
                    for i in range(nk):
                        if i + LA < nk:
                            smm(i + LA)
                        sp = sps.pop(i)
                        kt = kts[i]
                        pt_ = Pt[(gi + i) % NP][:, 0:nq]
                        S.act(pt_, sp, AF.Exp, scale=0.125)
                        if KEEPWARM:
                            S.add('tensor', lambda e: e.matmul(PS[7][:, 0:KEEPWARM], identb, QK[:, 0, 0:KEEPWARM],
                                                               start=True, stop=True), reads=[], writes=[])
                        if half == 0:
                            S.mm(O[:, 0:nq], [(VA[:, kt, kvh, :], pt_)], start=(i == 0), stop=(i == nk - 1), acc=(i > 0))
                        else:
                            S.mm(O[:, 0:nq], [(VB[:, kt, kvh, :], pt_)], start=(i == 0), stop=(i == nk - 1), acc=(i > 0))
                    gi += nk
                    yo_ = Yo[it % 2]
                    pb = PS[6]
                    if half == 0:
                        S.recip(rsb[64:65, 0:nq], O[64:65, 0:nq])
                        S.mm(pb[0:64, 0:nq], [(onesf[64:65, 0:64], rsb[64:65, 0:nq])])
                        S.copy(bcs[0:64, 0:nq], pb[0:64, 0:nq], eng='vector')
                        S.tt(yo_[0:64, 0:nq], O[0:64, 0:nq], bcs[0:64, 0:nq], ALU.mult)
                        S.dma(YT[512 + h * 64:512 + (h + 1) * 64, q0:q0 + nq], yo_[0:64, 0:nq], q='gpsimd')
                    else:
                        S.recip(rsb[0:1, 0:nq], O[0:1, 0:nq])
                        S.mm(pb[:, 0:nq], [(onesf[0:1, :], rsb[0:1, 0:nq])])
                        S.copy(bcs[64:128, 0:nq], pb[64:128, 0:nq], eng='vector')
                        S.tt(yo_[64:128, 0:nq], O[64:128, 0:nq], bcs[64:128, 0:nq], ALU.mult)
                        S.dma(YT[512 + h * 64:512 + (h + 1) * 64, q0:q0 + nq], yo_[64:128, 0:nq], q='gpsimd')
                    it += 1
            S.barrier()
            A.reset(mk)

        def phase_scan(l, kind, zT, need_ctx, nlt):
            mk = A.mark()
            is_gla = (kind == 'gla')
            pc0 = 0 if is_gla else 1296
            yrow = 256 if is_gla else 768
            qk_all = A.alloc([128, NTILE, 256], BF16)
            vg_all = A.alloc([128, NTILE, 512], BF16)
            QKT = A.alloc([128, 2, NT], BF16)
            oacc = A.alloc([128, NTILE, 256], F32)
            PTv = PT.rearrange("(t p) c -> p t c", p=128)
            for i_ in range(17):
                sl = slice(2 * i_, 2 * i_ + 2)
                S.dma(qk_all[:, sl, :], PTv[:, sl, pc0:pc0 + 256], q=('sync' if i_ % 2 == 0 else 'gpsimd'))
                S.dma(vg_all[:, sl, :], PTv[:, sl, pc0 + 256:pc0 + 768], q=('gpsimd' if i_ % 2 == 0 else 'sync'))
            gn = A.alloc([128, 256], F32)
            S.dma(gn, W[l]['bc'][:, (0 if is_gla else 256):(256 if is_gla else 512)])
            mk2 = A.mark()
            if not is_gla:
                rope = A.alloc([128, 32, 32], F32)
                S.dma(rope, ropeR_in.rearrange("p (t k) -> p t k", k=32))
                NB = 4
                tA = A.alloc([128, NB, 8, 16], F32)
                tB = A.alloc([128, NB, 8, 16], F32)
                tC = A.alloc([128, NB, 8, 16], F32)
                tD = A.alloc([128, NB, 8, 16], F32)
                for bi in range(32 // NB):
                    tl0 = bi * NB
                    v4 = qk_all[:, tl0:tl0 + NB, :].rearrange("p t (h e) -> p t h e", e=32)
                    x1 = v4[:, :, :, 0:16]
                    x2 = v4[:, :, :, 16:32]
                    cosv = rope[:, tl0:tl0 + NB, 0:16].unsqueeze(2).to_broadcast([128, NB, 8, 16])
                    sinv = rope[:, tl0:tl0 + NB, 16:32].unsqueeze(2).to_broadcast([128, NB, 8, 16])
                    S.tt(tA, x1, cosv, ALU.mult)
                    S.tt(tB, x2, sinv, ALU.mult, eng='gpsimd')
                    S.tt(tC, x2, cosv, ALU.mult)
                    S.tt(tD, x1, sinv, ALU.mult, eng='gpsimd')
                    S.tt(x1, tA, tB, ALU.subtract)
                    S.tt(x2, tC, tD, ALU.add, eng='gpsimd')
            for tl in range(NTILE):
                pt = PSB[6][:, 0:256].rearrange("p (j k) -> p j k", k=128)
                for j in range(2):
                    S.transpose(pt[:, j, :], qk_all[:, tl, j * 128:(j + 1) * 128], identb)
                S.copy(QKT[:, :, tl * 128:(tl + 1) * 128], pt, eng=('vector' if tl % 2 == 0 else 'scalar'))
            S.barrier()
            A.reset(mk2)
            if stop_after == kind + '_prep':
                raise _Stop()
            Sf = A.alloc([128, 256], F32)
            Sb = A.alloc([128, 256], BF16)
            kiT = [A.alloc([128, 4, 128], BF16) for _ in range(2)]
            kvm = A.alloc([128, 256], F32)
            if is_gla:
                WA = A.alloc([32, 256], BF16)
                was = A.alloc([32, 256], F32)
                S.dma(was, W[l]['waug'])
                S.copy(WA, was)
                ee = [A.alloc([128, 128], F32) for _ in range(2)]
                ll = [A.alloc([128, 128], F32) for _ in range(2)]
                E1s = [A.alloc([128, 128], F32) for _ in range(2)]
                E2s = [A.alloc([128, 128], F32) for _ in range(2)]
                E3s = [A.alloc([128, 128], F32) for _ in range(2)]
            LNS = math.log(32 ** -0.5)
            NQ = 6
            qdT = [A.alloc([128, 128], BF16) for _ in range(NQ)]
            kend = [A.alloc([128, 128], BF16) for _ in range(NQ)]
            scm = [A.alloc([128, 4, 128], BF16) for _ in range(3)]
            if is_gla:
                dcs = [A.alloc([128, 1], F32) for _ in range(NQ)]
            for d_ in range(2):
                S.memset(Sf, 0.0)
                S.memset(Sb, 0.0)
                order = [32, 33] + list(range(32)) if d_ == 0 else [33, 32] + list(range(31, -1, -1))
                if stop_after and stop_after.startswith(kind + '_n'):
                    order = order[:int(stop_after[len(kind) + 2:])]
                n_st = len(order)
                want = [need_ctx or ch < nlt for ch in order]
                tks = [slice(ch * 128, (ch + 1) * 128) for ch in order]
                if not is_gla:
                    o_ = 1156 + d_ * 384
                    cE1 = cst[:, o_:o_ + 128]
                    cE2 = cst[:, o_ + 128:o_ + 256]
                    cE3 = cst[:, o_ + 256:o_ + 384]
                    cdc = cst[:, 1924:1925]

                def stA(j):
                    if is_gla:
                        S.mm(PS[0][:, 0:128], [(zT[0:32, tks[j]], WA[0:32, d_ * 128:(d_ + 1) * 128])])

                def stB(j):
                    if is_gla:
                        S.act(ee[j % 2], PS[0][:, 0:128], AF.Exp, scale=-1.0)
                        S.act(ll[j % 2], ee[j % 2], AF.Ln, bias=1.0)

                def stC(j):
                    if is_gla:
                        S.mm(PS[1][:, 0:128], [(ll[j % 2], TRI[d_])])
                        S.mm(PS[1][:, 128:256], [(TRIR[d_], ll[j % 2])])

                def stD(j):
                    if is_gla:
                        pbT = PS[1][:, 0:128]
                        S.act(E1s[j % 2], pbT, AF.Exp, bias=LNS)
                        S.act(E2s[j % 2], pbT, AF.Exp, scale=-1.0)
                        S.act(E3s[j % 2], PS[1][:, 128:256], AF.Exp)
                        col = 127 if d_ == 0 else 0
                        S.act(dcs[j % NQ], pbT[:, col:col + 1], AF.Exp)

                def stE(j):
                    E1, E2, E3 = (E1s[j % 2], E2s[j % 2], E3s[j % 2]) if is_gla else (cE1, cE2, cE3)
                    S.tt(qdT[j % NQ], QKT[:, 0, tks[j]], E1, ALU.mult, eng='gpsimd')
                    if want[j]:
                        for hh in range(4):
                            S.stt(kiT[j % 2][:, hh, :], QKT[:, 1, tks[j]], HM[:, hh:hh + 1], E2, ALU.mult, ALU.mult)
                    S.tt(kend[j % NQ], qk_all[:, order[j], 128:256], E3, ALU.mult, eng='gpsimd')

                def stF(j):
                    if want[j]:
                        psc = PS[2 + j % 2][:, :].rearrange("p (h c) -> p h c", c=128)
                        for hh in range(4):
                            S.mm(psc[:, hh, :], [(kiT[j % 2][:, hh, :], qdT[j % NQ])])

                def stG(j):
                    if want[j]:
                        psc = PS[2 + j % 2][:, :].rearrange("p (h c) -> p h c", c=128)
                        S.tt(scm[j % 3], psc, bcast_mid(maskb16[:, d_, :], 4), ALU.mult)

                def stH(j):
                    ch = order[j]
                    qd = qdT[j % NQ]
                    ke = kend[j % NQ]
                    sm = scm[j % 3]
                    dc = dcs[j % NQ] if is_gla else cdc
                    vch = vg_all[:, ch, 0:256]
                    if want[j]:
                        po = PS[4 + j % 2][:, 0:256]
                        S.mm(po, [(qd, Sb)], start=True, stop=False)
                        for hh in range(4):
                            S.mm(po[:, hh * 64:(hh + 1) * 64], [(sm[:, hh, :], vch[:, hh * 64:(hh + 1) * 64])],
                                 start=False, stop=True, acc=True)
                        if d_ == 0:
                            S.copy(oacc[:, ch, :], po, eng='scalar')
                        else:
                            S.tt(oacc[:, ch, :], oacc[:, ch, :], po, ALU.add)
                    pkv = PS[6 + j % 2][:, 0:256]
                    S.mm(pkv, [(ke, vch)])
                    S.tt(kvm, pkv, BDm, ALU.mult)
                    S.stt(Sb, Sf, dc, kvm, ALU.mult, ALU.add)
                    S.stt(Sf, Sf, dc, kvm, ALU.mult, ALU.add)

                stages = [(stH, 0), (stG, 1), (stF, 2), (stE, 3), (stD, 4), (stC, 5), (stB, 6), (stA, 7)]
                for t in range(-7, n_st):
                    for fn_, lead in stages:
                        j = t + lead
                        if 0 <= j < n_st:
                            fn_(j)
            if stop_after and stop_after.startswith(kind + '_n'):
                raise _Stop()
            NB = 2
            f1 = A.alloc([128, NB, 4, 64], F32)
            ssq = A.alloc([128, NB, 4], F32)
            sg = A.alloc([128, NB, 256], F32)
            yb = [A.alloc([128, NB, 256], BF16) for _ in range(2)]
            yts = [A.alloc([128, 2, NB * 128], BF16) for _ in range(2)]
            ntl = NTILE if need_ctx else nlt
            for bi in range(ntl // NB):
                tl0 = bi * NB
                o4 = oacc[:, tl0:tl0 + NB, :].rearrange("p t (h e) -> p t h e", e=64)
                S.tt(f1, o4, o4, ALU.mult)
                S.reduce(ssq, f1)
                S.act(ssq, ssq, AF.Sqrt, bias=EPS, scale=1.0 / 64)
                S.recip(ssq, ssq)
                S.tt(f1, o4, ssq.unsqueeze(3).to_broadcast([128, NB, 4, 64]), ALU.mult)
                f1f = f1.rearrange("p t h e -> p t (h e)")
                S.tt(f1f, f1f, bcast_mid(gn, NB), ALU.mult, eng='gpsimd')
                S.act(sg, vg_all[:, tl0:tl0 + NB, 256:512], AF.Silu)
                y_ = yb[bi % 2]
                S.tt(y_, f1f, sg, ALU.mult)
                yt_ = yts[bi % 2]
                pt = PSB[6][:, 0:2 * NB * 128].rearrange("p (j k) -> p j k", j=2)
                for t_ in range(NB):
                    for j in range(2):
                        S.transpose(pt[:, j, t_ * 128:(t_ + 1) * 128], y_[:, t_, j * 128:(j + 1) * 128], identb)
                S.copy(yt_, pt, eng='scalar')
                S.dma(YT[yrow:yrow + 256, tl0 * 128:(tl0 + NB) * 128].rearrange("(c p) t -> p c t", p=128), yt_, q='gpsimd')
            S.barrier()
            A.reset(mk)

        def phase_C(l, Xsrc, last, need_ctx, nlg):
            mk = A.mark()
            WOUT = A.alloc([128, 8, D], BF16)
            WUP = A.alloc([128, 8, 4 * D], BF16)
            WDN = A.alloc([128, 32, D], BF16)
            hid = A.alloc([128, 32, 256], BF16)
            S.dma(WOUT, WOb.rearrange("(k p) n -> p k n", p=128))
            for k2 in range(4):
                S.dma(WUP[:, 2 * k2:2 * k2 + 2, :], WUb[k2 * 256:(k2 + 1) * 256, :].rearrange("(k p) n -> p k n", p=128),
                      q=('gpsimd' if k2 % 2 == 0 else 'sync'))
            for k8 in range(4):
                S.dma(WDN[:, 8 * k8:8 * k8 + 8, :], WDb[k8 * 1024:(k8 + 1) * 1024, :].rearrange("(k p) n -> p k n", p=128),
                      q=('sync' if k8 % 2 == 0 else 'gpsimd'))
            xgs = [A.alloc([128, 8, 256], F32) for _ in range(2)]
            Yg = A.alloc([128, 8, 256], BF16)
            sq = A.alloc([128, 8, 256], BF16)
            tmp = [A.alloc([128, 256], F32) for _ in range(3)]
            h2T = A.alloc([128, 8, 256], BF16)
            rr = [tmp[1], tmp[2]] + [A.alloc([128, 256], F32) for _ in range(2)]
            Xv = Xsrc.rearrange("(c p) t -> p c t", p=128)
            XSv = XST.rearrange("(c p) t -> p c t", p=128)
            OUv = outT.rearrange("(c p) t -> p c t", p=128)
            YTv = YT.rearrange("(c p) t -> p c t", p=128)
            ngroups = 17 if need_ctx else nlg
            n = 256
            S.dma(xgs[0], Xv[:, :, 0:n])

            def down_part(gp, dc):
                xp = xgs[gp % 2]
                mp = 0 if gp < 16 else 1
                pd = PS[1 + dc % 2][:, 0:n]
                S.mm(pd, [(WDN[:, fc, dc * 128:(dc + 1) * 128], hid[:, fc, :]) for fc in range(32)])
                S.stt(xp[:, dc, :], pd, MOD[:, l, 5, :, mp][:, dc:dc + 1], xp[:, dc, :], ALU.mult, ALU.add)

            def finish_group(gp):
                xp = xgs[gp % 2]
                tp = gp * 256
                if not last:
                    S.dma(XSv[:, :, tp:tp + n], xp, q='gpsimd')
                    if dbg:
                        S.dma(dbg_out['X1'].rearrange("(c p) t -> p c t", p=128)[:, :, tp:tp + n], xp, q='gpsimd')
                else:
                    S.act(sq, xp, AF.Square)
                    ps = PS[0][:, 0:n]
                    S.mm(ps, [(onesb, sq[:, c, :]) for c in range(8)])
                    rstd = tmp[0]
                    S.act(rstd, ps, AF.Sqrt, bias=EPS)
                    S.recip(rstd, rstd)
                    for c in range(8):
                        S.stt(xp[:, c, :], xp[:, c, :], pp[:, 134 + c:135 + c], rstd, ALU.mult, ALU.mult)
                    S.dma(OUv[:, :, tp:tp + n], xp, q='gpsimd')

            for g in range(ngroups + 1):
                cur = g if g < ngroups else None
                prev = g - 1 if g > 0 else None
                if cur is not None:
                    t0 = g * 256
                    m = 0 if g < 16 else 1
                    xg = xgs[g % 2]
                    S.dma(Yg, YTv[:, :, t0:t0 + n], q='sync')
                    G1 = MOD[:, l, 2, :, m]
                    for dc in range(8):
                        po = PS[1 + dc % 2][:, 0:n]
                        S.mm(po, [(WOUT[:, kc, dc * 128:(dc + 1) * 128], Yg[:, kc, :]) for kc in range(8)])
                        S.stt(xg[:, dc, :], po, G1[:, dc:dc + 1], xg[:, dc, :], ALU.mult, ALU.add)
                    S.act(sq, xg, AF.Square)
                if prev is not None:
                    down_part(prev, 0)
                if cur is not None:
                    pss = PS[0][:, 0:n]
                    S.mm(pss, [(onesb, sq[:, c, :]) for c in range(8)])
                if prev is not None:
                    down_part(prev, 1)
                if cur is not None:
                    rstd = tmp[0]
                    S.act(rstd, pss, AF.Sqrt, bias=EPS)
                    S.recip(rstd, rstd)
                    Acol = MOD[:, l, 3, :, m]
                    Bcol = MOD[:, l, 4, :, m]
                for c in range(8):
                    if cur is not None:
                        t_ = tmp[1 + c % 2]
                        S.tt(t_, xg[:, c, :], rstd, ALU.mult, eng=('vector' if c % 2 == 0 else 'gpsimd'))
                        S.act(h2T[:, c, :], t_, AF.Identity, bias=Bcol[:, c:c + 1], scale=Acol[:, c:c + 1])
                    if prev is not None and c < 6:
                        down_part(prev, 2 + c)
                if prev is not None:
                    finish_group(prev)
                if cur is not None:
                    if g + 1 < ngroups:
                        S.dma(xgs[(g + 1) % 2], Xv[:, :, t0 + n:t0 + 2 * n])
                    for fc in range(32):
                        pu = PS[3 + fc % 4][:, 0:n]
                        S.mm(pu, [(WUP[:, kc, fc * 128:(fc + 1) * 128], h2T[:, kc, :]) for kc in range(8)])
                        r_ = rr[fc % 4]
                        S.act(r_, pu, AF.Relu)
                        S.tt(hid[:, fc, :], r_, r_, ALU.mult, eng=('vector' if fc % 2 == 0 else 'gpsimd'))
            S.barrier()
            A.reset(mk)

        def chk(name):
            if stop_after == name:
                raise _Stop()
        try:
          phase_mod()
          chk('mod')
          for l in range(nlayers):
              need_ctx = l < DEPTH - 1
              last = (l == DEPTH - 1)
              Xsrc = xT_in if l == 0 else XST
              S.dma(pp, W[l]['pp'])
              mk = A.mark()
              zT = A.alloc([32, NT], BF16)
              uTl = A.alloc([128, 2, TL + 30], BF16)
              uTc = A.alloc([128, 2, TC + 30], BF16)
              S.memset(zT, 1.0)
              S.memset(uTl, 0.0, eng='gpsimd')
              S.memset(uTc, 0.0, eng='gpsimd')
              phase_A(l, Xsrc, zT, uTl, uTc)
              if dbg and l == 0:
                  for t_ in range(NTILE):
                      S.dma(dbg_out['PT'][t_ * 128:(t_ + 1) * 128, :], PT[t_ * 128:(t_ + 1) * 128, :], q='gpsimd')
              chk('A')
              hf = 2 if last else 1
              phase_conv(l, uTl, uTc, need_ctx, 8 // hf)
              chk('conv')
              phase_att(l, need_ctx, 8 // hf)
              chk('att')
              phase_scan(l, 'gla', zT, need_ctx, 32 // hf)
              chk('gla')
              phase_scan(l, 'ret', zT, need_ctx, 32 // hf)
              chk('ret')
              if dbg and l == 0:
                  for c_ in range(8):
                      S.dma(dbg_out['YT'][c_ * 128:(c_ + 1) * 128, :], YT[c_ * 128:(c_ + 1) * 128, :], q='gpsimd')
              S.barrier()
              A.reset(mk)
              phase_C(l, Xsrc, last, need_ctx, 16 // hf)
        except _Stop:
            if dbg:
                for c_ in range(8):
                    S.dma(dbg_out['YT'][c_ * 128:(c_ + 1) * 128, :], YT[c_ * 128:(c_ + 1) * 128, :], q='gpsimd')
        S.finish()
        S.emit()
        print("ops", S.n_ops, "waits", S.n_waits, {e: len(S.items[e]) for e in ENGS})
    return nc


def make_consts(rev=False):
    cst = np.zeros((128, NCST), np.float64)
    cst[:, 0:128] = np.eye(128)
    s = np.arange(128)[:, None]
    c = np.arange(128)[None, :]
    cst[:, 128:256] = np.where(s <= c, -1.0 / 16, 0.0)
    cst[:, 256:384] = np.where(s >= c, -1.0 / 16, 0.0)
    cst[:, 384:512] = np.where(s > c, -1.0 / 16, 0.0)
    cst[:, 512:640] = np.where(s < c, -1.0 / 16, 0.0)
    cst[:, 640:768] = np.where(s <= c, 1.0, 0.0)
    cst[:, 768:896] = np.where(s >= c, 1.0, 0.0)
    f = np.arange(128)[:, None] // 32
    he = np.arange(256)[None, :] // 64
    cst[:, 896:1152] = (f == he).astype(np.float64)
    for h in range(4):
        cst[:, 1152 + h] = (np.arange(128) // 32 == h)
    lg = np.log1p(-np.exp2(-5.0 - np.arange(4)))
    lgf = lg[np.arange(128) // 32]
    sc = 32 ** -0.5
    cc = np.arange(128)
    cst[:, 1156:1284] = np.exp(lgf[:, None] * (cc[None, :] + 1))
    cst[:, 1284:1412] = np.exp(-lgf[:, None] * (cc[None, :] + 1)) * sc
    cst[:, 1412:1540] = np.exp(lgf[None, :] * (127 - cc[:, None])) * sc
    cst[:, 1540:1668] = np.exp(lgf[:, None] * (128 - cc[None, :]))
    cst[:, 1668:1796] = np.exp(-lgf[:, None] * (128 - cc[None, :])) * sc
    cst[:, 1796:1924] = np.exp(lgf[None, :] * cc[:, None]) * sc
    cst[:, 1924] = np.exp(lgf * 128)
    t = np.arange(TL)
    if rev:
        t = t[::-1]
    row = (t // 64).astype(np.float64)
    col = (t % 64).astype(np.float64)

    def tables(hd):
        n_ax = hd // 4
        inv = 10000.0 ** (-np.arange(n_ax, dtype=np.float64) / n_ax)
        inv = inv.astype(np.float32).astype(np.float64)
        ang = np.concatenate([row[:, None] * inv, col[:, None] * inv], -1).astype(np.float32)
        return np.cos(ang), np.sin(ang)
    ca, sa = tables(64)
    cr, sr = tables(32)
    ropeA = np.concatenate([ca, sa], -1).reshape(32, 128, 64).transpose(1, 0, 2).reshape(128, 32 * 64)
    ropeR = np.concatenate([cr, sr], -1).reshape(32, 128, 32).transpose(1, 0, 2).reshape(128, 32 * 32)
    return (cst.astype(np.float32), np.ascontiguousarray(ropeA, np.float32), np.ascontiguousarray(ropeR, np.float32))


def fm(v):
    v = np.asarray(v, np.float32)
    return v.reshape(-1, 128).T


def make_in_maps(inp):
    f32 = lambda a: np.ascontiguousarray(np.asarray(a, np.float32))
    consts = [make_consts(False), make_consts(True)]
    per_layer = [[], []]
    for rev in (0, 1):
        for l in range(DEPTH):
            pp = np.zeros((128, NPP), np.float32)
            pp[:, 0:8] = fm(inp['norm1_g'][l])
            pp[:, 8:16] = fm(inp['norm2_g'][l])
            pp[:, 16:64] = fm(inp['b_mod'][l])
            pp[:, 64:66] = fm(inp['conv_b_dw'][l])
            pp[:, 66:68] = fm(inp['conv_ln_g'][l])
            pp[:, 68:70] = fm(inp['conv_ln_b'][l])
            pp[:, 70:72] = fm(inp['conv_b_pw'][l])
            wdw = np.asarray(inp['conv_w_dw'][l], np.float32)
            if rev:
                wdw = wdw[::-1]
            for cc in range(2):
                pp[:, 72 + cc * 31:72 + (cc + 1) * 31] = wdw[:, cc * 128:(cc + 1) * 128].T
            pp[:, 134:142] = fm(inp['final_norm_g'])
            bc = np.zeros((128, NBC), np.float32)
            bc[:, 0:256] = np.asarray(inp['gla_norm_g'][l], np.float32).reshape(1, 256)
            bc[:, 256:512] = np.asarray(inp['ret_norm_g'][l], np.float32).reshape(1, 256)
            qg = np.asarray(inp['att_q_norm_g'][l], np.float32)
            kg = np.asarray(inp['att_k_norm_g'][l], np.float32)
            bc[:, 512:896] = np.concatenate([qg, qg, qg, qg, kg, kg])[None, :]
            waug = np.zeros((32, 256), np.float32)
            fo, bo = (128, 0) if rev else (0, 128)
            waug[0:16, fo:fo + 128] = inp['gla_w_a_f'][l]
            waug[0:16, bo:bo + 128] = inp['gla_w_a_b'][l]
            waug[16, fo:fo + 128] = inp['gla_b_a_f'][l]
            waug[16, bo:bo + 128] = inp['gla_b_a_b'][l]
            per_layer[rev].append({
                "w_mod%d" % l: f32(inp['w_mod'][l]), "w_in%d" % l: f32(inp['w_in'][l]),
                "w_out%d" % l: f32(inp['w_out'][l]), "w_up%d" % l: f32(inp['w_up'][l]),
                "w_down%d" % l: f32(inp['w_down'][l]), "pp%d" % l: pp, "bc%d" % l: bc,
                "waug%d" % l: waug, "wpw%d" % l: f32(inp['conv_w_pw'][l])})
    x = np.asarray(inp['x'], np.float32)
    ctx = np.asarray(inp['ctx'], np.float32)
    c = np.asarray(inp['c'], np.float32)
    cctx = np.asarray(inp['c_ctx'], np.float32)
    maps = []
    for core in range(8):
        b = core % 4
        rev = core // 4
        if rev:
            xT = np.ascontiguousarray(np.concatenate([x[b][::-1], ctx[b][::-1]], 0).T)
        else:
            xT = np.ascontiguousarray(np.concatenate([x[b], ctx[b]], 0).T)
        cvec = np.ascontiguousarray(np.stack([fm(c[b]), fm(cctx)], -1).reshape(128, 16))
        cst, ropeA, ropeR = consts[rev]
        m = {"xT": xT, "cvec": cvec, "cst": cst, "ropeA": ropeA, "ropeR": ropeR}
        for l in range(DEPTH):
            m.update(per_layer[rev][l])
        maps.append(m)
    return maps


_NC_CACHE = {}


def kernel(**inputs):
    maps = make_in_maps(inputs)
    if 'nc' not in _NC_CACHE:
        _NC_CACHE['nc'] = build_program()
    nc = _NC_CACHE['nc']
    res = run_bass_kernel_spmd(nc, maps, core_ids=list(range(8)))
    out = np.empty((4, TL, D), np.float32)
    h = TL // 2
    for b in range(4):
        out[b, 0:h] = res.results[b]["outT"].T
        out[b, h:TL] = res.results[b + 4]["outT"].T[::-1]
    return out
```

```python
import math
from contextlib import ExitStack
import numpy as np
import concourse.bass as bass
import concourse.mybir as mybir
from concourse.bass_utils import run_bass_kernel_spmd

F32 = mybir.dt.float32
BF16 = mybir.dt.bfloat16
U8 = mybir.dt.uint8
AF = mybir.ActivationFunctionType
ALU = mybir.AluOpType
AX = mybir.AxisListType

D = 1024
TL = 4096
TC = 256
NT = TL + TC
NTILE = NT // 128
PIN = 2576
PTW = 2064
EPS = 1e-6
NPP = 142
NBC = 896
NCST = 1925
DEPTH = 2

ENGS = ['tensor', 'vector', 'scalar', 'gpsimd', 'sync']
DMA_POOL = 24
KEEPWARM = 0
DSIZE = {F32: 4, BF16: 2, U8: 1}


class Sched:
    def __init__(self, nc, stack):
        self.nc = nc
        self.items = {e: [] for e in ENGS}
        self.seq = {e: 0 for e in ENGS}
        self.esem = {e: stack.enter_context(nc.semaphore("sq_" + e)) for e in ENGS if e != 'sync'}
        self.dpool = {}
        self.dcount = {}
        for q in ['sync', 'gpsimd']:
            self.dpool[q] = [stack.enter_context(nc.semaphore("dq_%s_%d" % (q, i))) for i in range(DMA_POOL)]
            self.dcount[q] = 0
        self.waited = {}
        self.recs = {}
        self.dram_rowlen = {}
        self.arena_name = None
        self.arena_allocs = []
        self.n_ops = 0
        self.n_waits = 0

    def box(self, ap):
        t = ap.tensor
        name = t.name
        off = int(ap.offset)
        pairs = [(int(s), int(c)) for s, c in ap.ap]
        mx = off + sum(s * (c - 1) for s, c in pairs if c > 0)
        if name in self.dram_rowlen:
            rowlen = self.dram_rowlen[name]
            es = 1
        else:
            rowlen = pairs[0][0]
            es = DSIZE[ap.dtype]
        r0 = off // rowlen
        r1 = mx // rowlen
        c0 = off % rowlen
        c1 = c0 + sum(s * (c - 1) for s, c in pairs if s < rowlen and c > 0)
        if c1 >= rowlen:
            c0, c1 = 0, rowlen - 1
        c0 *= es
        c1 = c1 * es + es - 1
        if name == self.arena_name:
            for (a, b, i) in self.arena_allocs:
                if a <= c0 < b:
                    assert c1 < b, "arena view crosses allocation"
                    name = (name, i)
                    break
            else:
                raise AssertionError("arena box not found")
        return name, (r0, r1, c0, c1)

    def _need(self, eng, ev, waits):
        sem, val = ev
        k = (eng, id(sem))
        if self.waited.get(k, 0) >= val:
            return
        self.waited[k] = val
        waits[id(sem)] = (sem, val)

    def add(self, eng, fn, reads=(), writes=(), dma=False, acc=False):
        waits = {}
        rb = [self.box(a) for a in reads]
        wb = [self.box(a) for a in writes]
        for name, b in rb:
            ps = isinstance(name, str) and name.startswith("ps")
            for r in self.recs.get(name, ()):
                if ps and r[5] != eng:
                    self._need(eng, r[6], waits)
                elif r[4] and not (r[1] < b[0] or b[1] < r[0] or r[3] < b[2] or b[3] < r[2]):
                    self._need(eng, r[6], waits)
        for name, b in wb:
            ps = isinstance(name, str) and name.startswith("ps")
            for r in self.recs.get(name, ()):
                if ps and r[5] != eng:
                    self._need(eng, r[6], waits)
                elif ps and eng == 'tensor':
                    continue
                elif not (r[1] < b[0] or b[1] < r[0] or r[3] < b[2] or b[3] < r[2]):
                    if acc and r[4] and r[5] == 'tensor':
                        continue
                    self._need(eng, r[6], waits)
        if dma:
            q = eng
            i = self.dcount[q]
            self.dcount[q] += 1
            sem = self.dpool[q][i % DMA_POOL]
            val = 16 * (i // DMA_POOL + 1)
            if i >= DMA_POOL:
                self._need(eng, (sem, val - 16), waits)
            ev = (sem, val)
            inc = (sem, 16)
        else:
            self.seq[eng] += 1
            ev = (self.esem[eng], self.seq[eng])
            inc = (self.esem[eng], 1)
        for name, b in wb:
            lst = self.recs.setdefault(name, [])
            lst[:] = [r for r in lst if not (b[0] <= r[0] and r[1] <= b[1] and b[2] <= r[2] and r[3] <= b[3])]
            lst.append([b[0], b[1], b[2], b[3], True, eng, ev])
        for name, b in rb:
            lst = self.recs.setdefault(name, [])
            if not dma:
                lst[:] = [r for r in lst if not ((not r[4]) and r[5] == eng and r[6][0] is ev[0]
                                                 and b[0] <= r[0] and r[1] <= b[1] and b[2] <= r[2] and r[3] <= b[3])]
            lst.append([b[0], b[1], b[2], b[3], False, eng, ev])
        self.items[eng].append((list(waits.values()), fn, inc))
        self.n_ops += 1
        self.n_waits += len(waits)
        return ev

    def barrier(self):
        for eng in ENGS:
            waits = {}
            for q in self.dpool:
                n = self.dcount[q]
                for j in range(min(n, DMA_POOL)):
                    uses = (n - 1 - j) // DMA_POOL + 1
                    self._need(eng, (self.dpool[q][j], 16 * uses), waits)
            for e in self.esem:
                if self.seq[e] > 0:
                    self._need(eng, (self.esem[e], self.seq[e]), waits)
            if waits:
                self.items[eng].append((list(waits.values()), None, None))
        self.recs = {}

    def dma(self, out, in_, q='sync'):
        return self.add(q, lambda e: e.dma_start(out=out, in_=in_), reads=[in_], writes=[out], dma=True)

    def mm(self, out, pairs, start=True, stop=True, acc=False):
        n = len(pairs)

        def fn(e):
            ins = None
            for i, p in enumerate(pairs):
                ins = e.matmul(out, p[0], p[1], start=(start and i == 0), stop=(stop and i == n - 1))
            return ins
        rd = []
        for p in pairs:
            rd += [p[0], p[1]]
        return self.add('tensor', fn, reads=rd, writes=[out], acc=acc)

    def transpose(self, out, in_, ident):
        return self.add('tensor', lambda e: e.transpose(out, in_, ident), reads=[in_, ident], writes=[out])

    def act(self, out, in_, func, bias=None, scale=None, accum_out=None):
        kw = {}
        rd = [in_]
        wr = [out]
        if bias is not None:
            kw['bias'] = bias
            if not isinstance(bias, (int, float)):
                rd.append(bias)
        if scale is not None:
            kw['scale'] = scale
            if not isinstance(scale, (int, float)):
                rd.append(scale)
        if accum_out is not None:
            kw['accum_out'] = accum_out
            wr.append(accum_out)
        return self.add('scalar', lambda e: e.activation(out, in_, func, **kw), reads=rd, writes=wr)

    def tt(self, out, in0, in1, op, eng='vector'):
        return self.add(eng, lambda e: e.tensor_tensor(out, in0, in1, op), reads=[in0, in1], writes=[out])

    def ts(self, out, in0, s1, s2, op0, op1=None, eng='vector'):
        rd = [in0]
        if not isinstance(s1, (int, float)):
            rd.append(s1)
        if s2 is not None and not isinstance(s2, (int, float)):
            rd.append(s2)
        if op1 is None:
            return self.add(eng, lambda e: e.tensor_scalar(out, in0, s1, None, op0), reads=rd, writes=[out])
        return self.add(eng, lambda e: e.tensor_scalar(out, in0, s1, s2, op0, op1), reads=rd, writes=[out])

    def stt(self, out, in0, scalar, in1, op0, op1, eng='vector'):
        eng = 'vector'
        rd = [in0, in1]
        if not isinstance(scalar, (int, float)):
            rd.append(scalar)
        return self.add(eng, lambda e: e.scalar_tensor_tensor(out, in0, scalar, in1, op0, op1), reads=rd, writes=[out])

    def copy(self, out, in_, eng='vector'):
        if eng == 'scalar':
            return self.add('scalar', lambda e: e.copy(out, in_), reads=[in_], writes=[out])
        return self.add(eng, lambda e: e.tensor_copy(out, in_), reads=[in_], writes=[out])

    def memset(self, ap, val, eng='vector'):
        return self.add(eng, lambda e: e.memset(ap, val), reads=[], writes=[ap])

    def recip(self, out, in_):
        return self.add('vector', lambda e: e.reciprocal(out, in_), reads=[in_], writes=[out])

    def reduce(self, out, in_, op=ALU.add, eng='vector'):
        return self.add(eng, lambda e: e.tensor_reduce(out, in_, AX.X, op), reads=[in_], writes=[out])

    def finish(self):
        self.barrier()

    def emit(self):
        nc = self.nc
        items = self.items

        def run(e, lst):
            for waits, fn, inc in lst:
                for sem, v in waits:
                    e.wait_ge(sem, v)
                if fn is None:
                    continue
                ins = fn(e)
                ins.then_inc(inc[0], inc[1])

        with nc.Block() as block:
            @block.tensor
            def _(e):
                run(e, items['tensor'])

            @block.vector
            def _(e):
                run(e, items['vector'])

            @block.scalar
            def _(e):
                run(e, items['scalar'])

            @block.gpsimd
            def _(e):
                run(e, items['gpsimd'])

            @block.sync
            def _(e):
                run(e, items['sync'])


class Arena:
    def __init__(self, S, t, nbytes):
        self.S = S
        self.t = t
        self.n = nbytes
        self.off = 0
        self.uid = 0
        S.arena_name = t.name

    def alloc(self, shape, dtype):
        es = DSIZE[dtype]
        free = 1
        for s in shape[1:]:
            free *= s
        nb = (free * es + 63) // 64 * 64
        assert self.off + nb <= self.n, "arena overflow: need %d have %d" % (self.off + nb, self.n)
        a = self.off
        self.off += nb
        self.uid += 1
        self.S.arena_allocs.append((a, a + nb, self.uid))
        ap = self.t[0:shape[0], a:a + free * es].bitcast(dtype)
        if len(shape) > 2:
            names = ["d%d" % i for i in range(len(shape) - 1)]
            kw = {names[i]: shape[i + 1] for i in range(len(shape) - 2)}
            ap = ap.rearrange("p (%s) -> p %s" % (" ".join(names), " ".join(names)), **kw)
        return ap

    def mark(self):
        return (self.off, len(self.S.arena_allocs))

    def reset(self, mark):
        self.off = mark[0]
        del self.S.arena_allocs[mark[1]:]


def bcast_mid(ap, n):
    return ap.unsqueeze(1).to_broadcast([ap.shape[0], n, ap.shape[1]])


def build_program(dbg=False, nlayers=DEPTH, stop_after=None):
    nc = bass.Bass("TRN2", target_bir_lowering=False)
    dt_in = lambda name, shape, dt=F32: nc.dram_tensor(name, list(shape), dt, kind="ExternalInput").ap()
    xT_in = dt_in("xT", [D, NT])
    cvec_in = dt_in("cvec", [128, 16])
    cst_in = dt_in("cst", [128, NCST])
    ropeA_in = dt_in("ropeA", [128, 32 * 64])
    ropeR_in = dt_in("ropeR", [128, 32 * 32])
    W = []
    for l in range(DEPTH):
        W.append(dict(
            w_mod=dt_in("w_mod%d" % l, [D, 6 * D]), w_in=dt_in("w_in%d" % l, [D, PIN]),
            w_out=dt_in("w_out%d" % l, [D, D]), w_up=dt_in("w_up%d" % l, [D, 4 * D]),
            w_down=dt_in("w_down%d" % l, [4 * D, D]), pp=dt_in("pp%d" % l, [128, NPP]),
            bc=dt_in("bc%d" % l, [128, NBC]), waug=dt_in("waug%d" % l, [32, 256]),
            wpw=dt_in("wpw%d" % l, [256, 256])))
    outT = nc.dram_tensor("outT", [D, TL // 2], F32, kind="ExternalOutput").ap()
    XST = nc.dram_tensor("XST", [D, NT], F32).ap()
    PT = nc.dram_tensor("PTs", [NT, PTW], BF16).ap()
    YT = nc.dram_tensor("YTs", [D, NT], BF16).ap()
    WOb = nc.dram_tensor("WOb", [D, D], BF16).ap()
    WUb = nc.dram_tensor("WUb", [D, 4 * D], BF16).ap()
    WDb = nc.dram_tensor("WDb", [4 * D, D], BF16).ap()
    WIb = nc.dram_tensor("WIb", [D, PIN], BF16).ap()
    dbg_out = {}
    if dbg:
        dbg_out['PT'] = nc.dram_tensor("dbgPT", [NT, PTW], BF16, kind="ExternalOutput").ap()
        dbg_out['YT'] = nc.dram_tensor("dbgYT", [D, NT], BF16, kind="ExternalOutput").ap()
        dbg_out['X1'] = nc.dram_tensor("dbgX1", [D, NT], F32, kind="ExternalOutput").ap()
        dbg_out['MOD'] = nc.dram_tensor("dbgMOD", [128, 192], F32, kind="ExternalOutput").ap()

    with ExitStack() as st:
        S = Sched(nc, st)
        for nm, ap_ in [("WIb", WIb), ("WOb", WOb), ("WUb", WUb), ("WDb", WDb), ("xT", xT_in), ("XST", XST), ("PTs", PT), ("YTs", YT), ("outT", outT), ("cvec", cvec_in),
                        ("cst", cst_in), ("ropeA", ropeA_in), ("ropeR", ropeR_in)]:
            S.dram_rowlen[nm] = ap_.shape[1]
        for l in range(DEPTH):
            for k, v in W[l].items():
                S.dram_rowlen[v.tensor.name] = v.shape[1]
        for k, v in dbg_out.items():
            S.dram_rowlen[v.tensor.name] = v.shape[1]
        ARENA_BYTES = 206 * 1024
        at = st.enter_context(nc.sbuf_tensor("arena", [128, ARENA_BYTES], U8))
        A = Arena(S, at, ARENA_BYTES)
        PS = [st.enter_context(nc.psum_tensor("ps%d" % i, [128, 512], F32)) for i in range(8)]
        PSB = [p[:].bitcast(BF16) for p in PS]

        cst = A.alloc([128, NCST], F32)
        S.dma(cst, cst_in)
        identb = A.alloc([128, 128], BF16)
        onesb = A.alloc([128, 128], BF16)
        onesf = A.alloc([128, 128], F32)
        maskb16 = A.alloc([128, 2, 128], BF16)
        MOD = A.alloc([128, DEPTH, 6, 8, 2], F32)
        pp = A.alloc([128, NPP], F32)
        S.copy(identb, cst[:, 0:128])
        S.memset(onesb, 1.0 / 1024)
        S.memset(onesf, 1.0)
        S.copy(maskb16[:, 0, :], cst[:, 640:768])
        S.copy(maskb16[:, 1, :], cst[:, 768:896])
        TRI = [cst[:, 128:256], cst[:, 256:384]]
        TRIR = [cst[:, 384:512], cst[:, 512:640]]
        BDm = cst[:, 896:1152]
        HM = cst[:, 1152:1156]
        base_mark = A.mark()

        class _Stop(Exception):
            pass

        def load_cast(dst, src, stg, width, engs=('vector', 'gpsimd')):
            n = dst.shape[1]
            i = 0
            c0 = 0
            while c0 < n:
                w = min(width, n - c0)
                sg = stg[i % len(stg)]
                S.dma(sg[:, 0:w], src[:, c0:c0 + w], q=('sync' if i % 2 == 0 else 'gpsimd'))
                S.copy(dst[:, c0:c0 + w], sg[:, 0:w], eng=engs[i % len(engs)])
                c0 += w
                i += 1

        def phase_mod():
            mk = A.mark()
            cs = A.alloc([128, 8, 2], F32)
            sc = A.alloc([128, 8, 2], F32)
            S.dma(cs, cvec_in.rearrange("p (c m) -> p c m", m=2))
            S.act(sc, cs, AF.Silu)
            wst = [A.alloc([128, 6144], F32) for _ in range(2)]
            acc = A.alloc([128, 48, 2], F32)
            ppl = A.alloc([128, NPP], F32)
            for l in range(nlayers):
                S.dma(ppl, W[l]['pp'])
                for k in range(8):
                    wk = wst[k % 2]
                    S.dma(wk, W[l]['w_mod'][k * 128:(k + 1) * 128, :], q=('sync' if k % 2 == 0 else 'gpsimd'))
                    pm = PS[k % 2][:, 0:96].rearrange("p (j m) -> p j m", m=2)
                    for j in range(48):
                        S.mm(pm[:, j, :], [(wk[:, j * 128:(j + 1) * 128], sc[:, k, :])])
                    if k == 0:
                        S.copy(acc, pm)
                    else:
                        S.tt(acc, acc, pm, ALU.add)
                bm = ppl[:, 16:64].unsqueeze(2).to_broadcast([128, 48, 2])
                S.tt(acc, acc, bm, ALU.add)
                a4 = acc.rearrange("p (w c) m -> p w c m", w=6)
                g1n = ppl[:, 0:8].unsqueeze(2).to_broadcast([128, 8, 2])
                g2n = ppl[:, 8:16].unsqueeze(2).to_broadcast([128, 8, 2])
                S.stt(MOD[:, l, 0], a4[:, 1], 1.0, g1n, ALU.add, ALU.mult)
                S.copy(MOD[:, l, 1], a4[:, 0])
                S.copy(MOD[:, l, 2], a4[:, 2])
                S.stt(MOD[:, l, 3], a4[:, 4], 1.0, g2n, ALU.add, ALU.mult)
                S.copy(MOD[:, l, 4], a4[:, 3])
                S.copy(MOD[:, l, 5], a4[:, 5])
            if dbg:
                S.dma(dbg_out['MOD'], MOD.rearrange("p l w c m -> p (l w c m)"), q='gpsimd')
            S.barrier()
            A.reset(mk)

        def norm_mod(xg, n, sq, tmp, hT, Acol, Bcol, pbank):
            S.act(sq[:, :, 0:n], xg[:, :, 0:n], AF.Square)
            ps = PS[pbank][:, 0:n]
            S.mm(ps, [(onesb, sq[:, c, 0:n]) for c in range(8)])
            rstd = tmp[0][:, 0:n]
            S.act(rstd, ps, AF.Sqrt, bias=EPS)
            S.recip(rstd, rstd)
            for c in range(8):
                t = tmp[1 + c % 2][:, 0:n]
                S.tt(t, xg[:, c, 0:n], rstd, ALU.mult, eng=('vector' if c % 2 == 0 else 'gpsimd'))
                S.act(hT[:, c, 0:n], t, AF.Identity, bias=Bcol[:, c:c + 1], scale=Acol[:, c:c + 1])

        def phase_A(l, Xsrc, zT, uTl, uTc):
            mk = A.mark()
            WIN = A.alloc([128, 8, PIN], BF16)
            if l == 0:
                stg = [A.alloc([128, PIN], F32) for _ in range(2)]
                for k in range(8):
                    S.dma(stg[k % 2], W[l]['w_in'][k * 128:(k + 1) * 128, :], q=('sync' if k % 2 == 0 else 'gpsimd'))
                    S.copy(WIN[:, k, :], stg[k % 2], eng=('vector' if k % 2 == 0 else 'gpsimd'))
            else:
                for k2 in range(4):
                    S.dma(WIN[:, 2 * k2:2 * k2 + 2, :], WIb[k2 * 256:(k2 + 1) * 256, :].rearrange("(k p) n -> p k n", p=128),
                          q=('sync' if k2 % 2 == 0 else 'gpsimd'))
            xgs = [A.alloc([128, 8, 512], F32) for _ in range(2)]
            sq = A.alloc([128, 8, 512], BF16)
            tmp = [A.alloc([128, 512], F32) for _ in range(3)]
            hTs = [A.alloc([128, 8, 512], BF16) for _ in range(2)]
            OTs = [A.alloc([128, PTW], BF16) for _ in range(2)]
            sig = [A.alloc([128, 512], F32) for _ in range(2)]
            Xv = Xsrc.rearrange("(c p) t -> p c t", p=128)
            colblocks = [(512, 1024), (1024, 1536), (1536, 2048), (2048, 2560), (2560, 2576)]
            ti = 0

            def prep_group(g):
                n = 512 if g < 8 else 256
                m = 0 if g < 8 else 1
                S.dma(xgs[g % 2][:, :, 0:n], Xv[:, :, g * 512:g * 512 + n])
                norm_mod(xgs[g % 2], n, sq, tmp, hTs[g % 2], MOD[:, l, 0, :, m], MOD[:, l, 1, :, m], 0)
            prep_group(0)
            for g in range(9):
                n = 512 if g < 8 else 256
                t0 = g * 512
                m = 0 if g < 8 else 1
                hT = hTs[g % 2]
                if g + 1 < 9:
                    prep_group(g + 1)
                for tt_ in range(n // 128):
                    OT = OTs[ti % 2]
                    for bi, (c0, c1) in enumerate(colblocks):
                        w = c1 - c0
                        ps = PS[1 + (bi % 2)][:, 0:w]
                        S.mm(ps, [(hT[:, c, tt_ * 128:(tt_ + 1) * 128], WIN[:, c, c0:c1]) for c in range(8)])
                        if bi % 2 == 0:
                            S.copy(OT[:, c0 - 512:c1 - 512], ps, eng='scalar')
                        else:
                            S.copy(OT[:, c0 - 512:c1 - 512], ps, eng='vector')
                    tok = t0 + tt_ * 128
                    S.dma(PT[tok:tok + 128, :], OT, q='gpsimd')
                    ti += 1
                for cc in range(2):
                    pa = PS[3][:, 0:n]
                    pg = PS[4][:, 0:n]
                    S.mm(pa, [(WIN[:, c, cc * 128:(cc + 1) * 128], hT[:, c, 0:n]) for c in range(8)])
                    S.mm(pg, [(WIN[:, c, 256 + cc * 128:256 + (cc + 1) * 128], hT[:, c, 0:n]) for c in range(8)])
                    sg = sig[cc][:, 0:n]
                    S.act(sg, pg, AF.Sigmoid)
                    if g < 8:
                        dst = uTl[:, cc, 15 + t0:15 + t0 + n]
                    else:
                        dst = uTc[:, cc, 15:15 + n]
                    S.tt(dst, pa, sg, ALU.mult)
                pz = PS[5][0:16, 0:n]
                S.mm(pz, [(WIN[:, c, 1280:1296], hT[:, c, 0:n]) for c in range(8)])
                S.copy(zT[0:16, t0:t0 + n], pz)
            S.barrier()
            A.reset(mk)

        def phase_conv(l, uTl, uTc, need_ctx, nblk):
            mk = A.mark()
            DG = A.alloc([128, 2, 31, 128], BF16)
            WPW = A.alloc([128, 2, 256], BF16)
            wst = A.alloc([128, 2, 256], F32)
            S.dma(wst, W[l]['wpw'].rearrange("(c p) n -> p c n", p=128))
            S.copy(WPW, wst)
            for cc in range(2):
                for j in range(31):
                    S.ts(DG[:, cc, j, :], identb, pp[:, 72 + cc * 31 + j:73 + cc * 31 + j], None, ALU.mult,
                         eng=('vector' if j % 2 == 0 else 'gpsimd'))
            y = [A.alloc([128, 512], F32) for _ in range(2)]
            ysq = [A.alloc([128, 512], F32) for _ in range(2)]
            msb = A.alloc([128, 512], F32)
            m2 = A.alloc([128, 512], F32)
            rstd = A.alloc([128, 512], F32)
            t1 = [A.alloc([128, 512], F32) for _ in range(2)]
            sb = [A.alloc([128, 512], BF16) for _ in range(2)]
            yo = [A.alloc([128, 2, 512], BF16) for _ in range(2)]
            blocks = [(uTl, b * 512, 512, b * 512) for b in range(nblk)]
            if need_ctx:
                blocks.append((uTc, 0, 256, TL))
            for bi, (uT, t0, n, tok0) in enumerate(blocks):
                for cc in range(2):
                    pc = PS[cc][:, 0:n]
                    S.mm(pc, [(DG[:, cc, j, :], uT[:, cc, t0 + j:t0 + j + n]) for j in range(31)])
                    S.act(y[cc][:, 0:n], pc, AF.Identity, bias=pp[:, 64 + cc:65 + cc])
                    S.act(ysq[cc][:, 0:n], pc, AF.Square, bias=pp[:, 64 + cc:65 + cc])
                pm = PS[2][:, 0:n]
                pq = PS[3][:, 0:n]
                S.mm(pm, [(onesf, y[0][:, 0:n]), (onesf, y[1][:, 0:n])])
                S.mm(pq, [(onesf, ysq[0][:, 0:n]), (onesf, ysq[1][:, 0:n])])
                S.act(msb[:, 0:n], pm, AF.Identity, scale=1.0 / 256)
                S.act(m2[:, 0:n], pm, AF.Square, scale=1.0 / 256)
                S.stt(rstd[:, 0:n], pq, 1.0 / 256, m2[:, 0:n], ALU.mult, ALU.subtract)
                S.act(rstd[:, 0:n], rstd[:, 0:n], AF.Sqrt, bias=EPS)
                S.recip(rstd[:, 0:n], rstd[:, 0:n])
                for cc in range(2):
                    e_ = 'vector' if cc == 0 else 'gpsimd'
                    S.tt(t1[cc][:, 0:n], y[cc][:, 0:n], msb[:, 0:n], ALU.subtract, eng=e_)
                    S.tt(t1[cc][:, 0:n], t1[cc][:, 0:n], rstd[:, 0:n], ALU.mult, eng=e_)
                    S.act(sb[cc][:, 0:n], t1[cc][:, 0:n], AF.Silu, bias=pp[:, 68 + cc:69 + cc], scale=pp[:, 66 + cc:67 + cc])
                yob = yo[bi % 2]
                for co in range(2):
                    ppw = PS[4 + co][:, 0:n]
                    S.mm(ppw, [(WPW[:, ci, co * 128:(co + 1) * 128], sb[ci][:, 0:n]) for ci in range(2)])
                    S.act(yob[:, co, 0:n], ppw, AF.Identity, bias=pp[:, 70 + co:71 + co])
                S.dma(YT[0:256, tok0:tok0 + n].rearrange("(c p) t -> p c t", p=128), yob[:, :, 0:n], q='gpsimd')
            S.barrier()
            A.reset(mk)

        def phase_att(l, need_ctx, nqb):
            mk = A.mark()
            QK = A.alloc([128, 4, NT], BF16)
            VA = A.alloc([128, NTILE, 2, 128], BF16)
            QZ = A.alloc([128, 4, NT], BF16)
            VB = A.alloc([128, NTILE, 2, 128], BF16)
            rope = A.alloc([128, 32, 64], F32)
            bc = A.alloc([128, 384], F32)
            S.dma(rope, ropeA_in.rearrange("p (t k) -> p t k", k=64))
            S.dma(bc, W[l]['bc'][:, 512:896])
            S.memset(VA, 0.0)
            S.memset(VA[:, :, :, 64:65], 1.0)
            S.memset(QZ, 0.0, eng='gpsimd')
            S.memset(VB, 0.0, eng='gpsimd')
            S.memset(VB[:, :, :, 0:1], 1.0, eng='gpsimd')
            NB = 2
            raw = [A.alloc([128, NB, 512], BF16) for _ in range(2)]
            f1 = A.alloc([128, NB, 6, 64], F32)
            f2 = A.alloc([128, NB, 6, 64], F32)
            ssq = A.alloc([128, NB, 6], F32)
            qr = [A.alloc([128, NB, 8, 64], BF16) for _ in range(2)]
            tA = A.alloc([128, NB, 6, 32], F32)
            tB = A.alloc([128, NB, 6, 32], F32)
            for bi in range(NTILE // NB):
                tl0 = bi * NB
                rw = raw[bi % 2]
                qb_ = qr[bi % 2]
                S.dma(rw, PT[tl0 * 128:(tl0 + NB) * 128, 784:1296].rearrange("(t p) c -> p t c", p=128))
                qk = rw[:, :, 0:384].rearrange("p t (h e) -> p t h e", e=64)
                S.tt(f1, qk, qk, ALU.mult)
                S.reduce(ssq, f1)
                S.act(ssq, ssq, AF.Sqrt, bias=EPS, scale=1.0 / 64)
                S.recip(ssq, ssq)
                S.tt(f1, qk, ssq.unsqueeze(3).to_broadcast([128, NB, 6, 64]), ALU.mult)
                gq = bc.rearrange("p (h e) -> p h e", e=64).unsqueeze(1).to_broadcast([128, NB, 6, 64])
                S.tt(f2, f1, gq, ALU.mult, eng='gpsimd')
                is_lat = tl0 < 32
                if is_lat:
                    cosv = rope[:, tl0:tl0 + NB, 0:32].unsqueeze(2).to_broadcast([128, NB, 6, 32])
                    sinv = rope[:, tl0:tl0 + NB, 32:64].unsqueeze(2).to_broadcast([128, NB, 6, 32])
                    x1 = f2[:, :, :, 0:32]
                    x2 = f2[:, :, :, 32:64]
                    o1 = f1[:, :, :, 0:32]
                    o2 = f1[:, :, :, 32:64]
                    S.tt(tA, x1, cosv, ALU.mult)
                    S.tt(tB, x2, sinv, ALU.mult, eng='gpsimd')
                    S.tt(o1, tA, tB, ALU.subtract)
                    S.tt(tA, x2, cosv, ALU.mult)
                    S.tt(tB, x1, sinv, ALU.mult, eng='gpsimd')
                    S.tt(o2, tA, tB, ALU.add)
                    src = f1
                else:
                    src = f2
                S.copy(qb_[:, :, 0:4, :], src[:, :, 0:4, :])
                kdst = qb_[:, :, 4:8, :].rearrange("p t (k r) e -> p t k r e", r=2)
                for r_ in range(2):
                    S.copy(kdst[:, :, :, r_, :], src[:, :, 4:6, :], eng='gpsimd')
                for t_ in range(NB):
                    tl = tl0 + t_
                    pt = PSB[6][:, 0:512].rearrange("p (j k) -> p j k", k=128)
                    for j in range(4):
                        S.transpose(pt[:, j, :], qb_[:, t_, 2 * j:2 * j + 2, :].rearrange("p a e -> p (a e)"), identb)
                    S.copy(QK[:, :, tl * 128:(tl + 1) * 128], pt, eng=('vector' if t_ % 2 == 0 else 'scalar'))
                    for j in range(2):
                        S.copy(QZ[0:64, 2 * j, tl * 128:(tl + 1) * 128], pt[0:64, j, :], eng='vector')
                        S.copy(QZ[64:128, 2 * j + 1, tl * 128:(tl + 1) * 128], pt[64:128, j, :], eng='scalar')
                    vv = rw[:, t_, 384:512].rearrange("p (k e) -> p k e", e=64)
                    S.copy(VA[:, tl, :, 0:64], vv, eng='gpsimd')
                    S.copy(VB[:, tl, :, 64:128], vv, eng='gpsimd')
            NS, LA, NP = 4, 3, 5
            Pt = [A.alloc([128, 512], BF16) for _ in range(NP)]
            rsb = A.alloc([128, 512], F32)
            bcs = A.alloc([128, 512], F32)
            Yo = [A.alloc([128, 512], BF16) for _ in range(2)]
            qblocks = [(qb * 512, 512, list(range(NTILE))) for qb in range(nqb)]
            if need_ctx:
                qblocks.append((TL, 256, [32, 33]))
            it = 0
            gi = 0
            cvs = [A.alloc([128, 2048], F32) for _ in range(2)]
            cvb = [A.alloc([128, 2048], BF16) for _ in range(2)]
            pieces = []
            for k in range(8):
                pieces.append((W[l]['w_out'][k * 128:(k + 1) * 128, :], WOb[k * 128:(k + 1) * 128, :], 1024))
            for k in range(8):
                for c_ in range(2):
                    pieces.append((W[l]['w_up'][k * 128:(k + 1) * 128, c_ * 2048:(c_ + 1) * 2048],
                                   WUb[k * 128:(k + 1) * 128, c_ * 2048:(c_ + 1) * 2048], 2048))
            for k in range(32):
                pieces.append((W[l]['w_down'][k * 128:(k + 1) * 128, :], WDb[k * 128:(k + 1) * 128, :], 1024))
            if l + 1 < nlayers:
                for k in range(8):
                    pieces.append((W[l + 1]['w_in'][k * 128:(k + 1) * 128, 0:2048], WIb[k * 128:(k + 1) * 128, 0:2048], 2048))
                    pieces.append((W[l + 1]['w_in'][k * 128:(k + 1) * 128, 2048:PIN], WIb[k * 128:(k + 1) * 128, 2048:PIN], PIN - 2048))
            n_iters = 4 * len(qblocks)
            ppi = (len(pieces) + n_iters - 1) // n_iters
            pci = [0]

            def convert_some():
                for _ in range(ppi):
                    if pci[0] >= len(pieces):
                        return
                    src, dst, w = pieces[pci[0]]
                    sg = cvs[pci[0] % 2]
                    cb = cvb[pci[0] % 2]
                    S.dma(sg[:, 0:w], src, q='sync')
                    S.copy(cb[:, 0:w], sg[:, 0:w], eng='gpsimd')
                    S.dma(dst, cb[:, 0:w], q='gpsimd')
                    pci[0] += 1
            for h in range(4):
                pair, half = h // 2, h % 2
                kvh = pair
                p0 = 64 * half
                for (q0, nq, kts) in qblocks:
                    convert_some()
                    O = PS[4 + it % 2]
                    rhsq = QZ[:, h, q0:q0 + nq]
                    nk = len(kts)
                    sps = {}

                    def smm(i):
                        kt = kts[i]
                        sp = PS[(gi + i) % NS][:, 0:nq]
                        S.mm(sp, [(QK[:, 2 + kvh, kt * 128:(kt + 1) * 128], rhsq)])
                        sps[i] = sp
                    for i in range(min(LA, nk)):
                        smm(i)
                    for i in range(nk):
                        if i + LA < nk:
                            smm(i + LA)
                        sp = sps.pop(i)
                        kt = kts[i]
                        pt_ = Pt[(gi + i) % NP][:, 0:nq]
                        S.act(pt_, sp, AF.Exp, scale=0.125)
                        if KEEPWARM:
                            S.add('tensor', lambda e: e.matmul(PS[7][:, 0:KEEPWARM], identb, QK[:, 0, 0:KEEPWARM],
                                                               start=True, stop=True), reads=[], writes=[])
                        if half == 0:
                            S.mm(O[:, 0:nq], [(VA[:, kt, kvh, :], pt_)], start=(i == 0), stop=(i == nk - 1), acc=(i > 0))
                        else:
                            S.mm(O[:, 0:nq], [(VB[:, kt, kvh, :], pt_)], start=(i == 0), stop=(i == nk - 1), acc=(i > 0))
                    gi += nk
                    yo_ = Yo[it % 2]
                    pb = PS[6]
                    if half == 0:
                        S.recip(rsb[64:65, 0:nq], O[64:65, 0:nq])
                        S.mm(pb[0:64, 0:nq], [(onesf[64:65, 0:64], rsb[64:65, 0:nq])])
                        S.copy(bcs[0:64, 0:nq], pb[0:64, 0:nq], eng='vector')
                        S.tt(yo_[0:64, 0:nq], O[0:64, 0:nq], bcs[0:64, 0:nq], ALU.mult)
                        S.dma(YT[512 + h * 64:512 + (h + 1) * 64, q0:q0 + nq], yo_[0:64, 0:nq], q='gpsimd')
                    else:
                        S.recip(rsb[0:1, 0:nq], O[0:1, 0:nq])
                        S.mm(pb[:, 0:nq], [(onesf[0:1, :], rsb[0:1, 0:nq])])
                        S.copy(bcs[64:128, 0:nq], pb[64:128, 0:nq], eng='vector')
                        S.tt(yo_[64:128, 0:nq], O[64:128, 0:nq], bcs[64:128, 0:nq], ALU.mult)
                        S.dma(YT[512 + h * 64:512 + (h + 1) * 64, q0:q0 + nq], yo_[64:128, 0:nq], q='gpsimd')
                    it += 1
            S.barrier()
            A.reset(mk)

        def phase_scan(l, kind, zT, need_ctx, nlt):
            mk = A.mark()
            is_gla = (kind == 'gla')
            pc0 = 0 if is_gla else 1296
            yrow = 256 if is_gla else 768
            qk_all = A.alloc([128, NTILE, 256], BF16)
            vg_all = A.alloc([128, NTILE, 512], BF16)
            QKT = A.alloc([128, 2, NT], BF16)
            oacc = A.alloc([128, NTILE, 256], F32)
            PTv = PT.rearrange("(t p) c -> p t c", p=128)
            for i_ in range(17):
                sl = slice(2 * i_, 2 * i_ + 2)
                S.dma(qk_all[:, sl, :], PTv[:, sl, pc0:pc0 + 256], q=('sync' if i_ % 2 == 0 else 'gpsimd'))
                S.dma(vg_all[:, sl, :], PTv[:, sl, pc0 + 256:pc0 + 768], q=('gpsimd' if i_ % 2 == 0 else 'sync'))
            gn = A.alloc([128, 256], F32)
            S.dma(gn, W[l]['bc'][:, (0 if is_gla else 256):(256 if is_gla else 512)])
            mk2 = A.mark()
            if not is_gla:
                rope = A.alloc([128, 32, 32], F32)
                S.dma(rope, ropeR_in.rearrange("p (t k) -> p t k", k=32))
                NB = 4
                tA = A.alloc([128, NB, 8, 16], F32)
                tB = A.alloc([128, NB, 8, 16], F32)
                tC = A.alloc([128, NB, 8, 16], F32)
                tD = A.alloc([128, NB, 8, 16], F32)
                for bi in range(32 // NB):
                    tl0 = bi * NB
                    v4 = qk_all[:, tl0:tl0 + NB, :].rearrange("p t (h e) -> p t h e", e=32)
                    x1 = v4[:, :, :, 0:16]
                    x2 = v4[:, :, :, 16:32]
                    cosv = rope[:, tl0:tl0 + NB, 0:16].unsqueeze(2).to_broadcast([128, NB, 8, 16])
                    sinv = rope[:, tl0:tl0 + NB, 16:32].unsqueeze(2).to_broadcast([128, NB, 8, 16])
                    S.tt(tA, x1, cosv, ALU.mult)
                    S.tt(tB, x2, sinv, ALU.mult, eng='gpsimd')
                    S.tt(tC, x2, cosv, ALU.mult)
                    S.tt(tD, x1, sinv, ALU.mult, eng='gpsimd')
                    S.tt(x1, tA, tB, ALU.subtract)
                    S.tt(x2, tC, tD, ALU.add, eng='gpsimd')
            for tl in range(NTILE):
                pt = PSB[6][:, 0:256].rearrange("p (j k) -> p j k", k=128)
                for j in range(2):
                    S.transpose(pt[:, j, :], qk_all[:, tl, j * 128:(j + 1) * 128], identb)
                S.copy(QKT[:, :, tl * 128:(tl + 1) * 128], pt, eng=('vector' if tl % 2 == 0 else 'scalar'))
            S.barrier()
            A.reset(mk2)
            if stop_after == kind + '_prep':
                raise _Stop()
            Sf = A.alloc([128, 256], F32)
            Sb = A.alloc([128, 256], BF16)
            kiT = [A.alloc([128, 4, 128], BF16) for _ in range(2)]
            kvm = A.alloc([128, 256], F32)
            if is_gla:
                WA = A.alloc([32, 256], BF16)
                was = A.alloc([32, 256], F32)
                S.dma(was, W[l]['waug'])
                S.copy(WA, was)
                ee = [A.alloc([128, 128], F32) for _ in range(2)]
                ll = [A.alloc([128, 128], F32) for _ in range(2)]
                E1s = [A.alloc([128, 128], F32) for _ in range(2)]
                E2s = [A.alloc([128, 128], F32) for _ in range(2)]
                E3s = [A.alloc([128, 128], F32) for _ in range(2)]
            LNS = math.log(32 ** -0.5)
            NQ = 6
            qdT = [A.alloc([128, 128], BF16) for _ in range(NQ)]
            kend = [A.alloc([128, 128], BF16) for _ in range(NQ)]
            scm = [A.alloc([128, 4, 128], BF16) for _ in range(3)]
            if is_gla:
                dcs = [A.alloc([128, 1], F32) for _ in range(NQ)]
            for d_ in range(2):
                S.memset(Sf, 0.0)
                S.memset(Sb, 0.0)
                order = [32, 33] + list(range(32)) if d_ == 0 else [33, 32] + list(range(31, -1, -1))
                if stop_after and stop_after.startswith(kind + '_n'):
                    order = order[:int(stop_after[len(kind) + 2:])]
                n_st = len(order)
                want = [need_ctx or ch < nlt for ch in order]
                tks = [slice(ch * 128, (ch + 1) * 128) for ch in order]
                if not is_gla:
                    o_ = 1156 + d_ * 384
                    cE1 = cst[:, o_:o_ + 128]
                    cE2 = cst[:, o_ + 128:o_ + 256]
                    cE3 = cst[:, o_ + 256:o_ + 384]
                    cdc = cst[:, 1924:1925]

                def stA(j):
                    if is_gla:
                        S.mm(PS[0][:, 0:128], [(zT[0:32, tks[j]], WA[0:32, d_ * 128:(d_ + 1) * 128])])

                def stB(j):
                    if is_gla:
                        S.act(ee[j % 2], PS[0][:, 0:128], AF.Exp, scale=-1.0)
                        S.act(ll[j % 2], ee[j % 2], AF.Ln, bias=1.0)

                def stC(j):
                    if is_gla:
                        S.mm(PS[1][:, 0:128], [(ll[j % 2], TRI[d_])])
                        S.mm(PS[1][:, 128:256], [(TRIR[d_], ll[j % 2])])

                def stD(j):
                    if is_gla:
                        pbT = PS[1][:, 0:128]
                        S.act(E1s[j % 2], pbT, AF.Exp, bias=LNS)
                        S.act(E2s[j % 2], pbT, AF.Exp, scale=-1.0)
                        S.act(E3s[j % 2], PS[1][:, 128:256], AF.Exp)
                        col = 127 if d_ == 0 else 0
                        S.act(dcs[j % NQ], pbT[:, col:col + 1], AF.Exp)

                def stE(j):
                    E1, E2, E3 = (E1s[j % 2], E2s[j % 2], E3s[j % 2]) if is_gla else (cE1, cE2, cE3)
                    S.tt(qdT[j % NQ], QKT[:, 0, tks[j]], E1, ALU.mult, eng='gpsimd')
                    if want[j]:
                        for hh in range(4):
                            S.stt(kiT[j % 2][:, hh, :], QKT[:, 1, tks[j]], HM[:, hh:hh + 1], E2, ALU.mult, ALU.mult)
                    S.tt(kend[j % NQ], qk_all[:, order[j], 128:256], E3, ALU.mult, eng='gpsimd')

                def stF(j):
                    if want[j]:
                        psc = PS[2 + j % 2][:, :].rearrange("p (h c) -> p h c", c=128)
                        for hh in range(4):
                            S.mm(psc[:, hh, :], [(kiT[j % 2][:, hh, :], qdT[j % NQ])])

                def stG(j):
                    if want[j]:
                        psc = PS[2 + j % 2][:, :].rearrange("p (h c) -> p h c", c=128)
                        S.tt(scm[j % 3], psc, bcast_mid(maskb16[:, d_, :], 4), ALU.mult)

                def stH(j):
                    ch = order[j]
                    qd = qdT[j % NQ]
                    ke = kend[j % NQ]
                    sm = scm[j % 3]
                    dc = dcs[j % NQ] if is_gla else cdc
                    vch = vg_all[:, ch, 0:256]
                    if want[j]:
                        po = PS[4 + j % 2][:, 0:256]
                        S.mm(po, [(qd, Sb)], start=True, stop=False)
                        for hh in range(4):
                            S.mm(po[:, hh * 64:(hh + 1) * 64], [(sm[:, hh, :], vch[:, hh * 64:(hh + 1) * 64])],
                                 start=False, stop=(hh == 3), acc=True)
                        if d_ == 0:
                            S.copy(oacc[:, ch, :], po, eng='scalar')
                        else:
                            S.tt(oacc[:, ch, :], oacc[:, ch, :], po, ALU.add)
                    pkv = PS[6 + j % 2][:, 0:256]
                    S.mm(pkv, [(ke, vch)])
                    S.tt(kvm, pkv, BDm, ALU.mult)
                    S.stt(Sb, Sf, dc, kvm, ALU.mult, ALU.add)
                    S.stt(Sf, Sf, dc, kvm, ALU.mult, ALU.add)

                stages = [(stH, 0), (stG, 1), (stF, 2), (stE, 3), (stD, 4), (stC, 5), (stB, 6), (stA, 7)]
                for t in range(-7, n_st):
                    for fn_, lead in stages:
                        j = t + lead
                        if 0 <= j < n_st:
                            fn_(j)
            if stop_after and stop_after.startswith(kind + '_n'):
                raise _Stop()
            NB = 2
            f1 = A.alloc([128, NB, 4, 64], F32)
            ssq = A.alloc([128, NB, 4], F32)
            sg = A.alloc([128, NB, 256], F32)
            yb = [A.alloc([128, NB, 256], BF16) for _ in range(2)]
            yts = [A.alloc([128, 2, NB * 128], BF16) for _ in range(2)]
            ntl = NTILE if need_ctx else nlt
            for bi in range(ntl // NB):
                tl0 = bi * NB
                o4 = oacc[:, tl0:tl0 + NB, :].rearrange("p t (h e) -> p t h e", e=64)
                S.tt(f1, o4, o4, ALU.mult)
                S.reduce(ssq, f1)
                S.act(ssq, ssq, AF.Sqrt, bias=EPS, scale=1.0 / 64)
                S.recip(ssq, ssq)
                S.tt(f1, o4, ssq.unsqueeze(3).to_broadcast([128, NB, 4, 64]), ALU.mult)
                f1f = f1.rearrange("p t h e -> p t (h e)")
                S.tt(f1f, f1f, bcast_mid(gn, NB), ALU.mult, eng='gpsimd')
                S.act(sg, vg_all[:, tl0:tl0 + NB, 256:512], AF.Silu)
                y_ = yb[bi % 2]
                S.tt(y_, f1f, sg, ALU.mult)
                yt_ = yts[bi % 2]
                pt = PSB[6][:, 0:2 * NB * 128].rearrange("p (j k) -> p j k", j=2)
                for t_ in range(NB):
                    for j in range(2):
                        S.transpose(pt[:, j, t_ * 128:(t_ + 1) * 128], y_[:, t_, j * 128:(j + 1) * 128], identb)
                S.copy(yt_, pt, eng='scalar')
                S.dma(YT[yrow:yrow + 256, tl0 * 128:(tl0 + NB) * 128].rearrange("(c p) t -> p c t", p=128), yt_, q='gpsimd')
            S.barrier()
            A.reset(mk)

        def phase_C(l, Xsrc, last, need_ctx, nlg):
            mk = A.mark()
            WOUT = A.alloc([128, 8, D], BF16)
            WUP = A.alloc([128, 8, 4 * D], BF16)
            WDN = A.alloc([128, 32, D], BF16)
            hid = A.alloc([128, 32, 256], BF16)
            S.dma(WOUT, WOb.rearrange("(k p) n -> p k n", p=128))
            for k2 in range(4):
                S.dma(WUP[:, 2 * k2:2 * k2 + 2, :], WUb[k2 * 256:(k2 + 1) * 256, :].rearrange("(k p) n -> p k n", p=128),
                      q=('gpsimd' if k2 % 2 == 0 else 'sync'))
            for k8 in range(4):
                S.dma(WDN[:, 8 * k8:8 * k8 + 8, :], WDb[k8 * 1024:(k8 + 1) * 1024, :].rearrange("(k p) n -> p k n", p=128),
                      q=('sync' if k8 % 2 == 0 else 'gpsimd'))
            xgs = [A.alloc([128, 8, 256], F32) for _ in range(2)]
            Yg = A.alloc([128, 8, 256], BF16)
            sq = A.alloc([128, 8, 256], BF16)
            tmp = [A.alloc([128, 256], F32) for _ in range(3)]
            h2T = A.alloc([128, 8, 256], BF16)
            rr = [tmp[1], tmp[2]] + [A.alloc([128, 256], F32) for _ in range(2)]
            Xv = Xsrc.rearrange("(c p) t -> p c t", p=128)
            XSv = XST.rearrange("(c p) t -> p c t", p=128)
            OUv = outT.rearrange("(c p) t -> p c t", p=128)
            YTv = YT.rearrange("(c p) t -> p c t", p=128)
            ngroups = 17 if need_ctx else nlg
            n = 256
            S.dma(xgs[0], Xv[:, :, 0:n])

            def down_part(gp, dc):
                xp = xgs[gp % 2]
                mp = 0 if gp < 16 else 1
                pd = PS[1 + dc % 2][:, 0:n]
                S.mm(pd, [(WDN[:, fc, dc * 128:(dc + 1) * 128], hid[:, fc, :]) for fc in range(32)])
                S.stt(xp[:, dc, :], pd, MOD[:, l, 5, :, mp][:, dc:dc + 1], xp[:, dc, :], ALU.mult, ALU.add)

            def finish_group(gp):
                xp = xgs[gp % 2]
                tp = gp * 256
                if not last:
                    S.dma(XSv[:, :, tp:tp + n], xp, q='gpsimd')
                    if dbg:
                        S.dma(dbg_out['X1'].rearrange("(c p) t -> p c t", p=128)[:, :, tp:tp + n], xp, q='gpsimd')
                else:
                    S.act(sq, xp, AF.Square)
                    ps = PS[0][:, 0:n]
                    S.mm(ps, [(onesb, sq[:, c, :]) for c in range(8)])
                    rstd = tmp[0]
                    S.act(rstd, ps, AF.Sqrt, bias=EPS)
                    S.recip(rstd, rstd)
                    for c in range(8):
                        S.stt(xp[:, c, :], xp[:, c, :], pp[:, 134 + c:135 + c], rstd, ALU.mult, ALU.mult)
                    S.dma(OUv[:, :, tp:tp + n], xp, q='gpsimd')

            for g in range(ngroups + 1):
                cur = g if g < ngroups else None
                prev = g - 1 if g > 0 else None
                if cur is not None:
                    t0 = g * 256
                    m = 0 if g < 16 else 1
                    xg = xgs[g % 2]
                    S.dma(Yg, YTv[:, :, t0:t0 + n], q='sync')
                    G1 = MOD[:, l, 2, :, m]
                    for dc in range(8):
                        po = PS[1 + dc % 2][:, 0:n]
                        S.mm(po, [(WOUT[:, kc, dc * 128:(dc + 1) * 128], Yg[:, kc, :]) for kc in range(8)])
                        S.stt(xg[:, dc, :], po, G1[:, dc:dc + 1], xg[:, dc, :], ALU.mult, ALU.add)
                    S.act(sq, xg, AF.Square)
                if prev is not None:
                    down_part(prev, 0)
                if cur is not None:
                    pss = PS[0][:, 0:n]
                    S.mm(pss, [(onesb, sq[:, c, :]) for c in range(8)])
                if prev is not None:
                    down_part(prev, 1)
                if cur is not None:
                    rstd = tmp[0]
                    S.act(rstd, pss, AF.Sqrt, bias=EPS)
                    S.recip(rstd, rstd)
                    Acol = MOD[:, l, 3, :, m]
                    Bcol = MOD[:, l, 4, :, m]
                for c in range(8):
                    if cur is not None:
                        t_ = tmp[1 + c % 2]
                        S.tt(t_, xg[:, c, :], rstd, ALU.mult, eng=('vector' if c % 2 == 0 else 'gpsimd'))
                        S.act(h2T[:, c, :], t_, AF.Identity, bias=Bcol[:, c:c + 1], scale=Acol[:, c:c + 1])
                    if prev is not None and c < 6:
                        down_part(prev, 2 + c)
                if prev is not None:
                    finish_group(prev)
                if cur is not None:
                    if g + 1 < ngroups:
                        S.dma(xgs[(g + 1) % 2], Xv[:, :, t0 + n:t0 + 2 * n])
                    for fc in range(32):
                        pu = PS[3 + fc % 4][:, 0:n]
                        S.mm(pu, [(WUP[:, kc, fc * 128:(fc + 1) * 128], h2T[:, kc, :]) for kc in range(8)])
                        r_ = rr[fc % 4]
                        S.act(r_, pu, AF.Relu)
                        S.tt(hid[:, fc, :], r_, r_, ALU.mult, eng=('vector' if fc % 2 == 0 else 'gpsimd'))
            S.barrier()
            A.reset(mk)

        def chk(name):
            if stop_after == name:
                raise _Stop()
        try:
          phase_mod()
          chk('mod')
          for l in range(nlayers):
              need_ctx = l < DEPTH - 1
              last = (l == DEPTH - 1)
              Xsrc = xT_in if l == 0 else XST
              S.dma(pp, W[l]['pp'])
              mk = A.mark()
              zT = A.alloc([32, NT], BF16)
              uTl = A.alloc([128, 2, TL + 30], BF16)
              uTc = A.alloc([128, 2, TC + 30], BF16)
              S.memset(zT, 1.0)
              S.memset(uTl, 0.0, eng='gpsimd')
              S.memset(uTc, 0.0, eng='gpsimd')
              phase_A(l, Xsrc, zT, uTl, uTc)
              if dbg and l == 0:
                  for t_ in range(NTILE):
                      S.dma(dbg_out['PT'][t_ * 128:(t_ + 1) * 128, :], PT[t_ * 128:(t_ + 1) * 128, :], q='gpsimd')
              chk('A')
              hf = 2 if last else 1
              phase_conv(l, uTl, uTc, need_ctx, 8 // hf)
              chk('conv')
              phase_att(l, need_ctx, 8 // hf)
              chk('att')
              phase_scan(l, 'gla', zT, need_ctx, 32 // hf)
              chk('gla')
              phase_scan(l, 'ret', zT, need_ctx, 32 // hf)
              chk('ret')
              if dbg and l == 0:
                  for c_ in range(8):
                      S.dma(dbg_out['YT'][c_ * 128:(c_ + 1) * 128, :], YT[c_ * 128:(c_ + 1) * 128, :], q='gpsimd')
              S.barrier()
              A.reset(mk)
              phase_C(l, Xsrc, last, need_ctx, 16 // hf)
        except _Stop:
            if dbg:
                for c_ in range(8):
                    S.dma(dbg_out['YT'][c_ * 128:(c_ + 1) * 128, :], YT[c_ * 128:(c_ + 1) * 128, :], q='gpsimd')
        S.finish()
        S.emit()
        print("ops", S.n_ops, "waits", S.n_waits, {e: len(S.items[e]) for e in ENGS})
    return nc


def make_consts(rev=False):
    cst = np.zeros((128, NCST), np.float64)
    cst[:, 0:128] = np.eye(128)
    s = np.arange(128)[:, None]
    c = np.arange(128)[None, :]
    cst[:, 128:256] = np.where(s <= c, -1.0 / 16, 0.0)
    cst[:, 256:384] = np.where(s >= c, -1.0 / 16, 0.0)
    cst[:, 384:512] = np.where(s > c, -1.0 / 16, 0.0)
    cst[:, 512:640] = np.where(s < c, -1.0 / 16, 0.0)
    cst[:, 640:768] = np.where(s <= c, 1.0, 0.0)
    cst[:, 768:896] = np.where(s >= c, 1.0, 0.0)
    f = np.arange(128)[:, None] // 32
    he = np.arange(256)[None, :] // 64
    cst[:, 896:1152] = (f == he).astype(np.float64)
    for h in range(4):
        cst[:, 1152 + h] = (np.arange(128) // 32 == h)
    lg = np.log1p(-np.exp2(-5.0 - np.arange(4)))
    lgf = lg[np.arange(128) // 32]
    sc = 32 ** -0.5
    cc = np.arange(128)
    cst[:, 1156:1284] = np.exp(lgf[:, None] * (cc[None, :] + 1))
    cst[:, 1284:1412] = np.exp(-lgf[:, None] * (cc[None, :] + 1)) * sc
    cst[:, 1412:1540] = np.exp(lgf[None, :] * (127 - cc[:, None])) * sc
    cst[:, 1540:1668] = np.exp(lgf[:, None] * (128 - cc[None, :]))
    cst[:, 1668:1796] = np.exp(-lgf[:, None] * (128 - cc[None, :])) * sc
    cst[:, 1796:1924] = np.exp(lgf[None, :] * cc[:, None]) * sc
    cst[:, 1924] = np.exp(lgf * 128)
    t = np.arange(TL)
    if rev:
        t = t[::-1]
    row = (t // 64).astype(np.float64)
    col = (t % 64).astype(np.float64)

    def tables(hd):
        n_ax = hd // 4
        inv = 10000.0 ** (-np.arange(n_ax, dtype=np.float64) / n_ax)
        inv = inv.astype(np.float32).astype(np.float64)
        ang = np.concatenate([row[:, None] * inv, col[:, None] * inv], -1).astype(np.float32)
        return np.cos(ang), np.sin(ang)
    ca, sa = tables(64)
    cr, sr = tables(32)
    ropeA = np.concatenate([ca, sa], -1).reshape(32, 128, 64).transpose(1, 0, 2).reshape(128, 32 * 64)
    ropeR = np.concatenate([cr, sr], -1).reshape(32, 128, 32).transpose(1, 0, 2).reshape(128, 32 * 32)
    return (cst.astype(np.float32), np.ascontiguousarray(ropeA, np.float32), np.ascontiguousarray(ropeR, np.float32))


def fm(v):
    v = np.asarray(v, np.float32)
    return v.reshape(-1, 128).T


def make_in_maps(inp):
    f32 = lambda a: np.ascontiguousarray(np.asarray(a, np.float32))
    consts = [make_consts(False), make_consts(True)]
    per_layer = [[], []]
    for rev in (0, 1):
        for l in range(DEPTH):
            pp = np.zeros((128, NPP), np.float32)
            pp[:, 0:8] = fm(inp['norm1_g'][l])
            pp[:, 8:16] = fm(inp['norm2_g'][l])
            pp[:, 16:64] = fm(inp['b_mod'][l])
            pp[:, 64:66] = fm(inp['conv_b_dw'][l])
            pp[:, 66:68] = fm(inp['conv_ln_g'][l])
            pp[:, 68:70] = fm(inp['conv_ln_b'][l])
            pp[:, 70:72] = fm(inp['conv_b_pw'][l])
            wdw = np.asarray(inp['conv_w_dw'][l], np.float32)
            if rev:
                wdw = wdw[::-1]
            for cc in range(2):
                pp[:, 72 + cc * 31:72 + (cc + 1) * 31] = wdw[:, cc * 128:(cc + 1) * 128].T
            pp[:, 134:142] = fm(inp['final_norm_g'])
            bc = np.zeros((128, NBC), np.float32)
            bc[:, 0:256] = np.asarray(inp['gla_norm_g'][l], np.float32).reshape(1, 256)
            bc[:, 256:512] = np.asarray(inp['ret_norm_g'][l], np.float32).reshape(1, 256)
            qg = np.asarray(inp['att_q_norm_g'][l], np.float32)
            kg = np.asarray(inp['att_k_norm_g'][l], np.float32)
            bc[:, 512:896] = np.concatenate([qg, qg, qg, qg, kg, kg])[None, :]
            waug = np.zeros((32, 256), np.float32)
            fo, bo = (128, 0) if rev else (0, 128)
            waug[0:16, fo:fo + 128] = inp['gla_w_a_f'][l]
            waug[0:16, bo:bo + 128] = inp['gla_w_a_b'][l]
            waug[16, fo:fo + 128] = inp['gla_b_a_f'][l]
            waug[16, bo:bo + 128] = inp['gla_b_a_b'][l]
            per_layer[rev].append({
                "w_mod%d" % l: f32(inp['w_mod'][l]), "w_in%d" % l: f32(inp['w_in'][l]),
                "w_out%d" % l: f32(inp['w_out'][l]), "w_up%d" % l: f32(inp['w_up'][l]),
                "w_down%d" % l: f32(inp['w_down'][l]), "pp%d" % l: pp, "bc%d" % l: bc,
                "waug%d" % l: waug, "wpw%d" % l: f32(inp['conv_w_pw'][l])})
    x = np.asarray(inp['x'], np.float32)
    ctx = np.asarray(inp['ctx'], np.float32)
    c = np.asarray(inp['c'], np.float32)
    cctx = np.asarray(inp['c_ctx'], np.float32)
    maps = []
    for core in range(8):
        b = core % 4
        rev = core // 4
        if rev:
            xT = np.ascontiguousarray(np.concatenate([x[b][::-1], ctx[b][::-1]], 0).T)
        else:
            xT = np.ascontiguousarray(np.concatenate([x[b], ctx[b]], 0).T)
        cvec = np.ascontiguousarray(np.stack([fm(c[b]), fm(cctx)], -1).reshape(128, 16))
        cst, ropeA, ropeR = consts[rev]
        m = {"xT": xT, "cvec": cvec, "cst": cst, "ropeA": ropeA, "ropeR": ropeR}
        for l in range(DEPTH):
            m.update(per_layer[rev][l])
        maps.append(m)
    return maps


_NC_CACHE = {}


def kernel(**inputs):
    maps = make_in_maps(inputs)
    if 'nc' not in _NC_CACHE:
        _NC_CACHE['nc'] = build_program()
    nc = _NC_CACHE['nc']
    res = run_bass_kernel_spmd(nc, maps, core_ids=list(range(8)))
    out = np.empty((4, TL, D), np.float32)
    h = TL // 2
    for b in range(4):
        out[b, 0:h] = res.results[b]["outT"].T
        out[b, h:TL] = res.results[b + 4]["outT"].T[::-1]
    return out
```

```python
import math
from contextlib import ExitStack
import numpy as np
import concourse.bass as bass
import concourse.mybir as mybir
from concourse.bass_utils import run_bass_kernel_spmd

F32 = mybir.dt.float32
BF16 = mybir.dt.bfloat16
U8 = mybir.dt.uint8
AF = mybir.ActivationFunctionType
ALU = mybir.AluOpType
AX = mybir.AxisListType

D = 1024
TL = 4096
TC = 256
NT = TL + TC
NTILE = NT // 128
PIN = 2576
PTW = 2064
EPS = 1e-6
NPP = 142
NBC = 896
NCST = 1925
DEPTH = 2

ENGS = ['tensor', 'vector', 'scalar', 'gpsimd', 'sync']
DMA_POOL = 24
KEEPWARM = 0
DSIZE = {F32: 4, BF16: 2, U8: 1}


class Sched:
    def __init__(self, nc, stack):
        self.nc = nc
        self.items = {e: [] for e in ENGS}
        self.seq = {e: 0 for e in ENGS}
        self.esem = {e: stack.enter_context(nc.semaphore("sq_" + e)) for e in ENGS if e != 'sync'}
        self.dpool = {}
        self.dcount = {}
        for q in ['sync', 'gpsimd']:
            self.dpool[q] = [stack.enter_context(nc.semaphore("dq_%s_%d" % (q, i))) for i in range(DMA_POOL)]
            self.dcount[q] = 0
        self.waited = {}
        self.recs = {}
        self.dram_rowlen = {}
        self.arena_name = None
        self.arena_allocs = []
        self.n_ops = 0
        self.n_waits = 0

    def box(self, ap):
        t = ap.tensor
        name = t.name
        off = int(ap.offset)
        pairs = [(int(s), int(c)) for s, c in ap.ap]
        mx = off + sum(s * (c - 1) for s, c in pairs if c > 0)
        if name in self.dram_rowlen:
            rowlen = self.dram_rowlen[name]
            es = 1
        else:
            rowlen = pairs[0][0]
            es = DSIZE[ap.dtype]
        r0 = off // rowlen
        r1 = mx // rowlen
        c0 = off % rowlen
        c1 = c0 + sum(s * (c - 1) for s, c in pairs if s < rowlen and c > 0)
        if c1 >= rowlen:
            c0, c1 = 0, rowlen - 1
        c0 *= es
        c1 = c1 * es + es - 1
        if name == self.arena_name:
            for (a, b, i) in self.arena_allocs:
                if a <= c0 < b:
                    assert c1 < b, "arena view crosses allocation"
                    name = (name, i)
                    break
            else:
                raise AssertionError("arena box not found")
        return name, (r0, r1, c0, c1)

    def _need(self, eng, ev, waits):
        sem, val = ev
        k = (eng, id(sem))
        if self.waited.get(k, 0) >= val:
            return
        self.waited[k] = val
        waits[id(sem)] = (sem, val)

    def add(self, eng, fn, reads=(), writes=(), dma=False, acc=False):
        waits = {}
        rb = [self.box(a) for a in reads]
        wb = [self.box(a) for a in writes]
        for name, b in rb:
            ps = isinstance(name, str) and name.startswith("ps")
            for r in self.recs.get(name, ()):
                if ps and r[5] != eng:
                    self._need(eng, r[6], waits)
                elif r[4] and not (r[1] < b[0] or b[1] < r[0] or r[3] < b[2] or b[3] < r[2]):
                    self._need(eng, r[6], waits)
        for name, b in wb:
            ps = isinstance(name, str) and name.startswith("ps")
            for r in self.recs.get(name, ()):
                if ps and r[5] != eng:
                    self._need(eng, r[6], waits)
                elif ps and eng == 'tensor':
                    continue
                elif not (r[1] < b[0] or b[1] < r[0] or r[3] < b[2] or b[3] < r[2]):
                    if acc and r[4] and r[5] == 'tensor':
                        continue
                    self._need(eng, r[6], waits)
        if dma:
            q = eng
            i = self.dcount[q]
            self.dcount[q] += 1
            sem = self.dpool[q][i % DMA_POOL]
            val = 16 * (i // DMA_POOL + 1)
            if i >= DMA_POOL:
                self._need(eng, (sem, val - 16), waits)
            ev = (sem, val)
            inc = (sem, 16)
        else:
            self.seq[eng] += 1
            ev = (self.esem[eng], self.seq[eng])
            inc = (self.esem[eng], 1)
        for name, b in wb:
            lst = self.recs.setdefault(name, [])
            lst[:] = [r for r in lst if not (b[0] <= r[0] and r[1] <= b[1] and b[2] <= r[2] and r[3] <= b[3])]
            lst.append([b[0], b[1], b[2], b[3], True, eng, ev])
        for name, b in rb:
            lst = self.recs.setdefault(name, [])
            if not dma:
                lst[:] = [r for r in lst if not ((not r[4]) and r[5] == eng and r[6][0] is ev[0]
                                                 and b[0] <= r[0] and r[1] <= b[1] and b[2] <= r[2] and r[3] <= b[3])]
            lst.append([b[0], b[1], b[2], b[3], False, eng, ev])
        self.items[eng].append((list(waits.values()), fn, inc))
        self.n_ops += 1
        self.n_waits += len(waits)
        return ev

    def barrier(self):
        for eng in ENGS:
            waits = {}
            for q in self.dpool:
                n = self.dcount[q]
                for j in range(min(n, DMA_POOL)):
                    uses = (n - 1 - j) // DMA_POOL + 1
                    self._need(eng, (self.dpool[q][j], 16 * uses), waits)
            for e in self.esem:
                if self.seq[e] > 0:
                    self._need(eng, (self.esem[e], self.seq[e]), waits)
            if waits:
                self.items[eng].append((list(waits.values()), None, None))
        self.recs = {}

    def dma(self, out, in_, q='sync'):
        return self.add(q, lambda e: e.dma_start(out=out, in_=in_), reads=[in_], writes=[out], dma=True)

    def mm(self, out, pairs, start=True, stop=True, acc=False):
        n = len(pairs)

        def fn(e):
            ins = None
            for i, p in enumerate(pairs):
                ins = e.matmul(out, p[0], p[1], start=(start and i == 0), stop=(stop and i == n - 1))
            return ins
        rd = []
        for p in pairs:
            rd += [p[0], p[1]]
        return self.add('tensor', fn, reads=rd, writes=[out], acc=acc)

    def transpose(self, out, in_, ident):
        return self.add('tensor', lambda e: e.transpose(out, in_, ident), reads=[in_, ident], writes=[out])

    def act(self, out, in_, func, bias=None, scale=None, accum_out=None):
        kw = {}
        rd = [in_]
        wr = [out]
        if bias is not None:
            kw['bias'] = bias
            if not isinstance(bias, (int, float)):
                rd.append(bias)
        if scale is not None:
            kw['scale'] = scale
            if not isinstance(scale, (int, float)):
                rd.append(scale)
        if accum_out is not None:
            kw['accum_out'] = accum_out
            wr.append(accum_out)
        return self.add('scalar', lambda e: e.activation(out, in_, func, **kw), reads=rd, writes=wr)

    def tt(self, out, in0, in1, op, eng='vector'):
        return self.add(eng, lambda e: e.tensor_tensor(out, in0, in1, op), reads=[in0, in1], writes=[out])

    def ts(self, out, in0, s1, s2, op0, op1=None, eng='vector'):
        rd = [in0]
        if not isinstance(s1, (int, float)):
            rd.append(s1)
        if s2 is not None and not isinstance(s2, (int, float)):
            rd.append(s2)
        if op1 is None:
            return self.add(eng, lambda e: e.tensor_scalar(out, in0, s1, None, op0), reads=rd, writes=[out])
        return self.add(eng, lambda e: e.tensor_scalar(out, in0, s1, s2, op0, op1), reads=rd, writes=[out])

    def stt(self, out, in0, scalar, in1, op0, op1, eng='vector'):
        eng = 'vector'
        rd = [in0, in1]
        if not isinstance(scalar, (int, float)):
            rd.append(scalar)
        return self.add(eng, lambda e: e.scalar_tensor_tensor(out, in0, scalar, in1, op0, op1), reads=rd, writes=[out])

    def copy(self, out, in_, eng='vector'):
        if eng == 'scalar':
            return self.add('scalar', lambda e: e.copy(out, in_), reads=[in_], writes=[out])
        return self.add(eng, lambda e: e.tensor_copy(out, in_), reads=[in_], writes=[out])

    def memset(self, ap, val, eng='vector'):
        return self.add(eng, lambda e: e.memset(ap, val), reads=[], writes=[ap])

    def recip(self, out, in_):
        return self.add('vector', lambda e: e.reciprocal(out, in_), reads=[in_], writes=[out])

    def reduce(self, out, in_, op=ALU.add, eng='vector'):
        return self.add(eng, lambda e: e.tensor_reduce(out, in_, AX.X, op), reads=[in_], writes=[out])

    def finish(self):
        self.barrier()

    def emit(self):
        nc = self.nc
        items = self.items

        def run(e, lst):
            for waits, fn, inc in lst:
                for sem, v in waits:
                    e.wait_ge(sem, v)
                if fn is None:
                    continue
                ins = fn(e)
                ins.then_inc(inc[0], inc[1])

        with nc.Block() as block:
            @block.tensor
            def _(e):
                run(e, items['tensor'])

            @block.vector
            def _(e):
                run(e, items['vector'])

            @block.scalar
            def _(e):
                run(e, items['scalar'])

            @block.gpsimd
            def _(e):
                run(e, items['gpsimd'])

            @block.sync
            def _(e):
                run(e, items['sync'])


class Arena:
    def __init__(self, S, t, nbytes):
        self.S = S
        self.t = t
        self.n = nbytes
        self.off = 0
        self.uid = 0
        S.arena_name = t.name

    def alloc(self, shape, dtype):
        es = DSIZE[dtype]
        free = 1
        for s in shape[1:]:
            free *= s
        nb = (free * es + 63) // 64 * 64
        assert self.off + nb <= self.n, "arena overflow: need %d have %d" % (self.off + nb, self.n)
        a = self.off
        self.off += nb
        self.uid += 1
        self.S.arena_allocs.append((a, a + nb, self.uid))
        ap = self.t[0:shape[0], a:a + free * es].bitcast(dtype)
        if len(shape) > 2:
            names = ["d%d" % i for i in range(len(shape) - 1)]
            kw = {names[i]: shape[i + 1] for i in range(len(shape) - 2)}
            ap = ap.rearrange("p (%s) -> p %s" % (" ".join(names), " ".join(names)), **kw)
        return ap

    def mark(self):
        return (self.off, len(self.S.arena_allocs))

    def reset(self, mark):
        self.off = mark[0]
        del self.S.arena_allocs[mark[1]:]


def bcast_mid(ap, n):
    return ap.unsqueeze(1).to_broadcast([ap.shape[0], n, ap.shape[1]])


def build_program(dbg=False, nlayers=DEPTH, stop_after=None):
    nc = bass.Bass("TRN2", target_bir_lowering=False)
    dt_in = lambda name, shape, dt=F32: nc.dram_tensor(name, list(shape), dt, kind="ExternalInput").ap()
    xT_in = dt_in("xT", [D, NT])
    cvec_in = dt_in("cvec", [128, 16])
    cst_in = dt_in("cst", [128, NCST])
    ropeA_in = dt_in("ropeA", [128, 32 * 64])
    ropeR_in = dt_in("ropeR", [128, 32 * 32])
    W = []
    for l in range(DEPTH):
        W.append(dict(
            w_mod=dt_in("w_mod%d" % l, [D, 6 * D]), w_in=dt_in("w_in%d" % l, [D, PIN]),
            w_out=dt_in("w_out%d" % l, [D, D]), w_up=dt_in("w_up%d" % l, [D, 4 * D]),
            w_down=dt_in("w_down%d" % l, [4 * D, D]), pp=dt_in("pp%d" % l, [128, NPP]),
            bc=dt_in("bc%d" % l, [128, NBC]), waug=dt_in("waug%d" % l, [32, 256]),
            wpw=dt_in("wpw%d" % l, [256, 256])))
    outT = nc.dram_tensor("outT", [D, TL // 2], F32, kind="ExternalOutput").ap()
    XST = nc.dram_tensor("XST", [D, NT], F32).ap()
    PT = nc.dram_tensor("PTs", [NT, PTW], BF16).ap()
    YT = nc.dram_tensor("YTs", [D, NT], BF16).ap()
    WOb = nc.dram_tensor("WOb", [D, D], BF16).ap()
    WUb = nc.dram_tensor("WUb", [D, 4 * D], BF16).ap()
    WDb = nc.dram_tensor("WDb", [4 * D, D], BF16).ap()
    WIb = nc.dram_tensor("WIb", [D, PIN], BF16).ap()
    dbg_out = {}
    if dbg:
        dbg_out['PT'] = nc.dram_tensor("dbgPT", [NT, PTW], BF16, kind="ExternalOutput").ap()
        dbg_out['YT'] = nc.dram_tensor("dbgYT", [D, NT], BF16, kind="ExternalOutput").ap()
        dbg_out['X1'] = nc.dram_tensor("dbgX1", [D, NT], F32, kind="ExternalOutput").ap()
        dbg_out['MOD'] = nc.dram_tensor("dbgMOD", [128, 192], F32, kind="ExternalOutput").ap()

    with ExitStack() as st:
        S = Sched(nc, st)
        for nm, ap_ in [("WIb", WIb), ("WOb", WOb), ("WUb", WUb), ("WDb", WDb), ("xT", xT_in), ("XST", XST), ("PTs", PT), ("YTs", YT), ("outT", outT), ("cvec", cvec_in),
                        ("cst", cst_in), ("ropeA", ropeA_in), ("ropeR", ropeR_in)]:
            S.dram_rowlen[nm] = ap_.shape[1]
        for l in range(DEPTH):
            for k, v in W[l].items():
                S.dram_rowlen[v.tensor.name] = v.shape[1]
        for k, v in dbg_out.items():
            S.dram_rowlen[v.tensor.name] = v.shape[1]
        ARENA_BYTES = 206 * 1024
        at = st.enter_context(nc.sbuf_tensor("arena", [128, ARENA_BYTES], U8))
        A = Arena(S, at, ARENA_BYTES)
        PS = [st.enter_context(nc.psum_tensor("ps%d" % i, [128, 512], F32)) for i in range(8)]
        PSB = [p[:].bitcast(BF16) for p in PS]

        cst = A.alloc([128, NCST], F32)
        S.dma(cst, cst_in)
        identb = A.alloc([128, 128], BF16)
        onesb = A.alloc([128, 128], BF16)
        onesf = A.alloc([128, 128], F32)
        maskb16 = A.alloc([128, 2, 128], BF16)
        MOD = A.alloc([128, DEPTH, 6, 8, 2], F32)
        pp = A.alloc([128, NPP], F32)
        S.copy(identb, cst[:, 0:128])
        S.memset(onesb, 1.0 / 1024)
        S.memset(onesf, 1.0)
        S.copy(maskb16[:, 0, :], cst[:, 640:768])
        S.copy(maskb16[:, 1, :], cst[:, 768:896])
        TRI = [cst[:, 128:256], cst[:, 256:384]]
        TRIR = [cst[:, 384:512], cst[:, 512:640]]
        BDm = cst[:, 896:1152]
        HM = cst[:, 1152:1156]
        base_mark = A.mark()

        class _Stop(Exception):
            pass

        def load_cast(dst, src, stg, width, engs=('vector', 'gpsimd')):
            n = dst.shape[1]
            i = 0
            c0 = 0
            while c0 < n:
                w = min(width, n - c0)
                sg = stg[i % len(stg)]
                S.dma(sg[:, 0:w], src[:, c0:c0 + w], q=('sync' if i % 2 == 0 else 'gpsimd'))
                S.copy(dst[:, c0:c0 + w], sg[:, 0:w], eng=engs[i % len(engs)])
                c0 += w
                i += 1

        def phase_mod():
            mk = A.mark()
            cs = A.alloc([128, 8, 2], F32)
            sc = A.alloc([128, 8, 2], F32)
            S.dma(cs, cvec_in.rearrange("p (c m) -> p c m", m=2))
            S.act(sc, cs, AF.Silu)
            wst = [A.alloc([128, 6144], F32) for _ in range(2)]
            acc = A.alloc([128, 48, 2], F32)
            ppl = A.alloc([128, NPP], F32)
            for l in range(nlayers):
                S.dma(ppl, W[l]['pp'])
                for k in range(8):
                    wk = wst[k % 2]
                    S.dma(wk, W[l]['w_mod'][k * 128:(k + 1) * 128, :], q=('sync' if k % 2 == 0 else 'gpsimd'))
                    pm = PS[k % 2][:, 0:96].rearrange("p (j m) -> p j m", m=2)
                    for j in range(48):
                        S.mm(pm[:, j, :], [(wk[:, j * 128:(j + 1) * 128], sc[:, k, :])])
                    if k == 0:
                        S.copy(acc, pm)
                    else:
                        S.tt(acc, acc, pm, ALU.add)
                bm = ppl[:, 16:64].unsqueeze(2).to_broadcast([128, 48, 2])
                S.tt(acc, acc, bm, ALU.add)
                a4 = acc.rearrange("p (w c) m -> p w c m", w=6)
                g1n = ppl[:, 0:8].unsqueeze(2).to_broadcast([128, 8, 2])
                g2n = ppl[:, 8:16].unsqueeze(2).to_broadcast([128, 8, 2])
                S.stt(MOD[:, l, 0], a4[:, 1], 1.0, g1n, ALU.add, ALU.mult)
                S.copy(MOD[:, l, 1], a4[:, 0])
                S.copy(MOD[:, l, 2], a4[:, 2])
                S.stt(MOD[:, l, 3], a4[:, 4], 1.0, g2n, ALU.add, ALU.mult)
                S.copy(MOD[:, l, 4], a4[:, 3])
                S.copy(MOD[:, l, 5], a4[:, 5])
            if dbg:
                S.dma(dbg_out['MOD'], MOD.rearrange("p l w c m -> p (l w c m)"), q='gpsimd')
            S.barrier()
            A.reset(mk)

        def norm_mod(xg, n, sq, tmp, hT, Acol, Bcol, pbank):
            S.act(sq[:, :, 0:n], xg[:, :, 0:n], AF.Square)
            ps = PS[pbank][:, 0:n]
            S.mm(ps, [(onesb, sq[:, c, 0:n]) for c in range(8)])
            rstd = tmp[0][:, 0:n]
            S.act(rstd, ps, AF.Sqrt, bias=EPS)
            S.recip(rstd, rstd)
            for c in range(8):
                t = tmp[1 + c % 2][:, 0:n]
                S.tt(t, xg[:, c, 0:n], rstd, ALU.mult, eng=('vector' if c % 2 == 0 else 'gpsimd'))
                S.act(hT[:, c, 0:n], t, AF.Identity, bias=Bcol[:, c:c + 1], scale=Acol[:, c:c + 1])

        def phase_A(l, Xsrc, zT, uTl, uTc):
            mk = A.mark()
            WIN = A.alloc([128, 8, PIN], BF16)
            if l == 0:
                stg = [A.alloc([128, PIN], F32) for _ in range(2)]
                for k in range(8):
                    S.dma(stg[k % 2], W[l]['w_in'][k * 128:(k + 1) * 128, :], q=('sync' if k % 2 == 0 else 'gpsimd'))
                    S.copy(WIN[:, k, :], stg[k % 2], eng=('vector' if k % 2 == 0 else 'gpsimd'))
            else:
                for k2 in range(4):
                    S.dma(WIN[:, 2 * k2:2 * k2 + 2, :], WIb[k2 * 256:(k2 + 1) * 256, :].rearrange("(k p) n -> p k n", p=128),
                          q=('sync' if k2 % 2 == 0 else 'gpsimd'))
            xgs = [A.alloc([128, 8, 512], F32) for _ in range(2)]
            sq = A.alloc([128, 8, 512], BF16)
            tmp = [A.alloc([128, 512], F32) for _ in range(3)]
            hTs = [A.alloc([128, 8, 512], BF16) for _ in range(2)]
            OTs = [A.alloc([128, PTW], BF16) for _ in range(2)]
            sig = [A.alloc([128, 512], F32) for _ in range(2)]
            Xv = Xsrc.rearrange("(c p) t -> p c t", p=128)
            colblocks = [(512, 1024), (1024, 1536), (1536, 2048), (2048, 2560), (2560, 2576)]
            ti = 0

            def prep_group(g):
                n = 512 if g < 8 else 256
                m = 0 if g < 8 else 1
                S.dma(xgs[g % 2][:, :, 0:n], Xv[:, :, g * 512:g * 512 + n])
                norm_mod(xgs[g % 2], n, sq, tmp, hTs[g % 2], MOD[:, l, 0, :, m], MOD[:, l, 1, :, m], 0)
            prep_group(0)
            for g in range(9):
                n = 512 if g < 8 else 256
                t0 = g * 512
                m = 0 if g < 8 else 1
                hT = hTs[g % 2]
                if g + 1 < 9:
                    prep_group(g + 1)
                for tt_ in range(n // 128):
                    OT = OTs[ti % 2]
                    for bi, (c0, c1) in enumerate(colblocks):
                        if l == DEPTH - 1 and 4 <= g < 8 and bi in (1, 4):
                            continue
                        w = c1 - c0
                        ps = PS[1 + (bi % 2)][:, 0:w]
                        S.mm(ps, [(hT[:, c, tt_ * 128:(tt_ + 1) * 128], WIN[:, c, c0:c1]) for c in range(8)])
                        if bi % 2 == 0:
                            S.copy(OT[:, c0 - 512:c1 - 512], ps, eng='scalar')
                        else:
                            S.copy(OT[:, c0 - 512:c1 - 512], ps, eng='vector')
                    tok = t0 + tt_ * 128
                    S.dma(PT[tok:tok + 128, :], OT, q='gpsimd')
                    ti += 1
                for cc in range(2):
                    pa = PS[3][:, 0:n]
                    pg = PS[4][:, 0:n]
                    S.mm(pa, [(WIN[:, c, cc * 128:(cc + 1) * 128], hT[:, c, 0:n]) for c in range(8)])
                    S.mm(pg, [(WIN[:, c, 256 + cc * 128:256 + (cc + 1) * 128], hT[:, c, 0:n]) for c in range(8)])
                    sg = sig[cc][:, 0:n]
                    S.act(sg, pg, AF.Sigmoid)
                    if g < 8:
                        dst = uTl[:, cc, 15 + t0:15 + t0 + n]
                    else:
                        dst = uTc[:, cc, 15:15 + n]
                    S.tt(dst, pa, sg, ALU.mult)
                pz = PS[5][0:16, 0:n]
                S.mm(pz, [(WIN[:, c, 1280:1296], hT[:, c, 0:n]) for c in range(8)])
                S.copy(zT[0:16, t0:t0 + n], pz)
            S.barrier()
            A.reset(mk)

        def phase_conv(l, uTl, uTc, need_ctx, nblk):
            mk = A.mark()
            DG = A.alloc([128, 2, 31, 128], BF16)
            WPW = A.alloc([128, 2, 256], BF16)
            wst = A.alloc([128, 2, 256], F32)
            S.dma(wst, W[l]['wpw'].rearrange("(c p) n -> p c n", p=128))
            S.copy(WPW, wst)
            for cc in range(2):
                for j in range(31):
                    S.ts(DG[:, cc, j, :], identb, pp[:, 72 + cc * 31 + j:73 + cc * 31 + j], None, ALU.mult,
                         eng=('vector' if j % 2 == 0 else 'gpsimd'))
            y = [A.alloc([128, 512], F32) for _ in range(2)]
            ysq = [A.alloc([128, 512], F32) for _ in range(2)]
            msb = A.alloc([128, 512], F32)
            m2 = A.alloc([128, 512], F32)
            rstd = A.alloc([128, 512], F32)
            t1 = [A.alloc([128, 512], F32) for _ in range(2)]
            sb = [A.alloc([128, 512], BF16) for _ in range(2)]
            yo = [A.alloc([128, 2, 512], BF16) for _ in range(2)]
            blocks = [(uTl, b * 512, 512, b * 512) for b in range(nblk)]
            if need_ctx:
                blocks.append((uTc, 0, 256, TL))
            for bi, (uT, t0, n, tok0) in enumerate(blocks):
                for cc in range(2):
                    pc = PS[cc][:, 0:n]
                    S.mm(pc, [(DG[:, cc, j, :], uT[:, cc, t0 + j:t0 + j + n]) for j in range(31)])
                    S.act(y[cc][:, 0:n], pc, AF.Identity, bias=pp[:, 64 + cc:65 + cc])
                    S.act(ysq[cc][:, 0:n], pc, AF.Square, bias=pp[:, 64 + cc:65 + cc])
                pm = PS[2][:, 0:n]
                pq = PS[3][:, 0:n]
                S.mm(pm, [(onesf, y[0][:, 0:n]), (onesf, y[1][:, 0:n])])
                S.mm(pq, [(onesf, ysq[0][:, 0:n]), (onesf, ysq[1][:, 0:n])])
                S.act(msb[:, 0:n], pm, AF.Identity, scale=1.0 / 256)
                S.act(m2[:, 0:n], pm, AF.Square, scale=1.0 / 256)
                S.stt(rstd[:, 0:n], pq, 1.0 / 256, m2[:, 0:n], ALU.mult, ALU.subtract)
                S.act(rstd[:, 0:n], rstd[:, 0:n], AF.Sqrt, bias=EPS)
                S.recip(rstd[:, 0:n], rstd[:, 0:n])
                for cc in range(2):
                    e_ = 'vector' if cc == 0 else 'gpsimd'
                    S.tt(t1[cc][:, 0:n], y[cc][:, 0:n], msb[:, 0:n], ALU.subtract, eng=e_)
                    S.tt(t1[cc][:, 0:n], t1[cc][:, 0:n], rstd[:, 0:n], ALU.mult, eng=e_)
                    S.act(sb[cc][:, 0:n], t1[cc][:, 0:n], AF.Silu, bias=pp[:, 68 + cc:69 + cc], scale=pp[:, 66 + cc:67 + cc])
                yob = yo[bi % 2]
                for co in range(2):
                    ppw = PS[4 + co][:, 0:n]
                    S.mm(ppw, [(WPW[:, ci, co * 128:(co + 1) * 128], sb[ci][:, 0:n]) for ci in range(2)])
                    S.act(yob[:, co, 0:n], ppw, AF.Identity, bias=pp[:, 70 + co:71 + co])
                S.dma(YT[0:256, tok0:tok0 + n].rearrange("(c p) t -> p c t", p=128), yob[:, :, 0:n], q='gpsimd')
            S.barrier()
            A.reset(mk)

        def phase_att(l, need_ctx, nqb):
            mk = A.mark()
            QK = A.alloc([128, 4, NT], BF16)
            VA = A.alloc([128, NTILE, 2, 128], BF16)
            QZ = A.alloc([128, 4, NT], BF16)
            VB = A.alloc([128, NTILE, 2, 128], BF16)
            rope = A.alloc([128, 32, 64], F32)
            bc = A.alloc([128, 384], F32)
            S.dma(rope, ropeA_in.rearrange("p (t k) -> p t k", k=64))
            S.dma(bc, W[l]['bc'][:, 512:896])
            S.memset(VA, 0.0)
            S.memset(VA[:, :, :, 64:65], 1.0)
            S.memset(QZ, 0.0, eng='gpsimd')
            S.memset(VB, 0.0, eng='gpsimd')
            S.memset(VB[:, :, :, 0:1], 1.0, eng='gpsimd')
            NB = 2
            raw = [A.alloc([128, NB, 512], BF16) for _ in range(2)]
            f1 = A.alloc([128, NB, 6, 64], F32)
            f2 = A.alloc([128, NB, 6, 64], F32)
            ssq = A.alloc([128, NB, 6], F32)
            qr = [A.alloc([128, NB, 8, 64], BF16) for _ in range(2)]
            tA = A.alloc([128, NB, 6, 32], F32)
            tB = A.alloc([128, NB, 6, 32], F32)
            for bi in range(NTILE // NB):
                tl0 = bi * NB
                rw = raw[bi % 2]
                qb_ = qr[bi % 2]
                S.dma(rw, PT[tl0 * 128:(tl0 + NB) * 128, 784:1296].rearrange("(t p) c -> p t c", p=128))
                qk = rw[:, :, 0:384].rearrange("p t (h e) -> p t h e", e=64)
                S.tt(f1, qk, qk, ALU.mult)
                S.reduce(ssq, f1)
                S.act(ssq, ssq, AF.Sqrt, bias=EPS, scale=1.0 / 64)
                S.recip(ssq, ssq)
                S.tt(f1, qk, ssq.unsqueeze(3).to_broadcast([128, NB, 6, 64]), ALU.mult)
                gq = bc.rearrange("p (h e) -> p h e", e=64).unsqueeze(1).to_broadcast([128, NB, 6, 64])
                S.tt(f2, f1, gq, ALU.mult, eng='gpsimd')
                is_lat = tl0 < 32
                if is_lat:
                    cosv = rope[:, tl0:tl0 + NB, 0:32].unsqueeze(2).to_broadcast([128, NB, 6, 32])
                    sinv = rope[:, tl0:tl0 + NB, 32:64].unsqueeze(2).to_broadcast([128, NB, 6, 32])
                    x1 = f2[:, :, :, 0:32]
                    x2 = f2[:, :, :, 32:64]
                    o1 = f1[:, :, :, 0:32]
                    o2 = f1[:, :, :, 32:64]
                    S.tt(tA, x1, cosv, ALU.mult)
                    S.tt(tB, x2, sinv, ALU.mult, eng='gpsimd')
                    S.tt(o1, tA, tB, ALU.subtract)
                    S.tt(tA, x2, cosv, ALU.mult)
                    S.tt(tB, x1, sinv, ALU.mult, eng='gpsimd')
                    S.tt(o2, tA, tB, ALU.add)
                    src = f1
                else:
                    src = f2
                S.copy(qb_[:, :, 0:4, :], src[:, :, 0:4, :])
                kdst = qb_[:, :, 4:8, :].rearrange("p t (k r) e -> p t k r e", r=2)
                for r_ in range(2):
                    S.copy(kdst[:, :, :, r_, :], src[:, :, 4:6, :], eng='gpsimd')
                for t_ in range(NB):
                    tl = tl0 + t_
                    pt = PSB[6][:, 0:512].rearrange("p (j k) -> p j k", k=128)
                    for j in range(4):
                        S.transpose(pt[:, j, :], qb_[:, t_, 2 * j:2 * j + 2, :].rearrange("p a e -> p (a e)"), identb)
                    S.copy(QK[:, :, tl * 128:(tl + 1) * 128], pt, eng=('vector' if t_ % 2 == 0 else 'scalar'))
                    for j in range(2):
                        S.copy(QZ[0:64, 2 * j, tl * 128:(tl + 1) * 128], pt[0:64, j, :], eng='vector')
                        S.copy(QZ[64:128, 2 * j + 1, tl * 128:(tl + 1) * 128], pt[64:128, j, :], eng='scalar')
                    vv = rw[:, t_, 384:512].rearrange("p (k e) -> p k e", e=64)
                    S.copy(VA[:, tl, :, 0:64], vv, eng='gpsimd')
                    S.copy(VB[:, tl, :, 64:128], vv, eng='gpsimd')
            NS, LA, NP = 4, 3, 5
            Pt = [A.alloc([128, 512], BF16) for _ in range(NP)]
            rsb = A.alloc([128, 512], F32)
            bcs = A.alloc([128, 512], F32)
            Yo = [A.alloc([128, 512], BF16) for _ in range(2)]
            qblocks = [(qb * 512, 512, list(range(NTILE))) for qb in range(nqb)]
            if need_ctx:
                qblocks.append((TL, 256, [32, 33]))
            it = 0
            gi = 0
            cvs = [A.alloc([128, 2048], F32) for _ in range(2)]
            cvb = [A.alloc([128, 2048], BF16) for _ in range(2)]
            pieces = []
            for k in range(8):
                pieces.append((W[l]['w_out'][k * 128:(k + 1) * 128, :], WOb[k * 128:(k + 1) * 128, :], 1024))
            for k in range(8):
                for c_ in range(2):
                    pieces.append((W[l]['w_up'][k * 128:(k + 1) * 128, c_ * 2048:(c_ + 1) * 2048],
                                   WUb[k * 128:(k + 1) * 128, c_ * 2048:(c_ + 1) * 2048], 2048))
            for k in range(32):
                pieces.append((W[l]['w_down'][k * 128:(k + 1) * 128, :], WDb[k * 128:(k + 1) * 128, :], 1024))
            if l + 1 < nlayers:
                for k in range(8):
                    pieces.append((W[l + 1]['w_in'][k * 128:(k + 1) * 128, 0:2048], WIb[k * 128:(k + 1) * 128, 0:2048], 2048))
                    pieces.append((W[l + 1]['w_in'][k * 128:(k + 1) * 128, 2048:PIN], WIb[k * 128:(k + 1) * 128, 2048:PIN], PIN - 2048))
            n_iters = 4 * len(qblocks)
            ppi = (len(pieces) + n_iters - 1) // n_iters
            pci = [0]

            def convert_some():
                for _ in range(ppi):
                    if pci[0] >= len(pieces):
                        return
                    src, dst, w = pieces[pci[0]]
                    sg = cvs[pci[0] % 2]
                    cb = cvb[pci[0] % 2]
                    S.dma(sg[:, 0:w], src, q='sync')
                    S.copy(cb[:, 0:w], sg[:, 0:w], eng='gpsimd')
                    S.dma(dst, cb[:, 0:w], q='gpsimd')
                    pci[0] += 1
            for h in range(4):
                pair, half = h // 2, h % 2
                kvh = pair
                p0 = 64 * half
                for (q0, nq, kts) in qblocks:
                    convert_some()
                    O = PS[4 + it % 2]
                    rhsq = QZ[:, h, q0:q0 + nq]
                    nk = len(kts)
                    sps = {}

                    def smm(i):
                        kt = kts[i]
                        sp = PS[(gi + i) % NS][:, 0:nq]
                        S.mm(sp, [(QK[:, 2 + kvh, kt * 128:(kt + 1) * 128], rhsq)])
                        sps[i] = sp
                    for i in range(min(LA, nk)):
                        smm(i)
                    for i in range(nk):
                        if i + LA < nk:
                            smm(i + LA)
                        sp = sps.pop(i)
                        kt = kts[i]
                        pt_ = Pt[(gi + i) % NP][:, 0:nq]
                        S.act(pt_, sp, AF.Exp, scale=0.125)
                        if KEEPWARM:
                            S.add('tensor', lambda e: e.matmul(PS[7][:, 0:KEEPWARM], identb, QK[:, 0, 0:KEEPWARM],
                                                               start=True, stop=True), reads=[], writes=[])
                        if half == 0:
                            S.mm(O[:, 0:nq], [(VA[:, kt, kvh, :], pt_)], start=(i == 0), stop=(i == nk - 1), acc=(i > 0))
                        else:
                            S.mm(O[:, 0:nq], [(VB[:, kt, kvh, :], pt_)], start=(i == 0), stop=(i == nk - 1), acc=(i > 0))
                    gi += nk
                    yo_ = Yo[it % 2]
                    pb = PS[6]
                    if half == 0:
                        S.recip(rsb[64:65, 0:nq], O[64:65, 0:nq])
                        S.mm(pb[0:64, 0:nq], [(onesf[64:65, 0:64], rsb[64:65, 0:nq])])
                        S.copy(bcs[0:64, 0:nq], pb[0:64, 0:nq], eng='vector')
                        S.tt(yo_[0:64, 0:nq], O[0:64, 0:nq], bcs[0:64, 0:nq], ALU.mult)
                        S.dma(YT[512 + h * 64:512 + (h + 1) * 64, q0:q0 + nq], yo_[0:64, 0:nq], q='gpsimd')
                    else:
                        S.recip(rsb[0:1, 0:nq], O[0:1, 0:nq])
                        S.mm(pb[:, 0:nq], [(onesf[0:1, :], rsb[0:1, 0:nq])])
                        S.copy(bcs[64:128, 0:nq], pb[64:128, 0:nq], eng='vector')
                        S.tt(yo_[64:128, 0:nq], O[64:128, 0:nq], bcs[64:128, 0:nq], ALU.mult)
                        S.dma(YT[512 + h * 64:512 + (h + 1) * 64, q0:q0 + nq], yo_[64:128, 0:nq], q='gpsimd')
                    it += 1
            S.barrier()
            A.reset(mk)

        def phase_scan(l, kind, zT, need_ctx, nlt):
            mk = A.mark()
            is_gla = (kind == 'gla')
            pc0 = 0 if is_gla else 1296
            yrow = 256 if is_gla else 768
            qk_all = A.alloc([128, NTILE, 256], BF16)
            vg_all = A.alloc([128, NTILE, 512], BF16)
            QKT = A.alloc([128, 2, NT], BF16)
            oacc = A.alloc([128, NTILE, 256], F32)
            PTv = PT.rearrange("(t p) c -> p t c", p=128)
            for i_ in range(17):
                sl = slice(2 * i_, 2 * i_ + 2)
                S.dma(qk_all[:, sl, :], PTv[:, sl, pc0:pc0 + 256], q=('sync' if i_ % 2 == 0 else 'gpsimd'))
                S.dma(vg_all[:, sl, :], PTv[:, sl, pc0 + 256:pc0 + 768], q=('gpsimd' if i_ % 2 == 0 else 'sync'))
            gn = A.alloc([128, 256], F32)
            S.dma(gn, W[l]['bc'][:, (0 if is_gla else 256):(256 if is_gla else 512)])
            mk2 = A.mark()
            if not is_gla:
                rope = A.alloc([128, 32, 32], F32)
                S.dma(rope, ropeR_in.rearrange("p (t k) -> p t k", k=32))
                NB = 4
                tA = A.alloc([128, NB, 8, 16], F32)
                tB = A.alloc([128, NB, 8, 16], F32)
                tC = A.alloc([128, NB, 8, 16], F32)
                tD = A.alloc([128, NB, 8, 16], F32)
                for bi in range(32 // NB):
                    tl0 = bi * NB
                    v4 = qk_all[:, tl0:tl0 + NB, :].rearrange("p t (h e) -> p t h e", e=32)
                    x1 = v4[:, :, :, 0:16]
                    x2 = v4[:, :, :, 16:32]
                    cosv = rope[:, tl0:tl0 + NB, 0:16].unsqueeze(2).to_broadcast([128, NB, 8, 16])
                    sinv = rope[:, tl0:tl0 + NB, 16:32].unsqueeze(2).to_broadcast([128, NB, 8, 16])
                    S.tt(tA, x1, cosv, ALU.mult)
                    S.tt(tB, x2, sinv, ALU.mult, eng='gpsimd')
                    S.tt(tC, x2, cosv, ALU.mult)
                    S.tt(tD, x1, sinv, ALU.mult, eng='gpsimd')
                    S.tt(x1, tA, tB, ALU.subtract)
                    S.tt(x2, tC, tD, ALU.add, eng='gpsimd')
            for tl in range(NTILE):
                pt = PSB[6][:, 0:256].rearrange("p (j k) -> p j k", k=128)
                for j in range(2):
                    S.transpose(pt[:, j, :], qk_all[:, tl, j * 128:(j + 1) * 128], identb)
                S.copy(QKT[:, :, tl * 128:(tl + 1) * 128], pt, eng=('vector' if tl % 2 == 0 else 'scalar'))
            S.barrier()
            A.reset(mk2)
            if stop_after == kind + '_prep':
                raise _Stop()
            Sf = A.alloc([128, 256], F32)
            Sb = A.alloc([128, 256], BF16)
            kiT = [A.alloc([128, 4, 128], BF16) for _ in range(2)]
            kvm = A.alloc([128, 256], F32)
            if is_gla:
                WA = A.alloc([32, 256], BF16)
                was = A.alloc([32, 256], F32)
                S.dma(was, W[l]['waug'])
                S.copy(WA, was)
                ee = [A.alloc([128, 128], F32) for _ in range(2)]
                ll = [A.alloc([128, 128], F32) for _ in range(2)]
                E1s = [A.alloc([128, 128], F32) for _ in range(2)]
                E2s = [A.alloc([128, 128], F32) for _ in range(2)]
                E3s = [A.alloc([128, 128], F32) for _ in range(2)]
            LNS = math.log(32 ** -0.5)
            NQ = 6
            qdT = [A.alloc([128, 128], BF16) for _ in range(NQ)]
            kend = [A.alloc([128, 128], BF16) for _ in range(NQ)]
            scm = [A.alloc([128, 4, 128], BF16) for _ in range(3)]
            if is_gla:
                dcs = [A.alloc([128, 1], F32) for _ in range(NQ)]
            for d_ in range(2):
                S.memset(Sf, 0.0)
                S.memset(Sb, 0.0)
                order = [32, 33] + list(range(32)) if d_ == 0 else [33, 32] + list(range(31, -1, -1))
                if stop_after and stop_after.startswith(kind + '_n'):
                    order = order[:int(stop_after[len(kind) + 2:])]
                n_st = len(order)
                want = [need_ctx or ch < nlt for ch in order]
                tks = [slice(ch * 128, (ch + 1) * 128) for ch in order]
                if not is_gla:
                    o_ = 1156 + d_ * 384
                    cE1 = cst[:, o_:o_ + 128]
                    cE2 = cst[:, o_ + 128:o_ + 256]
                    cE3 = cst[:, o_ + 256:o_ + 384]
                    cdc = cst[:, 1924:1925]

                def stA(j):
                    if is_gla:
                        S.mm(PS[0][:, 0:128], [(zT[0:32, tks[j]], WA[0:32, d_ * 128:(d_ + 1) * 128])])

                def stB(j):
                    if is_gla:
                        S.act(ee[j % 2], PS[0][:, 0:128], AF.Exp, scale=-1.0)
                        S.act(ll[j % 2], ee[j % 2], AF.Ln, bias=1.0)

                def stC(j):
                    if is_gla:
                        S.mm(PS[1][:, 0:128], [(ll[j % 2], TRI[d_])])
                        S.mm(PS[1][:, 128:256], [(TRIR[d_], ll[j % 2])])

                def stD(j):
                    if is_gla:
                        pbT = PS[1][:, 0:128]
                        S.act(E1s[j % 2], pbT, AF.Exp, bias=LNS)
                        S.act(E2s[j % 2], pbT, AF.Exp, scale=-1.0)
                        S.act(E3s[j % 2], PS[1][:, 128:256], AF.Exp)
                        col = 127 if d_ == 0 else 0
                        S.act(dcs[j % NQ], pbT[:, col:col + 1], AF.Exp)

                def stE(j):
                    E1, E2, E3 = (E1s[j % 2], E2s[j % 2], E3s[j % 2]) if is_gla else (cE1, cE2, cE3)
                    S.tt(qdT[j % NQ], QKT[:, 0, tks[j]], E1, ALU.mult, eng='gpsimd')
                    if want[j]:
                        for hh in range(4):
                            S.stt(kiT[j % 2][:, hh, :], QKT[:, 1, tks[j]], HM[:, hh:hh + 1], E2, ALU.mult, ALU.mult)
                    S.tt(kend[j % NQ], qk_all[:, order[j], 128:256], E3, ALU.mult, eng='gpsimd')

                def stF(j):
                    if want[j]:
                        psc = PS[2 + j % 2][:, :].rearrange("p (h c) -> p h c", c=128)
                        for hh in range(4):
                            S.mm(psc[:, hh, :], [(kiT[j % 2][:, hh, :], qdT[j % NQ])])

                def stG(j):
                    if want[j]:
                        psc = PS[2 + j % 2][:, :].rearrange("p (h c) -> p h c", c=128)
                        S.tt(scm[j % 3], psc, bcast_mid(maskb16[:, d_, :], 4), ALU.mult)

                def stH(j):
                    ch = order[j]
                    qd = qdT[j % NQ]
                    ke = kend[j % NQ]
                    sm = scm[j % 3]
                    dc = dcs[j % NQ] if is_gla else cdc
                    vch = vg_all[:, ch, 0:256]
                    if want[j]:
                        po = PS[4 + j % 2][:, 0:256]
                        S.mm(po, [(qd, Sb)], start=True, stop=False)
                        for hh in range(4):
                            S.mm(po[:, hh * 64:(hh + 1) * 64], [(sm[:, hh, :], vch[:, hh * 64:(hh + 1) * 64])],
                                 start=False, stop=(hh == 3), acc=True)
                        if d_ == 0:
                            S.copy(oacc[:, ch, :], po, eng='scalar')
                        else:
                            S.tt(oacc[:, ch, :], oacc[:, ch, :], po, ALU.add)
                    pkv = PS[6 + j % 2][:, 0:256]
                    S.mm(pkv, [(ke, vch)])
                    S.tt(kvm, pkv, BDm, ALU.mult)
                    S.stt(Sb, Sf, dc, kvm, ALU.mult, ALU.add)
                    S.stt(Sf, Sf, dc, kvm, ALU.mult, ALU.add)

                stages = [(stH, 0), (stG, 1), (stF, 2), (stE, 3), (stD, 4), (stC, 5), (stB, 6), (stA, 7)]
                for t in range(-7, n_st):
                    for fn_, lead in stages:
                        j = t + lead
                        if 0 <= j < n_st:
                            fn_(j)
            if stop_after and stop_after.startswith(kind + '_n'):
                raise _Stop()
            NB = 2
            f1 = A.alloc([128, NB, 4, 64], F32)
            ssq = A.alloc([128, NB, 4], F32)
            sg = A.alloc([128, NB, 256], F32)
            yb = [A.alloc([128, NB, 256], BF16) for _ in range(2)]
            yts = [A.alloc([128, 2, NB * 128], BF16) for _ in range(2)]
            ntl = NTILE if need_ctx else nlt
            for bi in range(ntl // NB):
                tl0 = bi * NB
                o4 = oacc[:, tl0:tl0 + NB, :].rearrange("p t (h e) -> p t h e", e=64)
                S.tt(f1, o4, o4, ALU.mult)
                S.reduce(ssq, f1)
                S.act(ssq, ssq, AF.Sqrt, bias=EPS, scale=1.0 / 64)
                S.recip(ssq, ssq)
                S.tt(f1, o4, ssq.unsqueeze(3).to_broadcast([128, NB, 4, 64]), ALU.mult)
                f1f = f1.rearrange("p t h e -> p t (h e)")
                S.tt(f1f, f1f, bcast_mid(gn, NB), ALU.mult, eng='gpsimd')
                S.act(sg, vg_all[:, tl0:tl0 + NB, 256:512], AF.Silu)
                y_ = yb[bi % 2]
                S.tt(y_, f1f, sg, ALU.mult)
                yt_ = yts[bi % 2]
                pt = PSB[6][:, 0:2 * NB * 128].rearrange("p (j k) -> p j k", j=2)
                for t_ in range(NB):
                    for j in range(2):
                        S.transpose(pt[:, j, t_ * 128:(t_ + 1) * 128], y_[:, t_, j * 128:(j + 1) * 128], identb)
                S.copy(yt_, pt, eng='scalar')
                S.dma(YT[yrow:yrow + 256, tl0 * 128:(tl0 + NB) * 128].rearrange("(c p) t -> p c t", p=128), yt_, q='gpsimd')
            S.barrier()
            A.reset(mk)

        def phase_C(l, Xsrc, last, need_ctx, nlg):
            mk = A.mark()
            WOUT = A.alloc([128, 8, D], BF16)
            WUP = A.alloc([128, 8, 4 * D], BF16)
            WDN = A.alloc([128, 32, D], BF16)
            hid = A.alloc([128, 32, 256], BF16)
            S.dma(WOUT, WOb.rearrange("(k p) n -> p k n", p=128))
            for k2 in range(4):
                S.dma(WUP[:, 2 * k2:2 * k2 + 2, :], WUb[k2 * 256:(k2 + 1) * 256, :].rearrange("(k p) n -> p k n", p=128),
                      q=('gpsimd' if k2 % 2 == 0 else 'sync'))
            for k8 in range(4):
                S.dma(WDN[:, 8 * k8:8 * k8 + 8, :], WDb[k8 * 1024:(k8 + 1) * 1024, :].rearrange("(k p) n -> p k n", p=128),
                      q=('sync' if k8 % 2 == 0 else 'gpsimd'))
            xgs = [A.alloc([128, 8, 256], F32) for _ in range(2)]
            Yg = A.alloc([128, 8, 256], BF16)
            sq = A.alloc([128, 8, 256], BF16)
            tmp = [A.alloc([128, 256], F32) for _ in range(3)]
            h2T = A.alloc([128, 8, 256], BF16)
            rr = [tmp[1], tmp[2]] + [A.alloc([128, 256], F32) for _ in range(2)]
            Xv = Xsrc.rearrange("(c p) t -> p c t", p=128)
            XSv = XST.rearrange("(c p) t -> p c t", p=128)
            OUv = outT.rearrange("(c p) t -> p c t", p=128)
            YTv = YT.rearrange("(c p) t -> p c t", p=128)
            ngroups = 17 if need_ctx else nlg
            n = 256
            S.dma(xgs[0], Xv[:, :, 0:n])

            def down_part(gp, dc):
                xp = xgs[gp % 2]
                mp = 0 if gp < 16 else 1
                pd = PS[1 + dc % 2][:, 0:n]
                S.mm(pd, [(WDN[:, fc, dc * 128:(dc + 1) * 128], hid[:, fc, :]) for fc in range(32)])
                S.stt(xp[:, dc, :], pd, MOD[:, l, 5, :, mp][:, dc:dc + 1], xp[:, dc, :], ALU.mult, ALU.add)

            def finish_group(gp):
                xp = xgs[gp % 2]
                tp = gp * 256
                if not last:
                    S.dma(XSv[:, :, tp:tp + n], xp, q='gpsimd')
                    if dbg:
                        S.dma(dbg_out['X1'].rearrange("(c p) t -> p c t", p=128)[:, :, tp:tp + n], xp, q='gpsimd')
                else:
                    S.act(sq, xp, AF.Square)
                    ps = PS[0][:, 0:n]
                    S.mm(ps, [(onesb, sq[:, c, :]) for c in range(8)])
                    rstd = tmp[0]
                    S.act(rstd, ps, AF.Sqrt, bias=EPS)
                    S.recip(rstd, rstd)
                    for c in range(8):
                        S.stt(xp[:, c, :], xp[:, c, :], pp[:, 134 + c:135 + c], rstd, ALU.mult, ALU.mult)
                    S.dma(OUv[:, :, tp:tp + n], xp, q='gpsimd')

            for g in range(ngroups + 1):
                cur = g if g < ngroups else None
                prev = g - 1 if g > 0 else None
                if cur is not None:
                    t0 = g * 256
                    m = 0 if g < 16 else 1
                    xg = xgs[g % 2]
                    S.dma(Yg, YTv[:, :, t0:t0 + n], q='sync')
                    G1 = MOD[:, l, 2, :, m]
                    for dc in range(8):
                        po = PS[1 + dc % 2][:, 0:n]
                        S.mm(po, [(WOUT[:, kc, dc * 128:(dc + 1) * 128], Yg[:, kc, :]) for kc in range(8)])
                        S.stt(xg[:, dc, :], po, G1[:, dc:dc + 1], xg[:, dc, :], ALU.mult, ALU.add)
                    S.act(sq, xg, AF.Square)
                if prev is not None:
                    down_part(prev, 0)
                if cur is not None:
                    pss = PS[0][:, 0:n]
                    S.mm(pss, [(onesb, sq[:, c, :]) for c in range(8)])
                if prev is not None:
                    down_part(prev, 1)
                if cur is not None:
                    rstd = tmp[0]
                    S.act(rstd, pss, AF.Sqrt, bias=EPS)
                    S.recip(rstd, rstd)
                    Acol = MOD[:, l, 3, :, m]
                    Bcol = MOD[:, l, 4, :, m]
                for c in range(8):
                    if cur is not None:
                        t_ = tmp[1 + c % 2]
                        S.tt(t_, xg[:, c, :], rstd, ALU.mult, eng=('vector' if c % 2 == 0 else 'gpsimd'))
                        S.act(h2T[:, c, :], t_, AF.Identity, bias=Bcol[:, c:c + 1], scale=Acol[:, c:c + 1])
                    if prev is not None and c < 6:
                        down_part(prev, 2 + c)
                if prev is not None:
                    finish_group(prev)
                if cur is not None:
                    if g + 1 < ngroups:
                        S.dma(xgs[(g + 1) % 2], Xv[:, :, t0 + n:t0 + 2 * n])
                    for fc in range(32):
                        pu = PS[3 + fc % 4][:, 0:n]
                        S.mm(pu, [(WUP[:, kc, fc * 128:(fc + 1) * 128], h2T[:, kc, :]) for kc in range(8)])
                        r_ = rr[fc % 4]
                        S.act(r_, pu, AF.Relu)
                        S.tt(hid[:, fc, :], r_, r_, ALU.mult, eng=('vector' if fc % 2 == 0 else 'gpsimd'))
            S.barrier()
            A.reset(mk)

        def chk(name):
            if stop_after == name:
                raise _Stop()
        try:
          phase_mod()
          chk('mod')
          for l in range(nlayers):
              need_ctx = l < DEPTH - 1
              last = (l == DEPTH - 1)
              Xsrc = xT_in if l == 0 else XST
              S.dma(pp, W[l]['pp'])
              mk = A.mark()
              zT = A.alloc([32, NT], BF16)
              uTl = A.alloc([128, 2, TL + 30], BF16)
              uTc = A.alloc([128, 2, TC + 30], BF16)
              S.memset(zT, 1.0)
              S.memset(uTl, 0.0, eng='gpsimd')
              S.memset(uTc, 0.0, eng='gpsimd')
              phase_A(l, Xsrc, zT, uTl, uTc)
              if dbg and l == 0:
                  for t_ in range(NTILE):
                      S.dma(dbg_out['PT'][t_ * 128:(t_ + 1) * 128, :], PT[t_ * 128:(t_ + 1) * 128, :], q='gpsimd')
              chk('A')
              hf = 2 if last else 1
              phase_conv(l, uTl, uTc, need_ctx, 8 // hf)
              chk('conv')
              phase_att(l, need_ctx, 8 // hf)
              chk('att')
              phase_scan(l, 'gla', zT, need_ctx, 32 // hf)
              chk('gla')
              phase_scan(l, 'ret', zT, need_ctx, 32 // hf)
              chk('ret')
              if dbg and l == 0:
                  for c_ in range(8):
                      S.dma(dbg_out['YT'][c_ * 128:(c_ + 1) * 128, :], YT[c_ * 128:(c_ + 1) * 128, :], q='gpsimd')
              S.barrier()
              A.reset(mk)
              phase_C(l, Xsrc, last, need_ctx, 16 // hf)
        except _Stop:
            if dbg:
                for c_ in range(8):
                    S.dma(dbg_out['YT'][c_ * 128:(c_ + 1) * 128, :], YT[c_ * 128:(c_ + 1) * 128, :], q='gpsimd')
        S.finish()
        S.emit()
        print("ops", S.n_ops, "waits", S.n_waits, {e: len(S.items[e]) for e in ENGS})
    return nc


def make_consts(rev=False):
    cst = np.zeros((128, NCST), np.float64)
    cst[:, 0:128] = np.eye(128)
    s = np.arange(128)[:, None]
    c = np.arange(128)[None, :]
    cst[:, 128:256] = np.where(s <= c, -1.0 / 16, 0.0)
    cst[:, 256:384] = np.where(s >= c, -1.0 / 16, 0.0)
    cst[:, 384:512] = np.where(s > c, -1.0 / 16, 0.0)
    cst[:, 512:640] = np.where(s < c, -1.0 / 16, 0.0)
    cst[:, 640:768] = np.where(s <= c, 1.0, 0.0)
    cst[:, 768:896] = np.where(s >= c, 1.0, 0.0)
    f = np.arange(128)[:, None] // 32
    he = np.arange(256)[None, :] // 64
    cst[:, 896:1152] = (f == he).astype(np.float64)
    for h in range(4):
        cst[:, 1152 + h] = (np.arange(128) // 32 == h)
    lg = np.log1p(-np.exp2(-5.0 - np.arange(4)))
    lgf = lg[np.arange(128) // 32]
    sc = 32 ** -0.5
    cc = np.arange(128)
    cst[:, 1156:1284] = np.exp(lgf[:, None] * (cc[None, :] + 1))
    cst[:, 1284:1412] = np.exp(-lgf[:, None] * (cc[None, :] + 1)) * sc
    cst[:, 1412:1540] = np.exp(lgf[None, :] * (127 - cc[:, None])) * sc
    cst[:, 1540:1668] = np.exp(lgf[:, None] * (128 - cc[None, :]))
    cst[:, 1668:1796] = np.exp(-lgf[:, None] * (128 - cc[None, :])) * sc
    cst[:, 1796:1924] = np.exp(lgf[None, :] * cc[:, None]) * sc
    cst[:, 1924] = np.exp(lgf * 128)
    t = np.arange(TL)
    if rev:
        t = t[::-1]
    row = (t // 64).astype(np.float64)
    col = (t % 64).astype(np.float64)

    def tables(hd):
        n_ax = hd // 4
        inv = 10000.0 ** (-np.arange(n_ax, dtype=np.float64) / n_ax)
        inv = inv.astype(np.float32).astype(np.float64)
        ang = np.concatenate([row[:, None] * inv, col[:, None] * inv], -1).astype(np.float32)
        return np.cos(ang), np.sin(ang)
    ca, sa = tables(64)
    cr, sr = tables(32)
    ropeA = np.concatenate([ca, sa], -1).reshape(32, 128, 64).transpose(1, 0, 2).reshape(128, 32 * 64)
    ropeR = np.concatenate([cr, sr], -1).reshape(32, 128, 32).transpose(1, 0, 2).reshape(128, 32 * 32)
    return (cst.astype(np.float32), np.ascontiguousarray(ropeA, np.float32), np.ascontiguousarray(ropeR, np.float32))


def fm(v):
    v = np.asarray(v, np.float32)
    return v.reshape(-1, 128).T


def make_in_maps(inp):
    f32 = lambda a: np.ascontiguousarray(np.asarray(a, np.float32))
    consts = [make_consts(False), make_consts(True)]
    per_layer = [[], []]
    for rev in (0, 1):
        for l in range(DEPTH):
            pp = np.zeros((128, NPP), np.float32)
            pp[:, 0:8] = fm(inp['norm1_g'][l])
            pp[:, 8:16] = fm(inp['norm2_g'][l])
            pp[:, 16:64] = fm(inp['b_mod'][l])
            pp[:, 64:66] = fm(inp['conv_b_dw'][l])
            pp[:, 66:68] = fm(inp['conv_ln_g'][l])
            pp[:, 68:70] = fm(inp['conv_ln_b'][l])
            pp[:, 70:72] = fm(inp['conv_b_pw'][l])
            wdw = np.asarray(inp['conv_w_dw'][l], np.float32)
            if rev:
                wdw = wdw[::-1]
            for cc in range(2):
                pp[:, 72 + cc * 31:72 + (cc + 1) * 31] = wdw[:, cc * 128:(cc + 1) * 128].T
            pp[:, 134:142] = fm(inp['final_norm_g'])
            bc = np.zeros((128, NBC), np.float32)
            bc[:, 0:256] = np.asarray(inp['gla_norm_g'][l], np.float32).reshape(1, 256)
            bc[:, 256:512] = np.asarray(inp['ret_norm_g'][l], np.float32).reshape(1, 256)
            qg = np.asarray(inp['att_q_norm_g'][l], np.float32)
            kg = np.asarray(inp['att_k_norm_g'][l], np.float32)
            bc[:, 512:896] = np.concatenate([qg, qg, qg, qg, kg, kg])[None, :]
            waug = np.zeros((32, 256), np.float32)
            fo, bo = (128, 0) if rev else (0, 128)
            waug[0:16, fo:fo + 128] = inp['gla_w_a_f'][l]
            waug[0:16, bo:bo + 128] = inp['gla_w_a_b'][l]
            waug[16, fo:fo + 128] = inp['gla_b_a_f'][l]
            waug[16, bo:bo + 128] = inp['gla_b_a_b'][l]
            per_layer[rev].append({
                "w_mod%d" % l: f32(inp['w_mod'][l]), "w_in%d" % l: f32(inp['w_in'][l]),
                "w_out%d" % l: f32(inp['w_out'][l]), "w_up%d" % l: f32(inp['w_up'][l]),
                "w_down%d" % l: f32(inp['w_down'][l]), "pp%d" % l: pp, "bc%d" % l: bc,
                "waug%d" % l: waug, "wpw%d" % l: f32(inp['conv_w_pw'][l])})
    x = np.asarray(inp['x'], np.float32)
    ctx = np.asarray(inp['ctx'], np.float32)
    c = np.asarray(inp['c'], np.float32)
    cctx = np.asarray(inp['c_ctx'], np.float32)
    maps = []
    for core in range(8):
        b = core % 4
        rev = core // 4
        if rev:
            xT = np.ascontiguousarray(np.concatenate([x[b][::-1], ctx[b][::-1]], 0).T)
        else:
            xT = np.ascontiguousarray(np.concatenate([x[b], ctx[b]], 0).T)
        cvec = np.ascontiguousarray(np.stack([fm(c[b]), fm(cctx)], -1).reshape(128, 16))
        cst, ropeA, ropeR = consts[rev]
        m = {"xT": xT, "cvec": cvec, "cst": cst, "ropeA": ropeA, "ropeR": ropeR}
        for l in range(DEPTH):
            m.update(per_layer[rev][l])
        maps.append(m)
    return maps


_NC_CACHE = {}


def kernel(**inputs):
    maps = make_in_maps(inputs)
    if 'nc' not in _NC_CACHE:
        _NC_CACHE['nc'] = build_program()
    nc = _NC_CACHE['nc']
    res = run_bass_kernel_spmd(nc, maps, core_ids=list(range(8)))
    out = np.empty((4, TL, D), np.float32)
    h = TL // 2
    for b in range(4):
        out[b, 0:h] = res.results[b]["outT"].T
        out[b, h:TL] = res.results[b + 4]["outT"].T[::-1]
    return out
```
